# Optimizing a Trainium2 kernel written in Bass

```python
import jax, jax.numpy as jnp
from jax import lax
import numpy as np

D_MODEL = 1024
BATCH = 16
SEQ = 2048
DEPTH = 1

GLA_HEADS = 4
GLA_DK = 128
GLA_DV = 256
GLA_GATE_RANK = 16
GLA_GATE_NORM = 16.0
GLA_CHUNK = 64
SWA_HEADS = 16
SWA_KV_HEADS = 4
SWA_HEAD_DIM = 64
SWA_WINDOW = 128
SWA_BLOCK = 128
ROPE_THETA = 500000.0
ROPE_DIM = SWA_HEAD_DIM // 4
PEER_HEADS = 8
PEER_NKEYS = 128
PEER_EXPERTS = PEER_NKEYS * PEER_NKEYS
PEER_QDIM = 256
PEER_TOPK = 16
PEER_TOKEN_BLOCK = 128
NORM_EPS = 1e-6

GLA_KW = GLA_HEADS * GLA_DK
GLA_VW = GLA_HEADS * GLA_DV
SWA_QW = SWA_HEADS * SWA_HEAD_DIM
SWA_KVW = SWA_KV_HEADS * SWA_HEAD_DIM
IN_SPLITS = (GLA_KW, GLA_KW, GLA_VW, GLA_VW, GLA_GATE_RANK, SWA_QW, SWA_KVW, SWA_KVW, D_MODEL, D_MODEL)
IN_WIDTH = sum(IN_SPLITS)
IN_SPLIT_POINTS = tuple(int(v) for v in np.cumsum(IN_SPLITS)[:-1])

kernel_name = 'hybrid_gla_swa_sink_peer'


def rmsnorm(x, g):
    xf = x.astype(jnp.float32)
    y = xf * lax.rsqrt(jnp.mean(xf * xf, axis=-1, keepdims=True) + NORM_EPS)
    return (y * g.astype(jnp.float32)).astype(x.dtype)


def rope_tables(seqlen):
    pos = jnp.arange(seqlen, dtype=jnp.float32)
    inv_freq = ROPE_THETA ** (-jnp.arange(0, ROPE_DIM, 2, dtype=jnp.float32) / ROPE_DIM)
    ang = pos[:, None] * inv_freq[None, :]
    return jnp.cos(ang), jnp.sin(ang)


def partial_rope(x, cos, sin):
    half = ROPE_DIM // 2
    xf = x.astype(jnp.float32)
    x1, x2, rest = xf[..., :half], xf[..., half:ROPE_DIM], xf[..., ROPE_DIM:]
    c, s = cos[:, None, :], sin[:, None, :]
    return jnp.concatenate([x1 * c - x2 * s, x2 * c + x1 * s, rest], axis=-1).astype(x.dtype)


def gla_chunked(q, k, v, log_a):
    bsz, seqlen, nh, dk = q.shape
    dv = v.shape[-1]
    c = GLA_CHUNK
    n = seqlen // c

    def to_chunks(t):
        return t.astype(jnp.float32).reshape(bsz, n, c, nh, t.shape[-1]).transpose(0, 3, 1, 2, 4)

    qc = to_chunks(q) * (dk ** -0.5)
    kc, vc, ac = to_chunks(k), to_chunks(v), to_chunks(log_a)
    b = jnp.cumsum(ac, axis=3)
    b_ref = b[:, :, :, c // 2:c // 2 + 1]
    b_last = b[:, :, :, -1:]
    causal = jnp.tril(jnp.ones((c, c), dtype=bool))
    att = jnp.einsum('bhntd,bhnsd->bhnts', qc * jnp.exp(b - b_ref), kc * jnp.exp(b_ref - b))
    att = jnp.where(causal, att, 0.0)
    o_intra = jnp.einsum('bhnts,bhnse->bhnte', att, vc)
    q_inter = qc * jnp.exp(b)
    k_end = kc * jnp.exp(b_last - b)
    decay_end = jnp.exp(b_last[:, :, :, 0])

    def step(state, xs):
        qi, ki, vi, di = xs
        out = jnp.einsum('bhtd,bhde->bhte', qi, state)
        state = di[..., None] * state + jnp.einsum('bhsd,bhse->bhde', ki, vi)
        return state, out

    state0 = jnp.zeros((bsz, nh, dk, dv), jnp.float32)
    xs = (jnp.moveaxis(q_inter, 2, 0), jnp.moveaxis(k_end, 2, 0), jnp.moveaxis(vc, 2, 0), jnp.moveaxis(decay_end, 2, 0))
    _, o_inter = lax.scan(step, state0, xs)
    o = o_intra + jnp.moveaxis(o_inter, 0, 2)
    return o.transpose(0, 2, 3, 1, 4).reshape(bsz, seqlen, nh, dv)


def swa_sink_attention(q, k, v, sinks):
    bsz, seqlen, nh, hd = q.shape
    kvh = k.shape[2]
    grp = nh // kvh
    L = SWA_BLOCK
    n = seqlen // L
    qb = q.reshape(bsz, n, L, kvh, grp, hd)
    kb = k.reshape(bsz, n, L, kvh, hd)
    vb = v.reshape(bsz, n, L, kvh, hd)
    kk = jnp.concatenate([jnp.concatenate([jnp.zeros_like(kb[:, :1]), kb[:, :-1]], axis=1), kb], axis=2)
    vv = jnp.concatenate([jnp.concatenate([jnp.zeros_like(vb[:, :1]), vb[:, :-1]], axis=1), vb], axis=2)
    s = jnp.einsum('bnqhgd,bnkhd->bnhgqk', qb, kk).astype(jnp.float32) * (hd ** -0.5)
    qpos = jnp.arange(L)[:, None] + L
    kpos = jnp.arange(2 * L)[None, :]
    rel = qpos - kpos
    band = (rel >= 0) & (rel < SWA_WINDOW)
    valid = band[None] & ((jnp.arange(n)[:, None, None] > 0) | (kpos[None] >= L))
    s = jnp.where(valid[None, :, None, None], s, -jnp.inf)
    sink = sinks.astype(jnp.float32).reshape(kvh, grp)[None, None, :, :, None, None]
    m = jnp.maximum(jnp.max(s, axis=-1, keepdims=True), sink)
    p = jnp.exp(s - m)
    denom = jnp.sum(p, axis=-1, keepdims=True) + jnp.exp(sink - m)
    o = jnp.einsum('bnhgqk,bnkhd->bnqhgd', (p / denom).astype(v.dtype), vv)
    return o.reshape(bsz, seqlen, nh, hd)


def peer_layer(xn, w_pq, sub_keys, expert_u, expert_v):
    bsz, seqlen, d = xn.shape
    t = bsz * seqlen
    xt = xn.reshape(t, d)
    qp = (xt @ w_pq).reshape(t, PEER_HEADS, 2, PEER_QDIM // 2)
    scores = jnp.einsum('thpd,hpkd->thpk', qp, sub_keys).astype(jnp.float32)
    s_top, i_top = lax.top_k(scores, PEER_TOPK)
    cand = (s_top[:, :, 0, :, None] + s_top[:, :, 1, None, :]).reshape(t, PEER_HEADS, PEER_TOPK * PEER_TOPK)
    cand_idx = (i_top[:, :, 0, :, None] * PEER_NKEYS + i_top[:, :, 1, None, :]).reshape(t, PEER_HEADS, PEER_TOPK * PEER_TOPK)
    best, pos = lax.top_k(cand, PEER_TOPK)
    idx = jnp.take_along_axis(cand_idx, pos, axis=-1)
    gate = jax.nn.softmax(best, axis=-1).astype(xn.dtype)
    hk = PEER_HEADS * PEER_TOPK
    nblk = t // PEER_TOKEN_BLOCK

    def expert_block(args):
        xb, ib, gb = args
        u = expert_u[ib]
        h = jax.nn.gelu(jnp.einsum('lkd,ld->lk', u, xb), approximate=False)
        return jnp.einsum('lk,lkd->ld', gb * h, expert_v[ib])

    out = lax.map(expert_block, (xt.reshape(nblk, PEER_TOKEN_BLOCK, d),
                                 idx.reshape(nblk, PEER_TOKEN_BLOCK, hk),
                                 gate.reshape(nblk, PEER_TOKEN_BLOCK, hk)))
    return out.reshape(bsz, seqlen, d)


def setup_inputs(seed: int = 0) -> dict:
    key = jax.random.key(seed)
    ks = jax.random.split(key, 17)
    f32 = jnp.float32
    nrm = lambda k, shape, scale: jax.random.normal(k, shape, f32) * scale
    return {
        'x': nrm(ks[0], (BATCH, SEQ, D_MODEL), 1.0),
        'norm_mix_g': 1.0 + nrm(ks[1], (DEPTH, D_MODEL), 0.02),
        'w_in': nrm(ks[2], (DEPTH, D_MODEL, IN_WIDTH), D_MODEL ** -0.5),
        'w_gk2': nrm(ks[3], (DEPTH, GLA_GATE_RANK, GLA_KW), GLA_GATE_RANK ** -0.5),
        'b_gk': nrm(ks[4], (DEPTH, GLA_KW), 0.1),
        'gla_norm_g': 1.0 + nrm(ks[5], (DEPTH, GLA_DV), 0.02),
        'q_norm_g': 1.0 + nrm(ks[6], (DEPTH, SWA_HEAD_DIM), 0.02),
        'k_norm_g': 1.0 + nrm(ks[7], (DEPTH, SWA_HEAD_DIM), 0.02),
        'attn_sinks': nrm(ks[8], (DEPTH, SWA_HEADS), 0.1),
        'w_branch_a': nrm(ks[9], (DEPTH, GLA_VW, D_MODEL), GLA_VW ** -0.5),
        'w_branch_b': nrm(ks[10], (DEPTH, SWA_QW, D_MODEL), SWA_QW ** -0.5),
        'w_out': nrm(ks[11], (DEPTH, D_MODEL, D_MODEL), D_MODEL ** -0.5),
        'norm_ffn_g': 1.0 + nrm(ks[12], (DEPTH, D_MODEL), 0.02),
        'w_peer_q': nrm(ks[13], (DEPTH, D_MODEL, PEER_HEADS * PEER_QDIM), D_MODEL ** -0.5),
        'peer_sub_keys': nrm(ks[14], (DEPTH, PEER_HEADS, 2, PEER_NKEYS, PEER_QDIM // 2), (PEER_QDIM // 2) ** -0.5),
        'peer_u': nrm(ks[15], (DEPTH, PEER_EXPERTS, D_MODEL), D_MODEL ** -0.5),
        'peer_v': nrm(ks[16], (DEPTH, PEER_EXPERTS, D_MODEL), D_MODEL ** -0.5),
    }


def reference(x, norm_mix_g, w_in, w_gk2, b_gk, gla_norm_g, q_norm_g, k_norm_g, attn_sinks,
              w_branch_a, w_branch_b, w_out, norm_ffn_g, w_peer_q, peer_sub_keys, peer_u, peer_v):
    bsz, seqlen, d = x.shape
    cos, sin = rope_tables(seqlen)
    for l in range(DEPTH):
        h = rmsnorm(x, norm_mix_g[l])
        proj = h @ w_in[l]
        gq, gk, gv, gr, glr, sq, sk, sv, gate_a, gate_b = jnp.split(proj, IN_SPLIT_POINTS, axis=-1)
        log_a = jax.nn.log_sigmoid((glr @ w_gk2[l] + b_gk[l]).astype(jnp.float32)) / GLA_GATE_NORM
        o_a = gla_chunked(gq.reshape(bsz, seqlen, GLA_HEADS, GLA_DK),
                          gk.reshape(bsz, seqlen, GLA_HEADS, GLA_DK),
                          gv.reshape(bsz, seqlen, GLA_HEADS, GLA_DV),
                          log_a.reshape(bsz, seqlen, GLA_HEADS, GLA_DK))
        o_a = rmsnorm(o_a, gla_norm_g[l]).astype(x.dtype) * jax.nn.silu(gr.reshape(bsz, seqlen, GLA_HEADS, GLA_DV))
        y_a = o_a.reshape(bsz, seqlen, GLA_VW) @ w_branch_a[l]
        q = partial_rope(rmsnorm(sq.reshape(bsz, seqlen, SWA_HEADS, SWA_HEAD_DIM), q_norm_g[l]), cos, sin)
        k = partial_rope(rmsnorm(sk.reshape(bsz, seqlen, SWA_KV_HEADS, SWA_HEAD_DIM), k_norm_g[l]), cos, sin)
        o_b = swa_sink_attention(q, k, sv.reshape(bsz, seqlen, SWA_KV_HEADS, SWA_HEAD_DIM), attn_sinks[l])
        y_b = o_b.reshape(bsz, seqlen, SWA_QW) @ w_branch_b[l]
        merged = jax.nn.sigmoid(gate_a) * y_a + jax.nn.sigmoid(gate_b) * y_b
        x = x + merged @ w_out[l]
        x = x + peer_layer(rmsnorm(x, norm_ffn_g[l]), w_peer_q[l], peer_sub_keys[l], peer_u[l], peer_v[l])
    return x
```

```python
import numpy as np
import concourse.bass as bass
import concourse.mybir as mybir
from concourse.bass_utils import run_bass_kernel_spmd
from contextlib import ExitStack

F32 = mybir.dt.float32
BF16 = mybir.dt.bfloat16
U32 = mybir.dt.uint32
I32 = mybir.dt.int32
AF = mybir.ActivationFunctionType
ALU = mybir.AluOpType
AX = mybir.AxisListType

EPS = 1e-6
W_IN_COLS = [0, 512, 1024, 1536, 2048, 2560, 3088, 3600, 4112, 4624, 5136, 5648, 6160]
NCH = 23
NSLOT = 3
NG = 7
ND = 4


class Prog:
    ENGS = ("pe", "dve", "act", "pool", "sp")

    def __init__(self, nc, es):
        self.nc = nc
        self.es = es
        self.q = {e: [] for e in self.ENGS}
        self.sem = {e: es.enter_context(nc.semaphore("sem_" + e)) for e in self.ENGS}
        self.semeng = {id(self.sem[e]): e for e in self.ENGS}
        self.cnt = {e: 0 for e in self.ENGS}
        self.waited = {e: {} for e in self.ENGS}
        self.lastw = {}
        self.readers = {}
        self.dsem = {}
        self.dcnt = {}

    def op(self, eng, fn, reads=(), writes=(), dma=None):
        if getattr(self, "dry", False):
            return None
        deps = []
        for r in reads:
            if r in self.lastw:
                deps.append(self.lastw[r])
        for w in writes:
            if w in self.lastw:
                deps.append(self.lastw[w])
            deps.extend(self.readers.get(w, []))
        wd = self.waited[eng]
        best = {}
        for (s, v) in deps:
            key = id(s)
            if eng == "pe" and self.semeng.get(key) == "pe":
                continue
            if wd.get(key, 0) >= v:
                continue
            if key not in best or best[key][1] < v:
                best[key] = (s, v)
        waits = []
        for key, (s, v) in best.items():
            wd[key] = v
            waits.append((s, v))
        if dma is None:
            self.cnt[eng] += 1
            ev = (self.sem[eng], self.cnt[eng])
            inc = 1
        else:
            if dma not in self.dsem:
                self.dsem[dma] = self.es.enter_context(self.nc.semaphore("d_" + dma))
                self.dcnt[dma] = 0
            self.dcnt[dma] += 16
            ev = (self.dsem[dma], self.dcnt[dma])
            inc = 16
        self.q[eng].append((waits, fn, ev[0], inc))
        for w in writes:
            self.lastw[w] = ev
            self.readers[w] = []
        for r in reads:
            if r not in writes:
                self.readers.setdefault(r, []).append(ev)
        return ev

    def final_wait(self, eng, evs):
        self.q[eng].append((list(evs), None, None, 0))

    def emit(self):
        nc = self.nc
        with nc.Block() as block:
            def mk(ename):
                def body(e):
                    for (waits, fn, s, inc) in self.q[ename]:
                        for (ws, wv) in waits:
                            e.wait_ge(ws, wv)
                        if fn is not None:
                            ins = fn(e)
                            ins.then_inc(s, inc)
                return body
            block.tensor(mk("pe"))
            block.vector(mk("dve"))
            block.scalar(mk("act"))
            block.gpsimd(mk("pool"))
            block.sync(mk("sp"))


def build_nc(NSEQ=2, NT=16, DBG=False, NR=128, STAGE=99):
    NTOK = NSEQ * NT * 128
    nc = bass.Bass("TRN2", target_bir_lowering=False)
    es = ExitStack()

    def din(name, shape, dt=F32):
        return nc.dram_tensor(name, list(shape), dt, kind="ExternalInput").ap()

    x_d = din("x", [NTOK, 1024])
    w_in_d = din("w_in", [1024, 6672])
    wa_d = din("w_branch_a", [1024, 1024])
    wb_d = din("w_branch_b", [1024, 1024])
    wo_d = din("w_out", [1024, 1024])
    wpq_d = din("w_peer_q", [1024, 2048])
    sk_d = din("sub_keys", [16, 128, 128])
    pu_d = din("peer_u", [16384, 1024])
    pv_d = din("peer_v", [16384, 1024])
    gmix_d = din("gmix_pk", [128, 8])
    gffn_d = din("gffn_pk", [128, 8])
    gffn_row_d = din("gffn_row", [1, 1024])
    ga_d = din("ga_row", [1, 256])
    gq_d = din("gq_row", [1, 64])
    gk_d = din("gk_row", [1, 64])
    sinks_d = din("sinks_row", [1, 16])
    wgk2_d = din("w_gk2", [16, 512])
    bgk_d = din("b_gk", [1, 512])
    cmat_d = din("cmat", [5, 128, 128])
    mprev_d = din("mprev", [128, 128])
    cm_d = din("cm", [128, 2])
    cos_d = din("rcos", [128, 16 * 8])
    sin_d = din("rsin", [128, 16 * 8])
    out_d = nc.dram_tensor("out", [NTOK, 1024], F32, kind="ExternalOutput").ap()
    wscr = nc.dram_tensor("wscr", [NCH, 128, 4096], BF16, kind="Internal").ap()
    uvs = nc.dram_tensor("uvscr", [16384, 2048], BF16, kind="Internal").ap()
    dbg_d = {}
    if DBG:
        for nm in ("x2", "ya", "yb", "oa", "ob"):
            dbg_d[nm] = nc.dram_tensor("dbg_" + nm, [NTOK, 1024], F32, kind="ExternalOutput").ap()
        dbg_d["idx"] = nc.dram_tensor("dbg_idx", [NTOK, 128], F32, kind="ExternalOutput").ap()
        dbg_d["wgt"] = nc.dram_tensor("dbg_wgt", [NTOK, 128], F32, kind="ExternalOutput").ap()

    with es:
        p = Prog(nc, es)
        p.dry = False

        def sb(name, shape, dt=F32):
            return es.enter_context(nc.sbuf_tensor("s_" + name, list(shape), dt))

        identf = sb("identf", [128, 128]); identb = sb("identb", [128, 128], BF16)
        tri = sb("tri", [128, 128]); tri2 = sb("tri2", [128, 128])
        mgla = sb("mgla", [128, 128]); mk2 = sb("mk2", [128, 512])
        cm = sb("cm", [128, 2])
        rcos = sb("rcos", [128, 128]); rsin = sb("rsin", [128, 128])
        gmix = sb("gmix", [128, 8]); gffn = sb("gffn", [128, 8])
        gffn_rep = sb("gffn_rep", [128, 1024])
        ga_rep = sb("ga_rep", [128, 1024]); gq_rep = sb("gq_rep", [128, 1024]); gk_rep = sb("gk_rep", [128, 256])
        g256 = sb("g256", [128, 256]); g64q = sb("g64q", [128, 64]); g64k = sb("g64k", [128, 64])
        esink = sb("esink", [128, 16])
        wgk = sb("wgk", [17, 512]); glra = sb("glra", [17, 128])
        wglr_f = sb("wglr_f", [128, 128]); wglr = sb("wglr", [128, 128], BF16)
        skT = sb("skT", [128, 2048], BF16)
        ones_bf = sb("ones_bf", [128, 2], BF16)
        wring = [sb("wring%d" % i, [128, 4096], BF16) for i in range(NSLOT)]
        FS = sb("FS", [128, 5120])
        xts = [sb("xt%d" % i, [128, 1024]) for i in range(2)]; x2s = [sb("x2_%d" % i, [128, 1024]) for i in range(2)]
        junkA = sb("junkA", [128, 1024], BF16); junkD = sb("junkD", [128, 1024], BF16)
        qTs = sb("qTs", [128, 512]); kTs = sb("kTs", [128, 512]); ks = sb("ks", [128, 512])
        Lb = sb("Lb", [128, 512]); bTs = sb("bTs", [128, 512])
        E1 = sb("E1", [128, 512], BF16); E2 = sb("E2", [128, 512], BF16); E3 = sb("E3", [128, 512], BF16); Er = sb("Er", [128, 512], BF16)
        qtT = sb("qtT", [128, 512], BF16); ktT = sb("ktT", [128, 512], BF16)
        qi0 = sb("qi0", [128, 512], BF16); qi1 = sb("qi1", [128, 512], BF16)
        ke0 = sb("ke0", [128, 512], BF16); ke1 = sb("ke1", [128, 512], BF16)
        attm = sb("attm", [128, 512], BF16)
        vb = sb("vb", [128, 1024], BF16)
        S = sb("S", [128, 1024]); S1 = sb("S1", [128, 1024])
        Sb = sb("Sb", [128, 1024], BF16); S1b = sb("S1b", [128, 1024], BF16)
        nbref = sb("nbref", [128, 8]); pbref = sb("pbref", [128, 8]); dec = sb("dec", [128, 8])
        trA = sb("trA", [128, 1024], BF16); trB = sb("trB", [128, 1024], BF16)
        trC = sb("trC", [128, 1024], BF16); trD = sb("trD", [128, 1024], BF16)
        tmA = sb("tmA", [128, 1024], BF16); tmB = sb("tmB", [128, 1024], BF16)
        kq = sb("kq", [128, 256]); kn = sb("kn", [128, 256]); k2 = sb("k2", [128, 512], BF16)
        kT2 = [sb("kT2_%d" % i, [128, 512], BF16) for i in range(2)]
        vsw = [sb("vsw_%d" % i, [128, 256], BF16) for i in range(2)]
        rt = [sb("rt%d" % i, [128, 128]) for i in range(4)]
        praw = [sb("praw%d" % i, [128, 512], BF16) for i in range(2)]
        pTe = [sb("pTe%d" % i, [128, 512], BF16) for i in range(4)]
        pTo = [sb("pTo%d" % i, [128, 512], BF16) for i in range(4)]
        ss = sb("ss", [128, 2]); rs = sb("rs", [128, 2])
        ssqa = sb("ssqa", [128, 4]); rsa = sb("rsa", [128, 4])
        ssqq = sb("ssqq", [128, 16]); rsq = sb("rsq", [128, 16])
        ssqk = sb("ssqk", [128, 4]); rsk = sb("rsk", [128, 4])
        den = sb("den", [128, 16]); rden = sb("rden", [128, 16])
        qpT = sb("qpT", [128, 2048], BF16)
        sc2L = [sb("sc2_%d" % i, [128, 128]) for i in range(4)]
        tv = sb("tv", [128, 256]); tiu = sb("tiu", [128, 256], U32); tif = sb("tif", [128, 256])
        candL = [sb("cand_%d" % i, [128, 112]) for i in range(2)]; cand2L = [sb("cand2_%d" % i, [128, 112]) for i in range(2)]
        ciL = [sb("ci_%d" % i, [128, 112]) for i in range(2)]
        bv = sb("bv", [128, 128]); idxf = sb("idxf", [128, 128]); idxi = sb("idxi", [128, 128], I32)
        negm = sb("negm", [128, 8]); Z = sb("Z", [128, 8]); rZ = sb("rZ", [128, 8])
        eg = sb("eg", [128, 128]); gate = sb("gate", [128, 128])
        hd = sb("hd", [128, 128]); gl = sb("gl", [128, 128]); wgt = sb("wgt", [128, 128])
        Dr = [sb("Dr%d" % i, [128, 128], BF16) for i in range(ND)]
        RG = sb("RG", [128, NG * 2048], BF16)
        rg = [RG[:, i * 2048:(i + 1) * 2048] for i in range(NG)]

        def F(i, n=1):
            return FS[:, i * 1024:(i + n) * 1024]

        Tps = es.enter_context(nc.psum_tensor("Tps", [128, 1024], BF16))
        XN = es.enter_context(nc.psum_tensor("XN", [128, 1024], F32))
        NBK = 5
        banks = [es.enter_context(nc.psum_tensor("bk%d" % i, [128, 512], F32)) for i in range(NBK)]
        bstate = {"i": 0}

        def bank():
            i = bstate["i"]
            bstate["i"] = (i + 1) % NBK
            return banks[i], "bk%d" % i

        def PE(fn, r=(), w=()): return p.op("pe", fn, r, w)
        def DVE(fn, r=(), w=()): return p.op("dve", fn, r, w)
        def ACT(fn, r=(), w=()): return p.op("act", fn, r, w)
        def POOL(fn, r=(), w=()): return p.op("pool", fn, r, w)
        def DMA(fn, r=(), w=(), sem=None): return p.op("sp", fn, r, w, dma=sem)

        ldc = {"n": 0, "res": []}

        def load(out_ap, in_ap, wres):
            ldc["n"] += 1
            ldc["res"].append(wres)
            return DMA(lambda e: e.dma_start(out=out_ap, in_=in_ap), (), [wres], sem="ldc")

        def load_barrier():
            ev = (p.dsem["ldc"], p.dcnt["ldc"])
            for r in ldc["res"]:
                p.lastw[r] = ev
            ldc["res"] = []

        def transposes(src, nblk, rres, dst, wres, eng="act"):
            def f(e):
                ins = None
                for kc in range(nblk):
                    ins = e.transpose(out=Tps[:, kc * 128:(kc + 1) * 128], in_=src[:, kc * 128:(kc + 1) * 128], identity=identb[:])
                return ins
            PE(f, [rres, "identb"], ["T"])
            ACT(lambda e: e.copy(out=dst[:, 0:nblk * 128], in_=Tps[:, 0:nblk * 128]), ["T"], [wres])

        UVRES = ["ccs%d" % i for i in range(4)]
        ci_ = 0
        for q8 in range(16):
            r0, r1 = q8 * 1024, (q8 + 1) * 1024
            for (src_d, c0) in ((pu_d, 0), (pv_d, 1024)):
                p.op("pool", lambda e, r0=r0, r1=r1, src_d=src_d, c0=c0: e.dma_start(out=uvs[r0:r1, c0:c0 + 1024], in_=src_d[r0:r1, :]),
                     (), ["ccs%d" % (ci_ % 4)], dma="cc%d" % (ci_ % 4))
                ci_ += 1
        load(identf[:], cmat_d[0], "identf"); load(tri[:], cmat_d[1], "tri"); load(tri2[:], cmat_d[2], "tri2")
        load(mgla[:], cmat_d[3], "mgla")
        load(mk2[:, 0:128], mprev_d, "mk2"); load(mk2[:, 128:256], mprev_d, "mk2")
        load(mk2[:, 256:384], cmat_d[4], "mk2"); load(mk2[:, 384:512], cmat_d[4], "mk2")
        load(cm[:], cm_d, "cm"); load(rcos[:], cos_d, "rcos"); load(rsin[:], sin_d, "rsin")
        load(gmix[:], gmix_d, "gmix"); load(gffn[:], gffn_d, "gffn")
        load(gffn_rep[:], gffn_row_d.broadcast_to([128, 1024]), "gffn_rep")
        load(g256[:], ga_d.broadcast_to([128, 256]), "g256")
        load(g64q[:], gq_d.broadcast_to([128, 64]), "g64q"); load(g64k[:], gk_d.broadcast_to([128, 64]), "g64k")
        load(esink[:], sinks_d.broadcast_to([128, 16]), "esink")
        load(wgk[0:16, :], wgk2_d, "wgk"); load(wgk[16:17, :], bgk_d, "wgk")
        load(wglr_f[:].rearrange("p (k n) -> p k n", n=16),
             w_in_d.rearrange("(k p) n -> p k n", p=128)[:, :, 3072:3088], "wglr_f")
        for q4 in range(4):
            load(F(0, 2)[:, q4 * 512:(q4 + 1) * 512].rearrange("p (a d) -> p a d", d=128),
                 sk_d[q4 * 4:(q4 + 1) * 4].rearrange("a k d -> k a d"), "skl%d" % q4)
        load_barrier()
        DVE(lambda e: e.tensor_copy(out=identb[:], in_=identf[:]), ["identf"], ["identb"])
        DVE(lambda e: e.tensor_copy(out=ga_rep[:].rearrange("p (h e) -> p h e", e=256),
                                    in_=g256[:, None, :].broadcast_to([128, 4, 256])), ["g256"], ["ga_rep"])
        DVE(lambda e: e.tensor_scalar(out=gq_rep[:].rearrange("p (h e) -> p h e", e=64),
                                      in0=g64q[:, None, :].broadcast_to([128, 16, 64]), scalar1=0.125, scalar2=None, op0=ALU.mult),
            ["g64q"], ["gq_rep"])
        DVE(lambda e: e.tensor_copy(out=gk_rep[:].rearrange("p (h e) -> p h e", e=64),
                                    in_=g64k[:, None, :].broadcast_to([128, 4, 64])), ["g64k"], ["gk_rep"])
        ACT(lambda e: e.activation(out=esink[:], in_=esink[:], func=AF.Exp), ["esink"], ["esink"])
        POOL(lambda e: e.memset(glra[:], 1.0), (), ["glra"])
        POOL(lambda e: e.memset(qi0[:], 0.0), (), ["qi0"])
        POOL(lambda e: e.memset(qi1[:], 0.0), (), ["qi1"])
        POOL(lambda e: e.memset(ones_bf[:], 1.0), (), ["ones_bf"])
        DVE(lambda e: e.tensor_tensor(out=wglr[:].rearrange("p (k n) -> p k n", n=16),
                                      in0=wglr_f[:].rearrange("p (k n) -> p k n", n=16),
                                      in1=gmix[:, :, None].broadcast_to([128, 8, 16]), op=ALU.mult),
            ["wglr_f", "gmix"], ["wglr"])
        for q4 in range(4):
            bk, bres = bank()

            def f(e, q4=q4, bk=bk):
                ins = None
                for a in range(4):
                    hp = q4 * 4 + a
                    ins = e.transpose(out=bk[:, a * 128:(a + 1) * 128], in_=F(0, 2)[:, hp * 128:(hp + 1) * 128], identity=identf[:])
                return ins
            PE(f, ["skl%d" % q4, "F0", "F1", "identf"], [bres])
            ACT(lambda e, q4=q4, bk=bk: e.copy(out=skT[:, q4 * 512:(q4 + 1) * 512], in_=bk[:, :]), [bres], ["skT"])

        def wsrc(c):
            if c < 13:
                return w_in_d, W_IN_COLS[c], gmix
            if c < 15:
                return wa_d, (c - 13) * 512, None
            if c < 17:
                return wb_d, (c - 15) * 512, None
            if c < 19:
                return wo_d, (c - 17) * 512, None
            return wpq_d, (c - 19) * 512, gffn

        hcount = 0
        for c in range(NCH):
            src, col, g = wsrc(c)
            slot = c % NSLOT
            wr = wring[slot]
            for half in range(2):
                st_i = hcount % 2
                hcount += 1
                stg = F(st_i * 2, 2)
                sres = ["F%d" % (st_i * 2), "F%d" % (st_i * 2 + 1)]
                srcap = src.rearrange("(k p) n -> p k n", p=128)[:, half * 4:(half + 1) * 4, col:col + 512]
                DMA(lambda e, stg=stg, srcap=srcap: e.dma_start(out=stg.rearrange("p (k n) -> p k n", n=512), in_=srcap),
                    (), sres, sem="stg%d" % st_i)
                for k4 in range(4):
                    kc = half * 4 + k4
                    o_ap = wr[:, kc * 512:(kc + 1) * 512]
                    i_ap = stg[:, k4 * 512:(k4 + 1) * 512]
                    if g is None:
                        if k4 % 2 == 0:
                            DVE(lambda e, o_ap=o_ap, i_ap=i_ap: e.tensor_copy(out=o_ap, in_=i_ap), sres, ["W%dk%d" % (slot, kc)])
                        else:
                            ACT(lambda e, o_ap=o_ap, i_ap=i_ap: e.copy(out=o_ap, in_=i_ap), sres, ["W%dk%d" % (slot, kc)])
                    else:
                        gs = g[:, kc:kc + 1]
                        if k4 % 2 == 0:
                            DVE(lambda e, o_ap=o_ap, i_ap=i_ap, gs=gs: e.tensor_scalar(out=o_ap, in0=i_ap, scalar1=gs, scalar2=None, op0=ALU.mult),
                                sres + ["gmix", "gffn"], ["W%dk%d" % (slot, kc)])
                        else:
                            ACT(lambda e, o_ap=o_ap, i_ap=i_ap, gs=gs: e.activation(out=o_ap, in_=i_ap, func=AF.Copy, scale=gs),
                                sres + ["gmix", "gffn"], ["W%dk%d" % (slot, kc)])
            DMA(lambda e, c=c, wr=wr: e.dma_start(out=wscr[c], in_=wr[:]), ["W%d" % slot] + ["W%dk%d" % (slot, k) for k in range(8)], ["wscr%d" % c], sem="wst%d" % slot)

        total_uses = NSEQ * NT * NCH
        wstate = {"issued": 0, "use": -1, "lastc": None, "rec": [], "order": None}

        def wget(n):
            c = n % NCH
            if wstate["lastc"] != c:
                wstate["use"] += 1
                wstate["lastc"] = c
                if p.dry:
                    wstate["rec"].append(c)
                else:
                    assert wstate["order"][wstate["use"] % NCH] == c, (wstate["use"], c)
            if p.dry:
                return wring[0], "W0"
            u = wstate["use"]
            while wstate["issued"] < min(total_uses, u + NSLOT):
                m = wstate["issued"]
                cid = wstate["order"][m % NCH]
                slot = m % NSLOT
                DMA(lambda e, cid=cid, slot=slot: e.dma_start(out=wring[slot][:], in_=wscr[cid]), ["wscr%d" % cid], ["W%d" % slot], sem="wld%d" % slot)
                wstate["issued"] += 1
            slot = u % NSLOT
            return wring[slot], "W%d" % slot

        def proj_tok(n, lhs, lres, bk, bres):
            W, wres = wget(n)

            def f(e):
                ins = None
                for kc in range(8):
                    ins = e.matmul(bk[:, :], lhsT=lhs[:, kc * 128:(kc + 1) * 128], rhs=W[:, kc * 512:(kc + 1) * 512], start=(kc == 0), stop=(kc == 7))
                return ins
            PE(f, [lres, wres], [bres])

        def proj_feat(n, rhs, rres, bk, bres):
            W, wres = wget(n)

            def f(e):
                ins = None
                for j in range(4):
                    for kc in range(8):
                        ins = e.matmul(bk[:, j * 128:(j + 1) * 128], lhsT=W[:, kc * 512 + j * 128: kc * 512 + (j + 1) * 128],
                                       rhs=rhs[:, kc * 128:(kc + 1) * 128], start=(kc == 0), stop=(kc == 7))
                return ins
            PE(f, [rres, wres], [bres])

        def tap(name, src, tok0, rres, ncols=1024):
            if DBG:
                DMA(lambda e: e.dma_start(out=dbg_d[name][tok0:tok0 + 128, :], in_=src), [rres], (), sem="dbg_" + name)

        def v3(ap, inner):
            return ap.rearrange("p (a b) -> p a b", b=inner)

        def rope(buf, bres, nh, ti):
            bvw = v3(buf, 64)
            x1 = bvw[:, :, 0:8]
            x2_ = bvw[:, :, 8:16]
            cs = rcos[:, None, ti * 8:(ti + 1) * 8].broadcast_to([128, nh, 8])
            sn = rsin[:, None, ti * 8:(ti + 1) * 8].broadcast_to([128, nh, 8])
            t = [v3(rt[i][:, 0:nh * 8], 8) for i in range(4)]
            DVE(lambda e: e.tensor_tensor(out=t[0], in0=x1, in1=cs, op=ALU.mult), [bres, "rcos"], ["rt0"])
            DVE(lambda e: e.tensor_tensor(out=t[1], in0=x2_, in1=sn, op=ALU.mult), [bres, "rsin"], ["rt1"])
            DVE(lambda e: e.tensor_tensor(out=t[2], in0=x2_, in1=cs, op=ALU.mult), [bres, "rcos"], ["rt2"])
            DVE(lambda e: e.tensor_tensor(out=t[3], in0=x1, in1=sn, op=ALU.mult), [bres, "rsin"], ["rt3"])
            DVE(lambda e: e.tensor_tensor(out=x1, in0=t[0], in1=t[1], op=ALU.subtract), ["rt0", "rt1", bres], [bres])
            DVE(lambda e: e.tensor_tensor(out=x2_, in0=t[2], in1=t[3], op=ALU.add), ["rt2", "rt3", bres], [bres])

        def rstd_op(ssq_ap, out_ap, n, rres, wres):
            ACT(lambda e: e.activation(out=out_ap, in_=ssq_ap, func=AF.Sqrt, bias=EPS, scale=1.0 / n), [rres], [wres])
            DVE(lambda e: e.reciprocal(out=out_ap, in_=out_ap), [wres], [wres])

        SCALE = 128.0 ** -0.5

        def mixer(si, ti):
            gi = si * NT + ti
            tok0 = gi * 128
            n0 = gi * NCH
            par = ti % 2
            xt = xts[gi % 2]; x2 = x2s[gi % 2]
            XT = "xt%d" % (gi % 2); X2 = "x2_%d" % (gi % 2)
            A4, B4, C4, D4, E4 = F(0), F(1), F(2), F(3), F(4)

            if ti == 0:
                POOL(lambda e: e.memset(S[:], 0.0), (), ["S"])
                POOL(lambda e: e.memset(Sb[:], 0.0), (), ["Sb"])
            DMA(lambda e: e.dma_start(out=xt[:], in_=x_d[tok0:tok0 + 128, :]), (), [XT], sem="ldx%d" % (gi % 2))
            ACT(lambda e: e.activation(out=junkA[:], in_=xt[:], func=AF.Square, accum_out=ss[:, 0:1]), [XT], ["ss0", "junkA"])
            rstd_op(ss[:, 0:1], rs[:, 0:1], 1024, "ss0", "rs0")
            DVE(lambda e: e.tensor_scalar(out=tmA[:], in0=xt[:], scalar1=rs[:, 0:1], scalar2=None, op0=ALU.mult), [XT, "rs0"], ["tmA"])
            transposes(tmA, 8, "tmA", trA, "trA")
            hT = trA
            yield

            def gla():
                bk, br = bank(); proj_feat(n0 + 0, hT, "trA", bk, br)
                ACT(lambda e, bk=bk: e.copy(out=qTs[:], in_=bk[:, :]), [br], ["qTs"])
                yield
                bk, br = bank(); proj_feat(n0 + 1, hT, "trA", bk, br)
                ACT(lambda e, bk=bk: e.copy(out=kTs[:], in_=bk[:, :]), [br], ["kTs"])
                bk, br = bank(); proj_tok(n0 + 1, hT, "trA", bk, br)
                ACT(lambda e, bk=bk: e.copy(out=ks[:], in_=bk[:, :]), [br], ["ks"])
                yield
                bk, br = bank()

                def f(e, bk=bk):
                    ins = None
                    for kc in range(8):
                        ins = e.matmul(bk[0:16, 0:128], lhsT=wglr[:, kc * 16:(kc + 1) * 16], rhs=hT[:, kc * 128:(kc + 1) * 128], start=(kc == 0), stop=(kc == 7))
                    return ins
                PE(f, ["trA", "wglr"], [br])
                ACT(lambda e, bk=bk: e.copy(out=glra[0:16, :], in_=bk[0:16, 0:128]), [br], ["glra"])
                bk, br = bank()
                PE(lambda e, bk=bk: e.matmul(bk[:, :], lhsT=glra[0:17, :], rhs=wgk[0:17, :], start=True, stop=True), ["glra", "wgk"], [br])
                ACT(lambda e, bk=bk: e.activation(out=Lb[:], in_=bk[:, :], func=AF.Exp, scale=-1.0), [br], ["Lb"])
                ACT(lambda e: e.activation(out=Lb[:], in_=Lb[:], func=AF.Ln, bias=1.0), ["Lb"], ["Lb"])
                bkA, brA = bank()

                def f(e, bkA=bkA):
                    ins = None
                    for h in range(4):
                        ins = e.matmul(bkA[:, h * 128:(h + 1) * 128], lhsT=Lb[:, h * 128:(h + 1) * 128], rhs=tri[:], start=True, stop=True)
                    return ins
                PE(f, ["Lb", "tri"], [brA])
                bkB, brB = bank()
                PE(lambda e, bkB=bkB: e.matmul(bkB[:, :], lhsT=tri2[:], rhs=Lb[:], start=True, stop=True), ["Lb", "tri2"], [brB])
                ACT(lambda e, bkA=bkA: e.copy(out=bTs[:], in_=bkA[:, :]), [brA], ["bTs"])
                ACT(lambda e, bkB=bkB: e.activation(out=Er[:], in_=bkB[:, :], func=AF.Exp), [brB], ["Er"])
                yield
                bTv = v3(bTs[:], 64)
                DVE(lambda e: e.tensor_scalar(out=nbref[:], in0=bTv[:, :, 32], scalar1=-1.0, scalar2=None, op0=ALU.mult), ["bTs"], ["nbref"])
                DVE(lambda e: e.tensor_copy(out=pbref[:], in_=bTv[:, :, 32]), ["bTs"], ["pbref"])
                ACT(lambda e: e.activation(out=dec[:], in_=bTv[:, :, 63], func=AF.Exp), ["bTs"], ["dec"])
                for g in range(8):
                    ACT(lambda e, g=g: e.activation(out=E1[:, g * 64:(g + 1) * 64], in_=bTs[:, g * 64:(g + 1) * 64], func=AF.Exp,
                                                    bias=nbref[:, g:g + 1], scale=1.0), ["bTs", "nbref"], ["E1"])
                for g in range(8):
                    ACT(lambda e, g=g: e.activation(out=E2[:, g * 64:(g + 1) * 64], in_=bTs[:, g * 64:(g + 1) * 64], func=AF.Exp,
                                                    bias=pbref[:, g:g + 1], scale=-1.0), ["bTs", "pbref"], ["E2"])
                ACT(lambda e: e.activation(out=E3[:], in_=bTs[:], func=AF.Exp), ["bTs"], ["E3"])
                yield
                DVE(lambda e: e.scalar_tensor_tensor(out=qtT[:], in0=qTs[:], scalar=SCALE, op0=ALU.mult, in1=E1[:], op1=ALU.mult), ["qTs", "E1"], ["qtT"])
                DVE(lambda e: e.tensor_tensor(out=ktT[:], in0=kTs[:], in1=E2[:], op=ALU.mult), ["kTs", "E2"], ["ktT"])
                DVE(lambda e: e.scalar_tensor_tensor(out=v3(qi0[:], 128)[:, :, 0:64], in0=v3(qTs[:], 128)[:, :, 0:64], scalar=SCALE, op0=ALU.mult,
                                                     in1=v3(E3[:], 128)[:, :, 0:64], op1=ALU.mult), ["qTs", "E3"], ["qi0"])
                DVE(lambda e: e.scalar_tensor_tensor(out=v3(qi1[:], 128)[:, :, 64:128], in0=v3(qTs[:], 128)[:, :, 64:128], scalar=SCALE, op0=ALU.mult,
                                                     in1=v3(E3[:], 128)[:, :, 64:128], op1=ALU.mult), ["qTs", "E3"], ["qi1"])
                DVE(lambda e: e.scalar_tensor_tensor(out=ke0[:], in0=ks[:], scalar=cm[:, 0:1], op0=ALU.mult, in1=Er[:], op1=ALU.mult), ["ks", "Er", "cm"], ["ke0"])
                DVE(lambda e: e.scalar_tensor_tensor(out=ke1[:], in0=ks[:], scalar=cm[:, 1:2], op0=ALU.mult, in1=Er[:], op1=ALU.mult), ["ks", "Er", "cm"], ["ke1"])
                yield
                for hf in range(2):
                    bk, br = bank(); proj_tok(n0 + 2 + hf, hT, "trA", bk, br)
                    ACT(lambda e, bk=bk, hf=hf: e.copy(out=vb[:, hf * 512:(hf + 1) * 512], in_=bk[:, :]), [br], ["vb"])
                    yield
                for hf in range(2):
                    bk, br = bank(); proj_tok(n0 + 4 + hf, hT, "trA", bk, br)
                    ACT(lambda e, bk=bk, hf=hf: e.activation(out=A4[:, hf * 512:(hf + 1) * 512], in_=bk[:, :], func=AF.Sigmoid), [br], ["F0"])
                    DVE(lambda e, bk=bk, hf=hf: e.tensor_tensor(out=B4[:, hf * 512:(hf + 1) * 512], in0=bk[:, :], in1=A4[:, hf * 512:(hf + 1) * 512], op=ALU.mult),
                        [br, "F0"], ["F1"])
                    yield
                POOL(lambda e: e.tensor_tensor(out=B4, in0=B4, in1=ga_rep[:], op=ALU.mult), ["F1", "ga_rep"], ["F1"])
                bkC, brC = bank()

                def f(e, bkC=bkC):
                    ins = None
                    for h in range(4):
                        ins = e.matmul(bkC[:, h * 128:(h + 1) * 128], lhsT=ktT[:, h * 128:(h + 1) * 128], rhs=qtT[:, h * 128:(h + 1) * 128], start=True, stop=True)
                    return ins
                PE(f, ["ktT", "qtT"], [brC])
                DVE(lambda e, bkC=bkC: e.tensor_tensor(out=v3(attm[:], 128), in0=v3(bkC[:, :], 128), in1=mgla[:, None, :].broadcast_to([128, 4, 128]), op=ALU.mult),
                    [brC, "mgla"], ["attm"])

                def dstate(ke, keres):
                    bks = [bank(), bank()]

                    def f(e):
                        ins = None
                        for h in range(4):
                            ins = e.matmul(bks[h // 2][0][:, (h % 2) * 256:(h % 2) * 256 + 256], lhsT=ke[:, h * 128:(h + 1) * 128],
                                           rhs=vb[:, h * 256:(h + 1) * 256], start=True, stop=True)
                        return ins
                    PE(f, [keres, "vb"], [bks[0][1], bks[1][1]])
                    return bks
                d0 = dstate(ke0, "ke0")
                for h in range(4):
                    DVE(lambda e, h=h: e.scalar_tensor_tensor(out=S1[:, h * 256:(h + 1) * 256], in0=S[:, h * 256:(h + 1) * 256], scalar=dec[:, 2 * h:2 * h + 1],
                                                              op0=ALU.mult, in1=d0[h // 2][0][:, (h % 2) * 256:(h % 2) * 256 + 256], op1=ALU.add),
                        ["S", "dec", d0[h // 2][1]], ["S1"])
                ACT(lambda e: e.copy(out=S1b[:], in_=S1[:]), ["S1"], ["S1b"])
                yield
                ob = [bank(), bank()]

                def f(e):
                    ins = None
                    for h in range(4):
                        o_ap = ob[h // 2][0][:, (h % 2) * 256:(h % 2) * 256 + 256]
                        e.matmul(o_ap, lhsT=attm[:, h * 128:(h + 1) * 128], rhs=vb[:, h * 256:(h + 1) * 256], start=True, stop=False)
                        e.matmul(o_ap, lhsT=qi0[:, h * 128:(h + 1) * 128], rhs=Sb[:, h * 256:(h + 1) * 256], start=False, stop=False)
                        ins = e.matmul(o_ap, lhsT=qi1[:, h * 128:(h + 1) * 128], rhs=S1b[:, h * 256:(h + 1) * 256], start=False, stop=True)
                    return ins
                PE(f, ["attm", "vb", "qi0", "qi1", "Sb", "S1b"], [ob[0][1], ob[1][1]])
                for h in range(4):
                    ACT(lambda e, h=h: e.activation(out=junkA[:, 0:256], in_=ob[h // 2][0][:, (h % 2) * 256:(h % 2) * 256 + 256], func=AF.Square,
                                                    accum_out=ssqa[:, h:h + 1]), [ob[h // 2][1]], ["ssqa", "junkA"])
                rstd_op(ssqa[:], rsa[:], 256, "ssqa", "rsa")
                o_a = tmB
                for h in range(4):
                    DVE(lambda e, h=h: e.scalar_tensor_tensor(out=o_a[:, h * 256:(h + 1) * 256], in0=ob[h // 2][0][:, (h % 2) * 256:(h % 2) * 256 + 256],
                                                              scalar=rsa[:, h:h + 1], op0=ALU.mult, in1=B4[:, h * 256:(h + 1) * 256], op1=ALU.mult),
                        [ob[h // 2][1], "rsa", "F1"], ["tmB"])
                if DBG:
                    DVE(lambda e: e.tensor_copy(out=A4, in_=o_a[:]), ["tmB"], ["F0"])
                    tap("oa", A4, tok0, "F0")
                transposes(o_a, 8, "tmB", trB, "trB")
                yield
                d1 = dstate(ke1, "ke1")
                for h in range(4):
                    DVE(lambda e, h=h: e.scalar_tensor_tensor(out=S[:, h * 256:(h + 1) * 256], in0=S1[:, h * 256:(h + 1) * 256], scalar=dec[:, 2 * h + 1:2 * h + 2],
                                                              op0=ALU.mult, in1=d1[h // 2][0][:, (h % 2) * 256:(h % 2) * 256 + 256], op1=ALU.add),
                        ["S1", "dec", d1[h // 2][1]], ["S"])
                ACT(lambda e: e.copy(out=Sb[:], in_=S[:]), ["S"], ["Sb"])
                yield


                for hf in range(2):
                    bk, br = bank(); proj_tok(n0 + 13 + hf, trB, "trB", bk, br)
                    DVE(lambda e, bk=bk, hf=hf: e.tensor_tensor(out=B4[:, hf * 512:(hf + 1) * 512], in0=bk[:, :], in1=D4[:, hf * 512:(hf + 1) * 512], op=ALU.mult),
                        [br, "F3"], ["F1"])
                    yield

            def swa():
                qn = C4
                for hf in range(2):
                    bk, br = bank(); proj_tok(n0 + 6 + hf, hT, "trA", bk, br)
                    ACT(lambda e, bk=bk, hf=hf: e.activation(out=A4[:, hf * 512:(hf + 1) * 512], in_=bk[:, :], func=AF.Square), [br], ["F0"])
                    DVE(lambda e, hf=hf: e.tensor_reduce(out=ssqq[:, hf * 8:(hf + 1) * 8], in_=v3(A4[:, hf * 512:(hf + 1) * 512], 64), axis=AX.X, op=ALU.add),
                        ["F0"], ["ssqq%d" % hf])
                    rstd_op(ssqq[:, hf * 8:(hf + 1) * 8], rsq[:, hf * 8:(hf + 1) * 8], 64, "ssqq%d" % hf, "rsq%d" % hf)
                    DVE(lambda e, bk=bk, hf=hf: e.tensor_tensor(out=v3(qn[:, hf * 512:(hf + 1) * 512], 64), in0=v3(bk[:, :], 64),
                                                                in1=rsq[:, hf * 8:(hf + 1) * 8, None].broadcast_to([128, 8, 64]), op=ALU.mult),
                        [br, "rsq%d" % hf], ["F2"])
                    yield
                POOL(lambda e: e.tensor_tensor(out=qn, in0=qn, in1=gq_rep[:], op=ALU.mult), ["F2", "gq_rep"], ["F2"])
                rope(qn, "F2", 16, ti)
                ACT(lambda e: e.copy(out=tmA[:], in_=qn), ["F2"], ["tmA"])
                transposes(tmA, 8, "tmA", trC, "trC")
                yield
                bk, br = bank(); proj_tok(n0 + 8, hT, "trA", bk, br)
                ACT(lambda e, bk=bk: e.activation(out=kq[:], in_=bk[:, 0:256], func=AF.Square), [br], ["kq"])
                DVE(lambda e: e.tensor_reduce(out=ssqk[:], in_=v3(kq[:], 64), axis=AX.X, op=ALU.add), ["kq"], ["ssqk"])
                rstd_op(ssqk[:], rsk[:], 64, "ssqk", "rsk")
                DVE(lambda e, bk=bk: e.tensor_tensor(out=v3(kn[:], 64), in0=v3(bk[:, 0:256], 64), in1=rsk[:, :, None].broadcast_to([128, 4, 64]), op=ALU.mult),
                    [br, "rsk"], ["kn"])
                ACT(lambda e, bk=bk: e.copy(out=vsw[par][:], in_=bk[:, 256:512]), [br], ["vsw%d" % par])
                POOL(lambda e: e.tensor_tensor(out=kn[:], in0=kn[:], in1=gk_rep[:], op=ALU.mult), ["kn", "gk_rep"], ["kn"])
                rope(kn[:], "kn", 4, ti)
                k2v = k2[:].rearrange("p (g c d) -> p g c d", c=2, d=64)
                ACT(lambda e: e.copy(out=k2v[:, :, 0, :], in_=v3(kn[:], 64)), ["kn"], ["k2"])
                ACT(lambda e: e.copy(out=k2v[:, :, 1, :], in_=v3(kn[:], 64)), ["kn"], ["k2"])
                transposes(k2, 4, "k2", kT2[par], "kT2_%d" % par)
                yield
                has_prev = ti > 0
                ppar = 1 - par
                for g in range(4):
                    for odd in range(2):
                        bk, br = bank()
                        lo = 64 * odd

                        def f(e, bk=bk, lo=lo, g=g):
                            ins = None
                            if has_prev:
                                ins = e.matmul(bk[:, 0:256], lhsT=kT2[ppar][lo:lo + 64, g * 128:(g + 1) * 128], rhs=trC[lo:lo + 64, 2 * g * 128:(2 * g + 2) * 128],
                                               start=True, stop=True)
                            ins = e.matmul(bk[:, 256:512], lhsT=kT2[par][lo:lo + 64, g * 128:(g + 1) * 128], rhs=trC[lo:lo + 64, 2 * g * 128:(2 * g + 2) * 128],
                                           start=True, stop=True)
                            return ins
                        PE(f, ["trC", "kT2_0", "kT2_1"], [br])
                        c0 = 0 if has_prev else 256
                        pr = praw[odd]
                        dst = (pTo if odd else pTe)[g]
                        dres = ("pTo%d" if odd else "pTe%d") % g
                        ACT(lambda e, bk=bk, pr=pr, c0=c0: e.activation(out=pr[:, c0:512], in_=bk[:, c0:512], func=AF.Exp), [br], ["praw%d" % odd])
                        POOL(lambda e, pr=pr, dst=dst, c0=c0: e.tensor_tensor(out=dst[:, c0:512], in0=pr[:, c0:512], in1=mk2[:, c0:512], op=ALU.mult),
                             ["praw%d" % odd, "mk2"], [dres])
                    yield
                OB = [bank(), bank()]
                SBk, SBr = bank()

                def f(e):
                    ins = None
                    for h in range(16):
                        g, j = h // 4, h % 4
                        src = (pTo if j % 2 else pTe)[g]
                        slot = j // 2
                        o_ap = OB[h // 8][0][:, (h % 8) * 64:(h % 8) * 64 + 64]
                        if has_prev:
                            e.matmul(o_ap, lhsT=src[:, slot * 128:(slot + 1) * 128], rhs=vsw[ppar][:, g * 64:(g + 1) * 64], start=True, stop=False)
                        ins = e.matmul(o_ap, lhsT=src[:, 256 + slot * 128:256 + (slot + 1) * 128], rhs=vsw[par][:, g * 64:(g + 1) * 64], start=(not has_prev), stop=True)
                    for h in range(16):
                        g, j = h // 4, h % 4
                        src = (pTo if j % 2 else pTe)[g]
                        slot = j // 2
                        s_ap = SBk[:, h:h + 1]
                        if has_prev:
                            e.matmul(s_ap, lhsT=src[:, slot * 128:(slot + 1) * 128], rhs=ones_bf[:, 0:1], start=True, stop=False)
                        ins = e.matmul(s_ap, lhsT=src[:, 256 + slot * 128:256 + (slot + 1) * 128], rhs=ones_bf[:, 0:1], start=(not has_prev), stop=True)
                    return ins
                PE(f, ["pTe0", "pTe1", "pTe2", "pTe3", "pTo0", "pTo1", "pTo2", "pTo3", "vsw0", "vsw1", "ones_bf"], [OB[0][1], OB[1][1], SBr])
                DVE(lambda e: e.tensor_tensor(out=den[:], in0=SBk[:, 0:16], in1=esink[:], op=ALU.add), [SBr, "esink"], ["den"])
                DVE(lambda e: e.reciprocal(out=rden[:], in_=den[:]), ["den"], ["rden"])
                o_b = tmB
                for hf in range(2):
                    DVE(lambda e, hf=hf: e.tensor_tensor(out=v3(o_b[:, hf * 512:(hf + 1) * 512], 64), in0=v3(OB[hf][0][:, :], 64),
                                                         in1=rden[:, hf * 8:(hf + 1) * 8, None].broadcast_to([128, 8, 64]), op=ALU.mult),
                        [OB[hf][1], "rden"], ["tmB"])
                if DBG:
                    DVE(lambda e: e.tensor_copy(out=A4, in_=o_b[:]), ["tmB"], ["F0"])
                    tap("ob", A4, tok0, "F0")
                transposes(o_b, 8, "tmB", trD, "trD")
                yield

                for hf in range(2):
                    bk, br = bank(); proj_tok(n0 + 15 + hf, trD, "trD", bk, br)
                    DVE(lambda e, bk=bk, hf=hf: e.tensor_tensor(out=C4[:, hf * 512:(hf + 1) * 512], in0=bk[:, :], in1=E4[:, hf * 512:(hf + 1) * 512], op=ALU.mult),
                        [br, "F4"], ["F2"])
                    yield

            def gates():
                for hf in range(2):
                    bk, br = bank(); proj_tok(n0 + 9 + hf, hT, "trA", bk, br)
                    ACT(lambda e, bk=bk, hf=hf: e.activation(out=D4[:, hf * 512:(hf + 1) * 512], in_=bk[:, :], func=AF.Sigmoid), [br], ["F3"])
                    yield
                for hf in range(2):
                    bk, br = bank(); proj_tok(n0 + 11 + hf, hT, "trA", bk, br)
                    ACT(lambda e, bk=bk, hf=hf: e.activation(out=E4[:, hf * 512:(hf + 1) * 512], in_=bk[:, :], func=AF.Sigmoid), [br], ["F4"])
                    yield

            subs = [gla(), swa(), gates()]
            while subs:
                for g_ in list(subs):
                    try:
                        next(g_)
                    except StopIteration:
                        subs.remove(g_)
                yield
            POOL(lambda e: e.tensor_tensor(out=tmA[:], in0=B4, in1=C4, op=ALU.add), ["F1", "F2"], ["tmA"])
            transposes(tmA, 8, "tmA", trB, "trB")
            yield
            if STAGE <= 2.8:
                return
            for hf in range(2):
                bk, br = bank(); proj_tok(n0 + 17 + hf, trB, "trB", bk, br)
                DVE(lambda e, bk=bk, hf=hf: e.tensor_tensor(out=x2[:, hf * 512:(hf + 1) * 512], in0=bk[:, :], in1=xt[:, hf * 512:(hf + 1) * 512], op=ALU.add),
                    [br, XT], [X2])
                yield
            tap("x2", x2[:], tok0, X2)

        def peer(si, ti, nxt):
            gi = si * NT + ti
            tok0 = gi * 128
            n0 = gi * NCH
            xt = xts[gi % 2]; x2 = x2s[gi % 2]
            XT = "xt%d" % (gi % 2); X2 = "x2_%d" % (gi % 2)
            ACT(lambda e: e.activation(out=junkA[:], in_=x2[:], func=AF.Square, accum_out=ss[:, 1:2]), [X2], ["ss1", "junkA"])
            rstd_op(ss[:, 1:2], rs[:, 1:2], 1024, "ss1", "rs1")
            DVE(lambda e: e.tensor_scalar(out=tmA[:], in0=x2[:], scalar1=rs[:, 1:2], scalar2=None, op0=ALU.mult), [X2, "rs1"], ["tmA"])
            DVE(lambda e: e.scalar_tensor_tensor(out=XN[:, :], in0=x2[:], scalar=rs[:, 1:2], op0=ALU.mult, in1=gffn_rep[:], op1=ALU.mult),
                [X2, "rs1", "gffn_rep"], ["XN"])
            transposes(tmA, 8, "tmA", trA, "trA")
            for cc in range(4):
                bk, br = bank(); proj_feat(n0 + 19 + cc, trA, "trA", bk, br)
                ACT(lambda e, bk=bk, cc=cc: e.copy(out=qpT[:, cc * 512:(cc + 1) * 512], in_=bk[:, :]), [br], ["qpT"])
            sc = RG[:, 0:4096].bitcast(F32)
            SCR = ["rg0", "rg1", "rgu0", "rgu1"]
            for half in range(2):
                scb = [bank(), bank()]

                def f(e, scb=scb, half=half):
                    ins = None
                    for q in range(8):
                        hp = half * 8 + q
                        ins = e.matmul(scb[q // 4][0][:, (q % 4) * 128:(q % 4 + 1) * 128], lhsT=qpT[:, hp * 128:(hp + 1) * 128], rhs=skT[:, hp * 128:(hp + 1) * 128],
                                       start=True, stop=True)
                    return ins
                PE(f, ["qpT", "skT"], [b_[1] for b_ in scb])
                for i in range(2):
                    ACT(lambda e, i=i, scb=scb, half=half: e.copy(out=sc[:, (half * 2 + i) * 512:(half * 2 + i + 1) * 512], in_=scb[i][0][:, :]),
                        [scb[i][1]], SCR)

            def selback():
                for hg in range(4):
                    hps = [hg * 4 + q for q in range(4)]
                    sls = [sc[:, hp * 128:(hp + 1) * 128] for hp in hps]
                    for q, hp in enumerate(hps):
                        DVE(lambda e, sl=sls[q], hp=hp: e.max(out=tv[:, hp * 16:hp * 16 + 8], in_=sl), SCR, ["tv%d" % hp])
                    for q, hp in enumerate(hps):
                        DVE(lambda e, sl=sls[q], hp=hp: e.max_index(out=tiu[:, hp * 16:hp * 16 + 8], in_max=tv[:, hp * 16:hp * 16 + 8], in_values=sl), SCR + ["tv%d" % hp], ["tiu%d" % hp])
                    for q, hp in enumerate(hps):
                        DVE(lambda e, sl=sls[q], hp=hp, q=q: e.match_replace(out=sc2L[q][:], in_to_replace=tv[:, hp * 16:hp * 16 + 8], in_values=sl, imm_value=-1e30),
                            SCR + ["tv%d" % hp], ["sc2_%d" % q])
                    for q, hp in enumerate(hps):
                        DVE(lambda e, hp=hp, q=q: e.max(out=tv[:, hp * 16 + 8:hp * 16 + 16], in_=sc2L[q][:]), ["sc2_%d" % q], ["tvb%d" % hp])
                    for q, hp in enumerate(hps):
                        DVE(lambda e, hp=hp, q=q: e.max_index(out=tiu[:, hp * 16 + 8:hp * 16 + 16], in_max=tv[:, hp * 16 + 8:hp * 16 + 16], in_values=sc2L[q][:]),
                            ["sc2_%d" % q, "tvb%d" % hp], ["tiub%d" % hp])
                    yield
                TVALL = ["tv%d" % hp for hp in range(16)] + ["tvb%d" % hp for hp in range(16)]
                TIALL = ["tiu%d" % hp for hp in range(16)] + ["tiub%d" % hp for hp in range(16)]
                DVE(lambda e: e.tensor_copy(out=tif[:], in_=tiu[:]), TIALL, ["tif"])
                tfv = tif[:].rearrange("p (h c k) -> p h c k", c=2, k=16)
                tvv = tv[:].rearrange("p (h c k) -> p h c k", c=2, k=16)
                DVE(lambda e: e.tensor_scalar(out=tfv[:, :, 0, :], in0=tfv[:, :, 0, :], scalar1=128.0, scalar2=None, op0=ALU.mult), ["tif"], ["tif"])
                for hg in range(4):
                    hs_ = [hg * 2, hg * 2 + 1]
                    for q, h in enumerate(hs_):
                        for (dstL, srcv, rres, wn) in ((candL, tvv, TVALL, "cand_%d"), (ciL, tfv, ["tif"], "ci_%d")):
                            dst = dstL[q]
                            DVE(lambda e, h=h, dst=dst, srcv=srcv: e.tensor_tensor(out=v3(dst[:, 0:64], 16), in0=srcv[:, h, 0, 0:4, None].broadcast_to([128, 4, 16]),
                                                                                   in1=srcv[:, h, 1, None, :].broadcast_to([128, 4, 16]), op=ALU.add), rres, [wn % q])
                            DVE(lambda e, h=h, dst=dst, srcv=srcv: e.tensor_tensor(out=v3(dst[:, 64:112], 4), in0=srcv[:, h, 0, 4:16, None].broadcast_to([128, 12, 4]),
                                                                                   in1=srcv[:, h, 1, None, 0:4].broadcast_to([128, 12, 4]), op=ALU.add), rres, [wn % q])
                    for q, h in enumerate(hs_):
                        DVE(lambda e, h=h, q=q: e.max(out=bv[:, h * 16:h * 16 + 8], in_=candL[q][:]), ["cand_%d" % q], ["bv%d" % h])
                    for q, h in enumerate(hs_):
                        DVE(lambda e, h=h, q=q: e.match_replace(out=cand2L[q][:], in_to_replace=bv[:, h * 16:h * 16 + 8], in_values=candL[q][:], imm_value=-1e30),
                            ["cand_%d" % q, "bv%d" % h], ["cand2_%d" % q])
                    for q, h in enumerate(hs_):
                        DVE(lambda e, h=h, q=q: e.max(out=bv[:, h * 16 + 8:h * 16 + 16], in_=cand2L[q][:]), ["cand2_%d" % q], ["bvb%d" % h])
                    yield
                    for j in range(16):
                        for q, h in enumerate(hs_):
                            r = h * 16 + j
                            DVE(lambda e, r=r, q=q: e.scalar_tensor_tensor(out=junkD[:, (r % 8) * 128:(r % 8) * 128 + 112], in0=candL[q][:], scalar=bv[:, r:r + 1], op0=ALU.is_equal,
                                                                           in1=ciL[q][:], op1=ALU.mult, accum_out=idxf[:, r:r + 1]),
                                ["cand_%d" % q, "ci_%d" % q, "bv%d" % h, "bvb%d" % h], ["idxf%d" % r, "jd%d" % (r % 8)])
                        if j % 4 == 3:
                            yield
                BVALL = ["bv%d" % h for h in range(8)] + ["bvb%d" % h for h in range(8)]
                DVE(lambda e: e.tensor_scalar(out=idxf[:], in0=idxf[:], scalar1=16383.0, scalar2=0.0, op0=ALU.min, op1=ALU.max), ["idxf%d" % r for r in range(128)], ["idxf"])
                DVE(lambda e: e.tensor_copy(out=idxi[:], in_=idxf[:]), ["idxf"], ["idxi"])
                DVE(lambda e: e.tensor_scalar(out=negm[:], in0=v3(bv[:], 16)[:, :, 0], scalar1=-1.0, scalar2=None, op0=ALU.mult), BVALL, ["negm"])
                for h in range(8):
                    ACT(lambda e, h=h: e.activation(out=eg[:, h * 16:(h + 1) * 16], in_=bv[:, h * 16:(h + 1) * 16], func=AF.Exp, bias=negm[:, h:h + 1], scale=1.0,
                                                    accum_out=Z[:, h:h + 1]), BVALL + ["negm"], ["eg", "Z"])
                DVE(lambda e: e.reciprocal(out=rZ[:], in_=Z[:]), ["Z"], ["rZ"])
                DVE(lambda e: e.tensor_tensor(out=v3(gate[:], 16), in0=v3(eg[:], 16), in1=rZ[:, :, None].broadcast_to([128, 8, 16]), op=ALU.mult), ["eg", "rZ"], ["gate"])
                if DBG:
                    tap("idx", idxf[:], tok0, "idxf")

            sb_gen = selback()
            gens = [(sb_gen, 1)] + ([(nxt, 2)] if nxt is not None else [])
            while gens:
                for (g_, k_) in list(gens):
                    for _ in range(k_):
                        try:
                            next(g_)
                        except StopIteration:
                            gens.remove((g_, k_))
                            break

            PA = bank(); PB = bank()

            def gather(r):
                slot = r % NG
                p.op("pool", lambda e: e.indirect_dma_start(out=rg[slot], out_offset=None, in_=uvs,
                                                            in_offset=bass.IndirectOffsetOnAxis(ap=idxi[:, r:r + 1], axis=0)),
                     ["idxi"] + UVRES, ["rg%d" % slot, "rgu%d" % slot], dma="gr%d" % slot)

            def dot(r):
                slot = r % NG
                DVE(lambda e: e.scalar_tensor_tensor(out=rg[slot][:, 0:1024], in0=rg[slot][:, 0:1024], scalar=1.0, op0=ALU.mult, in1=XN[:, :], op1=ALU.mult,
                                                     accum_out=hd[:, r:r + 1]), ["rg%d" % slot, "XN"], ["hd%d" % r, "rgu%d" % slot])

            def combine(r):
                slot = r % NG
                ds = r % ND
                ACT(lambda e: e.activation(out=gl[:, r:r + 1], in_=hd[:, r:r + 1], func=AF.Gelu), ["hd%d" % r], ["gl%d" % r])
                DVE(lambda e: e.tensor_scalar(out=Dr[ds][:], in0=identf[:], scalar1=gl[:, r:r + 1], scalar2=gate[:, r:r + 1], op0=ALU.mult, op1=ALU.mult),
                    ["identf", "gl%d" % r, "gate"], ["Dr%d" % ds])

                def f(e):
                    e.matmul(PA[0][:, :], lhsT=Dr[ds][:], rhs=rg[slot][:, 1024:1536], start=(r == 0), stop=(r == NR - 1))
                    return e.matmul(PB[0][:, :], lhsT=Dr[ds][:], rhs=rg[slot][:, 1536:2048], start=(r == 0), stop=(r == NR - 1))
                PE(f, ["Dr%d" % ds, "rg%d" % slot], [PA[1], PB[1]])

            for r in range(min(NG, NR)):
                gather(r)
            dot(0)
            for r in range(NR):
                if r + 1 < NR:
                    dot(r + 1)
                combine(r)
                if r + NG < NR:
                    gather(r + NG)
            DVE(lambda e: e.tensor_tensor(out=xt[:, 0:512], in0=PA[0][:, :], in1=x2[:, 0:512], op=ALU.add), [PA[1], X2], [XT])
            DVE(lambda e: e.tensor_tensor(out=xt[:, 512:1024], in0=PB[0][:, :], in1=x2[:, 512:1024], op=ALU.add), [PB[1], X2], [XT])
            DMA(lambda e: e.dma_start(out=out_d[tok0:tok0 + 128, :], in_=xt[:]), [XT], (), sem="sto%d" % (gi % 2))

        tiles = [(si, ti) for si in range(NSEQ) for ti in range(NT)]
        p.dry = True
        sv = bstate["i"]
        for _ in mixer(0, 1 if NT > 1 else 0):
            pass
        bstate["i"] = sv
        wstate["order"] = list(wstate["rec"]) + [19, 20, 21, 22]
        assert sorted(wstate["order"]) == list(range(NCH)), wstate["order"]
        wstate["use"] = -1
        wstate["lastc"] = None
        p.dry = False
        if STAGE >= 1 and STAGE <= 3:
            for (si, ti) in tiles:
                for _ in mixer(si, ti):
                    pass
        elif STAGE > 3:
            for _ in mixer(*tiles[0]):
                pass
            for k, (si, ti) in enumerate(tiles):
                nxt = mixer(*tiles[k + 1]) if k + 1 < len(tiles) else None
                peer(si, ti, nxt)
        if STAGE == 0:
            DMA(lambda e: e.dma_start(out=xts[0][:, 0:128], in_=wscr[22][:, 0:256].bitcast(F32)), ["wscr22"], ["xt0"], sem="ldx0")
            DMA(lambda e: e.dma_start(out=out_d[0:128, 0:128], in_=xts[0][:, 0:128]), ["xt0"], (), sem="sto")
        evs = [(p.dsem[k], p.dcnt[k]) for k in p.dsem if k.startswith("sto") or k.startswith("dbg_")]
        p.final_wait("sp", evs)
        p.emit()
    return nc


def host_consts():
    ident = np.eye(128, dtype=np.float32)
    s = np.arange(128)[:, None]
    t = np.arange(128)[None, :]
    same = (s // 64) == (t // 64)
    tri = np.where(same & (s <= t), -1.0 / 16.0, 0.0).astype(np.float32)
    tri2 = np.where(same & (s > t), -1.0 / 16.0, 0.0).astype(np.float32)
    mgla = np.where(same & (s <= t), 1.0, 0.0).astype(np.float32)
    mown = np.where(s <= t, 1.0, 0.0).astype(np.float32)
    mprev = np.where(s > t, 1.0, 0.0).astype(np.float32)
    cmat = np.stack([ident, tri, tri2, mgla, mown]).astype(np.float32)
    cm = np.stack([(np.arange(128) < 64), (np.arange(128) >= 64)], axis=1).astype(np.float32)
    pos = np.arange(2048, dtype=np.float32)
    inv_freq = (np.float32(500000.0) ** (-np.arange(0, 16, 2, dtype=np.float32) / np.float32(16))).astype(np.float32)
    ang = (pos[:, None] * inv_freq[None, :]).astype(np.float32)
    cos = np.cos(ang).astype(np.float32).reshape(16, 128, 8).transpose(1, 0, 2).reshape(128, 128)
    sin = np.sin(ang).astype(np.float32).reshape(16, 128, 8).transpose(1, 0, 2).reshape(128, 128)
    return dict(cmat=cmat, mprev=mprev, cm=cm, rcos=np.ascontiguousarray(cos), rsin=np.ascontiguousarray(sin))


def make_in_maps(inputs, n_cores, NSEQ, NT):
    f = lambda a: np.ascontiguousarray(np.asarray(a, dtype=np.float32))
    x = f(inputs["x"])
    seq = NT * 128
    xs = x.reshape(-1, seq, 1024)
    assert xs.shape[0] == n_cores * NSEQ
    shared = dict(
        w_in=f(inputs["w_in"][0]), w_branch_a=f(inputs["w_branch_a"][0]), w_branch_b=f(inputs["w_branch_b"][0]),
        w_out=f(inputs["w_out"][0]), w_peer_q=f(inputs["w_peer_q"][0]),
        sub_keys=f(inputs["peer_sub_keys"][0]).reshape(16, 128, 128),
        peer_u=f(inputs["peer_u"][0]), peer_v=f(inputs["peer_v"][0]),
        gmix_pk=np.ascontiguousarray(f(inputs["norm_mix_g"][0]).reshape(8, 128).T),
        gffn_pk=np.ascontiguousarray(f(inputs["norm_ffn_g"][0]).reshape(8, 128).T),
        gffn_row=f(inputs["norm_ffn_g"][0]).reshape(1, 1024),
        ga_row=f(inputs["gla_norm_g"][0]).reshape(1, 256),
        gq_row=f(inputs["q_norm_g"][0]).reshape(1, 64),
        gk_row=f(inputs["k_norm_g"][0]).reshape(1, 64),
        sinks_row=f(inputs["attn_sinks"][0]).reshape(1, 16),
        w_gk2=f(inputs["w_gk2"][0]), b_gk=f(inputs["b_gk"][0]).reshape(1, 512),
    )
    shared.update(host_consts())
    maps = []
    for c in range(n_cores):
        m = dict(shared)
        m["x"] = np.ascontiguousarray(xs[c * NSEQ:(c + 1) * NSEQ].reshape(NSEQ * seq, 1024))
        maps.append(m)
    return maps


def kernel(**inputs):
    n = 8
    NSEQ, NT = 2, 16
    nc = build_nc(NSEQ, NT)
    in_maps = make_in_maps(inputs, n, NSEQ, NT)
    res = run_bass_kernel_spmd(nc, in_maps, core_ids=list(range(n)))
    out = np.concatenate([np.asarray(r["out"]).reshape(NSEQ, NT * 128, 1024) for r in res.results], axis=0)
    return out.astype(np.float32)
```

```python
import numpy as np
import concourse.bass as bass
import concourse.mybir as mybir
from concourse.bass_utils import run_bass_kernel_spmd
from contextlib import ExitStack

F32 = mybir.dt.float32
BF16 = mybir.dt.bfloat16
U32 = mybir.dt.uint32
I32 = mybir.dt.int32
AF = mybir.ActivationFunctionType
ALU = mybir.AluOpType
AX = mybir.AxisListType

EPS = 1e-6
W_IN_COLS = [0, 512, 1024, 1536, 2048, 2560, 3088, 3600, 4112, 4624, 5136, 5648, 6160]
NCH = 23
NSLOT = 3
NG = 7
ND = 4


class Prog:
    ENGS = ("pe", "dve", "act", "pool", "sp")

    def __init__(self, nc, es):
        self.nc = nc
        self.es = es
        self.q = {e: [] for e in self.ENGS}
        self.sem = {e: es.enter_context(nc.semaphore("sem_" + e)) for e in self.ENGS}
        self.semeng = {id(self.sem[e]): e for e in self.ENGS}
        self.cnt = {e: 0 for e in self.ENGS}
        self.waited = {e: {} for e in self.ENGS}
        self.lastw = {}
        self.readers = {}
        self.dsem = {}
        self.dcnt = {}

    def op(self, eng, fn, reads=(), writes=(), dma=None):
        if getattr(self, "dry", False):
            return None
        deps = []
        for r in reads:
            if r in self.lastw:
                deps.append(self.lastw[r])
        for w in writes:
            if w in self.lastw:
                deps.append(self.lastw[w])
            deps.extend(self.readers.get(w, []))
        wd = self.waited[eng]
        best = {}
        for (s, v) in deps:
            key = id(s)
            if eng == "pe" and self.semeng.get(key) == "pe":
                continue
            if wd.get(key, 0) >= v:
                continue
            if key not in best or best[key][1] < v:
                best[key] = (s, v)
        waits = []
        for key, (s, v) in best.items():
            wd[key] = v
            waits.append((s, v))
        if dma is None:
            self.cnt[eng] += 1
            ev = (self.sem[eng], self.cnt[eng])
            inc = 1
        else:
            if dma not in self.dsem:
                self.dsem[dma] = self.es.enter_context(self.nc.semaphore("d_" + dma))
                self.dcnt[dma] = 0
            self.dcnt[dma] += 16
            ev = (self.dsem[dma], self.dcnt[dma])
            inc = 16
        self.q[eng].append((waits, fn, ev[0], inc))
        for w in writes:
            self.lastw[w] = ev
            self.readers[w] = []
        for r in reads:
            if r not in writes:
                self.readers.setdefault(r, []).append(ev)
        return ev

    def final_wait(self, eng, evs):
        self.q[eng].append((list(evs), None, None, 0))

    def emit(self):
        nc = self.nc
        with nc.Block() as block:
            def mk(ename):
                def body(e):
                    for (waits, fn, s, inc) in self.q[ename]:
                        for (ws, wv) in waits:
                            e.wait_ge(ws, wv)
                        if fn is not None:
                            ins = fn(e)
                            ins.then_inc(s, inc)
                return body
            block.tensor(mk("pe"))
            block.vector(mk("dve"))
            block.scalar(mk("act"))
            block.gpsimd(mk("pool"))
            block.sync(mk("sp"))


def build_nc(NSEQ=2, NT=16, DBG=False, NR=128, STAGE=99):
    NTOK = NSEQ * NT * 128
    nc = bass.Bass("TRN2", target_bir_lowering=False)
    es = ExitStack()

    def din(name, shape, dt=F32):
        return nc.dram_tensor(name, list(shape), dt, kind="ExternalInput").ap()

    x_d = din("x", [NTOK, 1024])
    w_in_d = din("w_in", [1024, 6672])
    wa_d = din("w_branch_a", [1024, 1024])
    wb_d = din("w_branch_b", [1024, 1024])
    wo_d = din("w_out", [1024, 1024])
    wpq_d = din("w_peer_q", [1024, 2048])
    sk_d = din("sub_keys", [16, 128, 128])
    pu_d = din("peer_u", [16384, 1024])
    pv_d = din("peer_v", [16384, 1024])
    gmix_d = din("gmix_pk", [128, 8])
    gffn_d = din("gffn_pk", [128, 8])
    gffn_row_d = din("gffn_row", [1, 1024])
    ga_d = din("ga_row", [1, 256])
    gq_d = din("gq_row", [1, 64])
    gk_d = din("gk_row", [1, 64])
    sinks_d = din("sinks_row", [1, 16])
    wgk2_d = din("w_gk2", [16, 512])
    bgk_d = din("b_gk", [1, 512])
    cmat_d = din("cmat", [5, 128, 128])
    mprev_d = din("mprev", [128, 128])
    cm_d = din("cm", [128, 2])
    cos_d = din("rcos", [128, 16 * 8])
    sin_d = din("rsin", [128, 16 * 8])
    out_d = nc.dram_tensor("out", [NTOK, 1024], F32, kind="ExternalOutput").ap()
    wscr = nc.dram_tensor("wscr", [NCH, 128, 4096], BF16, kind="Internal").ap()
    uvs = nc.dram_tensor("uvscr", [16384, 2048], BF16, kind="Internal").ap()
    dbg_d = {}
    if DBG:
        for nm in ("x2", "ya", "yb", "oa", "ob"):
            dbg_d[nm] = nc.dram_tensor("dbg_" + nm, [NTOK, 1024], F32, kind="ExternalOutput").ap()
        dbg_d["idx"] = nc.dram_tensor("dbg_idx", [NTOK, 128], F32, kind="ExternalOutput").ap()
        dbg_d["wgt"] = nc.dram_tensor("dbg_wgt", [NTOK, 128], F32, kind="ExternalOutput").ap()

    with es:
        p = Prog(nc, es)
        p.dry = False

        def sb(name, shape, dt=F32):
            return es.enter_context(nc.sbuf_tensor("s_" + name, list(shape), dt))

        identf = sb("identf", [128, 128]); identb = sb("identb", [128, 128], BF16)
        tri = sb("tri", [128, 128]); tri2 = sb("tri2", [128, 128])
        mgla = sb("mgla", [128, 128]); mk2 = sb("mk2", [128, 512])
        cm = sb("cm", [128, 2])
        rcos = sb("rcos", [128, 128]); rsin = sb("rsin", [128, 128])
        gmix = sb("gmix", [128, 8]); gffn = sb("gffn", [128, 8])
        gffn_rep = sb("gffn_rep", [128, 1024])
        ga_rep = sb("ga_rep", [128, 1024]); gq_rep = sb("gq_rep", [128, 1024]); gk_rep = sb("gk_rep", [128, 256])
        g256 = sb("g256", [128, 256]); g64q = sb("g64q", [128, 64]); g64k = sb("g64k", [128, 64])
        esink = sb("esink", [128, 16])
        wgk = sb("wgk", [17, 512]); glra = sb("glra", [17, 128])
        wglr_f = sb("wglr_f", [128, 128]); wglr = sb("wglr", [128, 128], BF16)
        skT = sb("skT", [128, 2048], BF16)
        ones_bf = sb("ones_bf", [128, 2], BF16)
        wring = [sb("wring%d" % i, [128, 4096], BF16) for i in range(NSLOT)]
        FS = sb("FS", [128, 5120])
        xts = [sb("xt%d" % i, [128, 1024]) for i in range(2)]; x2s = [sb("x2_%d" % i, [128, 1024]) for i in range(2)]
        junkA = sb("junkA", [128, 1024], BF16); junkD = sb("junkD", [128, 1024], BF16)
        qTs = sb("qTs", [128, 512]); kTs = sb("kTs", [128, 512]); ks = sb("ks", [128, 512])
        Lb = sb("Lb", [128, 512]); bTs = sb("bTs", [128, 512])
        E1 = sb("E1", [128, 512], BF16); E2 = sb("E2", [128, 512], BF16); E3 = sb("E3", [128, 512], BF16); Er = sb("Er", [128, 512], BF16)
        qtT = sb("qtT", [128, 512], BF16); ktT = sb("ktT", [128, 512], BF16)
        qi0 = sb("qi0", [128, 512], BF16); qi1 = sb("qi1", [128, 512], BF16)
        ke0 = sb("ke0", [128, 512], BF16); ke1 = sb("ke1", [128, 512], BF16)
        attm = sb("attm", [128, 512], BF16)
        vb = sb("vb", [128, 1024], BF16)
        S = sb("S", [128, 1024]); S1 = sb("S1", [128, 1024])
        Sb = sb("Sb", [128, 1024], BF16); S1b = sb("S1b", [128, 1024], BF16)
        nbref = sb("nbref", [128, 8]); pbref = sb("pbref", [128, 8]); dec = sb("dec", [128, 8])
        trA = sb("trA", [128, 1024], BF16); trB = sb("trB", [128, 1024], BF16)
        trC = sb("trC", [128, 1024], BF16); trD = sb("trD", [128, 1024], BF16)
        tmA = sb("tmA", [128, 1024], BF16); tmB = sb("tmB", [128, 1024], BF16)
        kq = sb("kq", [128, 256]); kn = sb("kn", [128, 256]); k2 = sb("k2", [128, 512], BF16)
        kT2 = [sb("kT2_%d" % i, [128, 512], BF16) for i in range(2)]
        vsw = [sb("vsw_%d" % i, [128, 256], BF16) for i in range(2)]
        rt = [sb("rt%d" % i, [128, 128]) for i in range(4)]
        praw = [sb("praw%d" % i, [128, 512], BF16) for i in range(2)]
        pTe = [sb("pTe%d" % i, [128, 512], BF16) for i in range(4)]
        pTo = [sb("pTo%d" % i, [128, 512], BF16) for i in range(4)]
        ss = sb("ss", [128, 2]); rs = sb("rs", [128, 2])
        ssqa = sb("ssqa", [128, 4]); rsa = sb("rsa", [128, 4])
        ssqq = sb("ssqq", [128, 16]); rsq = sb("rsq", [128, 16])
        ssqk = sb("ssqk", [128, 4]); rsk = sb("rsk", [128, 4])
        den = sb("den", [128, 16]); rden = sb("rden", [128, 16])
        qpT = sb("qpT", [128, 2048], BF16)
        sc2L = [sb("sc2_%d" % i, [128, 128]) for i in range(4)]
        tv = sb("tv", [128, 256]); tiu = sb("tiu", [128, 256], U32); tif = sb("tif", [128, 256])
        candL = [sb("cand_%d" % i, [128, 112]) for i in range(2)]; cand2L = [sb("cand2_%d" % i, [128, 112]) for i in range(2)]
        ciL = [sb("ci_%d" % i, [128, 112]) for i in range(2)]
        bv = sb("bv", [128, 128]); idxf = sb("idxf", [128, 128]); idxi = sb("idxi", [128, 128], I32)
        negm = sb("negm", [128, 8]); Z = sb("Z", [128, 8]); rZ = sb("rZ", [128, 8])
        eg = sb("eg", [128, 128]); gate = sb("gate", [128, 128])
        hd = sb("hd", [128, 128]); gl = sb("gl", [128, 128]); wgt = sb("wgt", [128, 128])
        Dr = [sb("Dr%d" % i, [128, 128], BF16) for i in range(ND)]
        RG = sb("RG", [128, NG * 2048], BF16)
        rg = [RG[:, i * 2048:(i + 1) * 2048] for i in range(NG)]

        def F(i, n=1):
            return FS[:, i * 1024:(i + n) * 1024]

        Tps = es.enter_context(nc.psum_tensor("Tps", [128, 1024], BF16))
        XN = es.enter_context(nc.psum_tensor("XN", [128, 1024], F32))
        NBK = 5
        banks = [es.enter_context(nc.psum_tensor("bk%d" % i, [128, 512], F32)) for i in range(NBK)]
        bstate = {"i": 0}

        def bank():
            i = bstate["i"]
            bstate["i"] = (i + 1) % NBK
            return banks[i], "bk%d" % i

        def PE(fn, r=(), w=()): return p.op("pe", fn, r, w)
        def DVE(fn, r=(), w=()): return p.op("dve", fn, r, w)
        def ACT(fn, r=(), w=()): return p.op("act", fn, r, w)
        def POOL(fn, r=(), w=()): return p.op("pool", fn, r, w)
        def DMA(fn, r=(), w=(), sem=None): return p.op("sp", fn, r, w, dma=sem)

        ldc = {"n": 0, "res": []}

        def load(out_ap, in_ap, wres):
            ldc["n"] += 1
            ldc["res"].append(wres)
            return DMA(lambda e: e.dma_start(out=out_ap, in_=in_ap), (), [wres], sem="ldc")

        def load_barrier():
            ev = (p.dsem["ldc"], p.dcnt["ldc"])
            for r in ldc["res"]:
                p.lastw[r] = ev
            ldc["res"] = []

        def transposes(src, nblk, rres, dst, wres, eng="act"):
            def f(e):
                ins = None
                for kc in range(nblk):
                    ins = e.transpose(out=Tps[:, kc * 128:(kc + 1) * 128], in_=src[:, kc * 128:(kc + 1) * 128], identity=identb[:])
                return ins
            PE(f, [rres, "identb"], ["T"])
            ACT(lambda e: e.copy(out=dst[:, 0:nblk * 128], in_=Tps[:, 0:nblk * 128]), ["T"], [wres])

        UVRES = ["ccs%d" % i for i in range(4)]
        ci_ = 0
        for q8 in range(16):
            r0, r1 = q8 * 1024, (q8 + 1) * 1024
            for (src_d, c0) in ((pu_d, 0), (pv_d, 1024)):
                p.op("pool", lambda e, r0=r0, r1=r1, src_d=src_d, c0=c0: e.dma_start(out=uvs[r0:r1, c0:c0 + 1024], in_=src_d[r0:r1, :]),
                     (), ["ccs%d" % (ci_ % 4)], dma="cc%d" % (ci_ % 4))
                ci_ += 1
        load(identf[:], cmat_d[0], "identf"); load(tri[:], cmat_d[1], "tri"); load(tri2[:], cmat_d[2], "tri2")
        load(mgla[:], cmat_d[3], "mgla")
        load(mk2[:, 0:128], mprev_d, "mk2"); load(mk2[:, 128:256], mprev_d, "mk2")
        load(mk2[:, 256:384], cmat_d[4], "mk2"); load(mk2[:, 384:512], cmat_d[4], "mk2")
        load(cm[:], cm_d, "cm"); load(rcos[:], cos_d, "rcos"); load(rsin[:], sin_d, "rsin")
        load(gmix[:], gmix_d, "gmix"); load(gffn[:], gffn_d, "gffn")
        load(gffn_rep[:], gffn_row_d.broadcast_to([128, 1024]), "gffn_rep")
        load(g256[:], ga_d.broadcast_to([128, 256]), "g256")
        load(g64q[:], gq_d.broadcast_to([128, 64]), "g64q"); load(g64k[:], gk_d.broadcast_to([128, 64]), "g64k")
        load(esink[:], sinks_d.broadcast_to([128, 16]), "esink")
        load(wgk[0:16, :], wgk2_d, "wgk"); load(wgk[16:17, :], bgk_d, "wgk")
        load(wglr_f[:].rearrange("p (k n) -> p k n", n=16),
             w_in_d.rearrange("(k p) n -> p k n", p=128)[:, :, 3072:3088], "wglr_f")
        for q4 in range(4):
            load(F(0, 2)[:, q4 * 512:(q4 + 1) * 512].rearrange("p (a d) -> p a d", d=128),
                 sk_d[q4 * 4:(q4 + 1) * 4].rearrange("a k d -> k a d"), "skl%d" % q4)
        load_barrier()
        DVE(lambda e: e.tensor_copy(out=identb[:], in_=identf[:]), ["identf"], ["identb"])
        DVE(lambda e: e.tensor_copy(out=ga_rep[:].rearrange("p (h e) -> p h e", e=256),
                                    in_=g256[:, None, :].broadcast_to([128, 4, 256])), ["g256"], ["ga_rep"])
        DVE(lambda e: e.tensor_scalar(out=gq_rep[:].rearrange("p (h e) -> p h e", e=64),
                                      in0=g64q[:, None, :].broadcast_to([128, 16, 64]), scalar1=0.125, scalar2=None, op0=ALU.mult),
            ["g64q"], ["gq_rep"])
        DVE(lambda e: e.tensor_copy(out=gk_rep[:].rearrange("p (h e) -> p h e", e=64),
                                    in_=g64k[:, None, :].broadcast_to([128, 4, 64])), ["g64k"], ["gk_rep"])
        ACT(lambda e: e.activation(out=esink[:], in_=esink[:], func=AF.Exp), ["esink"], ["esink"])
        POOL(lambda e: e.memset(glra[:], 1.0), (), ["glra"])
        POOL(lambda e: e.memset(idxi[:], 0), (), ["idxi", "idxi_e"])
        POOL(lambda e: e.memset(qi0[:], 0.0), (), ["qi0"])
        POOL(lambda e: e.memset(qi1[:], 0.0), (), ["qi1"])
        POOL(lambda e: e.memset(ones_bf[:], 1.0), (), ["ones_bf"])
        DVE(lambda e: e.tensor_tensor(out=wglr[:].rearrange("p (k n) -> p k n", n=16),
                                      in0=wglr_f[:].rearrange("p (k n) -> p k n", n=16),
                                      in1=gmix[:, :, None].broadcast_to([128, 8, 16]), op=ALU.mult),
            ["wglr_f", "gmix"], ["wglr"])
        for q4 in range(4):
            bk, bres = bank()

            def f(e, q4=q4, bk=bk):
                ins = None
                for a in range(4):
                    hp = q4 * 4 + a
                    ins = e.transpose(out=bk[:, a * 128:(a + 1) * 128], in_=F(0, 2)[:, hp * 128:(hp + 1) * 128], identity=identf[:])
                return ins
            PE(f, ["skl%d" % q4, "F0", "F1", "identf"], [bres])
            ACT(lambda e, q4=q4, bk=bk: e.copy(out=skT[:, q4 * 512:(q4 + 1) * 512], in_=bk[:, :]), [bres], ["skT"])

        def wsrc(c):
            if c < 13:
                return w_in_d, W_IN_COLS[c], gmix
            if c < 15:
                return wa_d, (c - 13) * 512, None
            if c < 17:
                return wb_d, (c - 15) * 512, None
            if c < 19:
                return wo_d, (c - 17) * 512, None
            return wpq_d, (c - 19) * 512, gffn

        hcount = 0
        for c in range(NCH):
            src, col, g = wsrc(c)
            slot = c % NSLOT
            wr = wring[slot]
            for half in range(2):
                st_i = hcount % 2
                hcount += 1
                stg = F(st_i * 2, 2)
                sres = ["F%d" % (st_i * 2), "F%d" % (st_i * 2 + 1)]
                srcap = src.rearrange("(k p) n -> p k n", p=128)[:, half * 4:(half + 1) * 4, col:col + 512]
                DMA(lambda e, stg=stg, srcap=srcap: e.dma_start(out=stg.rearrange("p (k n) -> p k n", n=512), in_=srcap),
                    (), sres, sem="stg%d" % st_i)
                for k4 in range(4):
                    kc = half * 4 + k4
                    o_ap = wr[:, kc * 512:(kc + 1) * 512]
                    i_ap = stg[:, k4 * 512:(k4 + 1) * 512]
                    if g is None:
                        if k4 % 2 == 0:
                            DVE(lambda e, o_ap=o_ap, i_ap=i_ap: e.tensor_copy(out=o_ap, in_=i_ap), sres, ["W%dk%d" % (slot, kc)])
                        else:
                            ACT(lambda e, o_ap=o_ap, i_ap=i_ap: e.copy(out=o_ap, in_=i_ap), sres, ["W%dk%d" % (slot, kc)])
                    else:
                        gs = g[:, kc:kc + 1]
                        if k4 % 2 == 0:
                            DVE(lambda e, o_ap=o_ap, i_ap=i_ap, gs=gs: e.tensor_scalar(out=o_ap, in0=i_ap, scalar1=gs, scalar2=None, op0=ALU.mult),
                                sres + ["gmix", "gffn"], ["W%dk%d" % (slot, kc)])
                        else:
                            ACT(lambda e, o_ap=o_ap, i_ap=i_ap, gs=gs: e.activation(out=o_ap, in_=i_ap, func=AF.Copy, scale=gs),
                                sres + ["gmix", "gffn"], ["W%dk%d" % (slot, kc)])
            DMA(lambda e, c=c, wr=wr: e.dma_start(out=wscr[c], in_=wr[:]), ["W%d" % slot] + ["W%dk%d" % (slot, k) for k in range(8)], ["wscr%d" % c], sem="wst%d" % slot)

        total_uses = NSEQ * NT * NCH
        wstate = {"issued": 0, "use": -1, "lastc": None, "rec": [], "order": None}

        def wget(n):
            c = n % NCH
            if wstate["lastc"] != c:
                wstate["use"] += 1
                wstate["lastc"] = c
                if p.dry:
                    wstate["rec"].append(c)
                else:
                    assert wstate["order"][wstate["use"] % NCH] == c, (wstate["use"], c)
            if p.dry:
                return wring[0], "W0"
            u = wstate["use"]
            while wstate["issued"] < min(total_uses, u + NSLOT):
                m = wstate["issued"]
                cid = wstate["order"][m % NCH]
                slot = m % NSLOT
                DMA(lambda e, cid=cid, slot=slot: e.dma_start(out=wring[slot][:], in_=wscr[cid]), ["wscr%d" % cid], ["W%d" % slot], sem="wld%d" % slot)
                wstate["issued"] += 1
            slot = u % NSLOT
            return wring[slot], "W%d" % slot

        def proj_tok(n, lhs, lres, bk, bres):
            W, wres = wget(n)

            def f(e):
                ins = None
                for kc in range(8):
                    ins = e.matmul(bk[:, :], lhsT=lhs[:, kc * 128:(kc + 1) * 128], rhs=W[:, kc * 512:(kc + 1) * 512], start=(kc == 0), stop=(kc == 7))
                return ins
            PE(f, [lres, wres], [bres])

        def proj_feat(n, rhs, rres, bk, bres):
            W, wres = wget(n)

            def f(e):
                ins = None
                for j in range(4):
                    for kc in range(8):
                        ins = e.matmul(bk[:, j * 128:(j + 1) * 128], lhsT=W[:, kc * 512 + j * 128: kc * 512 + (j + 1) * 128],
                                       rhs=rhs[:, kc * 128:(kc + 1) * 128], start=(kc == 0), stop=(kc == 7))
                return ins
            PE(f, [rres, wres], [bres])

        def tap(name, src, tok0, rres, ncols=1024):
            if DBG:
                DMA(lambda e: e.dma_start(out=dbg_d[name][tok0:tok0 + 128, :], in_=src), [rres], (), sem="dbg_" + name)

        def v3(ap, inner):
            return ap.rearrange("p (a b) -> p a b", b=inner)

        def rope(buf, bres, nh, ti):
            bvw = v3(buf, 64)
            x1 = bvw[:, :, 0:8]
            x2_ = bvw[:, :, 8:16]
            cs = rcos[:, None, ti * 8:(ti + 1) * 8].broadcast_to([128, nh, 8])
            sn = rsin[:, None, ti * 8:(ti + 1) * 8].broadcast_to([128, nh, 8])
            t = [v3(rt[i][:, 0:nh * 8], 8) for i in range(4)]
            DVE(lambda e: e.tensor_tensor(out=t[0], in0=x1, in1=cs, op=ALU.mult), [bres, "rcos"], ["rt0"])
            DVE(lambda e: e.tensor_tensor(out=t[1], in0=x2_, in1=sn, op=ALU.mult), [bres, "rsin"], ["rt1"])
            DVE(lambda e: e.tensor_tensor(out=t[2], in0=x2_, in1=cs, op=ALU.mult), [bres, "rcos"], ["rt2"])
            DVE(lambda e: e.tensor_tensor(out=t[3], in0=x1, in1=sn, op=ALU.mult), [bres, "rsin"], ["rt3"])
            DVE(lambda e: e.tensor_tensor(out=x1, in0=t[0], in1=t[1], op=ALU.subtract), ["rt0", "rt1", bres], [bres])
            DVE(lambda e: e.tensor_tensor(out=x2_, in0=t[2], in1=t[3], op=ALU.add), ["rt2", "rt3", bres], [bres])

        def rstd_op(ssq_ap, out_ap, n, rres, wres):
            ACT(lambda e: e.activation(out=out_ap, in_=ssq_ap, func=AF.Sqrt, bias=EPS, scale=1.0 / n), [rres], [wres])
            DVE(lambda e: e.reciprocal(out=out_ap, in_=out_ap), [wres], [wres])

        SCALE = 128.0 ** -0.5

        def mixer(si, ti):
            gi = si * NT + ti
            tok0 = gi * 128
            n0 = gi * NCH
            par = ti % 2
            xt = xts[gi % 2]; x2 = x2s[gi % 2]
            XT = "xt%d" % (gi % 2); X2 = "x2_%d" % (gi % 2)
            A4, B4, C4, D4, E4 = F(0), F(1), F(2), F(3), F(4)

            if ti == 0:
                POOL(lambda e: e.memset(S[:], 0.0), (), ["S"])
                POOL(lambda e: e.memset(Sb[:], 0.0), (), ["Sb"])
            DMA(lambda e: e.dma_start(out=xt[:], in_=x_d[tok0:tok0 + 128, :]), (), [XT], sem="ldx%d" % (gi % 2))
            ACT(lambda e: e.activation(out=junkA[:], in_=xt[:], func=AF.Square, accum_out=ss[:, 0:1]), [XT], ["ss0", "junkA"])
            rstd_op(ss[:, 0:1], rs[:, 0:1], 1024, "ss0", "rs0")
            DVE(lambda e: e.tensor_scalar(out=tmA[:], in0=xt[:], scalar1=rs[:, 0:1], scalar2=None, op0=ALU.mult), [XT, "rs0"], ["tmA"])
            transposes(tmA, 8, "tmA", trA, "trA")
            hT = trA
            yield

            def gla():
                bk, br = bank(); proj_feat(n0 + 0, hT, "trA", bk, br)
                ACT(lambda e, bk=bk: e.copy(out=qTs[:], in_=bk[:, :]), [br], ["qTs"])
                yield
                bk, br = bank(); proj_feat(n0 + 1, hT, "trA", bk, br)
                ACT(lambda e, bk=bk: e.copy(out=kTs[:], in_=bk[:, :]), [br], ["kTs"])
                bk, br = bank(); proj_tok(n0 + 1, hT, "trA", bk, br)
                ACT(lambda e, bk=bk: e.copy(out=ks[:], in_=bk[:, :]), [br], ["ks"])
                yield
                bk, br = bank()

                def f(e, bk=bk):
                    ins = None
                    for kc in range(8):
                        ins = e.matmul(bk[0:16, 0:128], lhsT=wglr[:, kc * 16:(kc + 1) * 16], rhs=hT[:, kc * 128:(kc + 1) * 128], start=(kc == 0), stop=(kc == 7))
                    return ins
                PE(f, ["trA", "wglr"], [br])
                ACT(lambda e, bk=bk: e.copy(out=glra[0:16, :], in_=bk[0:16, 0:128]), [br], ["glra"])
                bk, br = bank()
                PE(lambda e, bk=bk: e.matmul(bk[:, :], lhsT=glra[0:17, :], rhs=wgk[0:17, :], start=True, stop=True), ["glra", "wgk"], [br])
                ACT(lambda e, bk=bk: e.activation(out=Lb[:], in_=bk[:, :], func=AF.Exp, scale=-1.0), [br], ["Lb"])
                ACT(lambda e: e.activation(out=Lb[:], in_=Lb[:], func=AF.Ln, bias=1.0), ["Lb"], ["Lb"])
                bkA, brA = bank()

                def f(e, bkA=bkA):
                    ins = None
                    for h in range(4):
                        ins = e.matmul(bkA[:, h * 128:(h + 1) * 128], lhsT=Lb[:, h * 128:(h + 1) * 128], rhs=tri[:], start=True, stop=True)
                    return ins
                PE(f, ["Lb", "tri"], [brA])
                bkB, brB = bank()
                PE(lambda e, bkB=bkB: e.matmul(bkB[:, :], lhsT=tri2[:], rhs=Lb[:], start=True, stop=True), ["Lb", "tri2"], [brB])
                ACT(lambda e, bkA=bkA: e.copy(out=bTs[:], in_=bkA[:, :]), [brA], ["bTs"])
                ACT(lambda e, bkB=bkB: e.activation(out=Er[:], in_=bkB[:, :], func=AF.Exp), [brB], ["Er"])
                yield
                bTv = v3(bTs[:], 64)
                DVE(lambda e: e.tensor_scalar(out=nbref[:], in0=bTv[:, :, 32], scalar1=-1.0, scalar2=None, op0=ALU.mult), ["bTs"], ["nbref"])
                DVE(lambda e: e.tensor_copy(out=pbref[:], in_=bTv[:, :, 32]), ["bTs"], ["pbref"])
                ACT(lambda e: e.activation(out=dec[:], in_=bTv[:, :, 63], func=AF.Exp), ["bTs"], ["dec"])
                for g in range(8):
                    ACT(lambda e, g=g: e.activation(out=E1[:, g * 64:(g + 1) * 64], in_=bTs[:, g * 64:(g + 1) * 64], func=AF.Exp,
                                                    bias=nbref[:, g:g + 1], scale=1.0), ["bTs", "nbref"], ["E1"])
                for g in range(8):
                    ACT(lambda e, g=g: e.activation(out=E2[:, g * 64:(g + 1) * 64], in_=bTs[:, g * 64:(g + 1) * 64], func=AF.Exp,
                                                    bias=pbref[:, g:g + 1], scale=-1.0), ["bTs", "pbref"], ["E2"])
                ACT(lambda e: e.activation(out=E3[:], in_=bTs[:], func=AF.Exp), ["bTs"], ["E3"])
                yield
                DVE(lambda e: e.scalar_tensor_tensor(out=qtT[:], in0=qTs[:], scalar=SCALE, op0=ALU.mult, in1=E1[:], op1=ALU.mult), ["qTs", "E1"], ["qtT"])
                DVE(lambda e: e.tensor_tensor(out=ktT[:], in0=kTs[:], in1=E2[:], op=ALU.mult), ["kTs", "E2"], ["ktT"])
                DVE(lambda e: e.scalar_tensor_tensor(out=v3(qi0[:], 128)[:, :, 0:64], in0=v3(qTs[:], 128)[:, :, 0:64], scalar=SCALE, op0=ALU.mult,
                                                     in1=v3(E3[:], 128)[:, :, 0:64], op1=ALU.mult), ["qTs", "E3"], ["qi0"])
                DVE(lambda e: e.scalar_tensor_tensor(out=v3(qi1[:], 128)[:, :, 64:128], in0=v3(qTs[:], 128)[:, :, 64:128], scalar=SCALE, op0=ALU.mult,
                                                     in1=v3(E3[:], 128)[:, :, 64:128], op1=ALU.mult), ["qTs", "E3"], ["qi1"])
                DVE(lambda e: e.scalar_tensor_tensor(out=ke0[:], in0=ks[:], scalar=cm[:, 0:1], op0=ALU.mult, in1=Er[:], op1=ALU.mult), ["ks", "Er", "cm"], ["ke0"])
                DVE(lambda e: e.scalar_tensor_tensor(out=ke1[:], in0=ks[:], scalar=cm[:, 1:2], op0=ALU.mult, in1=Er[:], op1=ALU.mult), ["ks", "Er", "cm"], ["ke1"])
                yield
                for hf in range(2):
                    bk, br = bank(); proj_tok(n0 + 2 + hf, hT, "trA", bk, br)
                    ACT(lambda e, bk=bk, hf=hf: e.copy(out=vb[:, hf * 512:(hf + 1) * 512], in_=bk[:, :]), [br], ["vb"])
                    yield
                for hf in range(2):
                    bk, br = bank(); proj_tok(n0 + 4 + hf, hT, "trA", bk, br)
                    ACT(lambda e, bk=bk, hf=hf: e.activation(out=A4[:, hf * 512:(hf + 1) * 512], in_=bk[:, :], func=AF.Sigmoid), [br], ["F0"])
                    DVE(lambda e, bk=bk, hf=hf: e.tensor_tensor(out=B4[:, hf * 512:(hf + 1) * 512], in0=bk[:, :], in1=A4[:, hf * 512:(hf + 1) * 512], op=ALU.mult),
                        [br, "F0"], ["F1"])
                    yield
                POOL(lambda e: e.tensor_tensor(out=B4, in0=B4, in1=ga_rep[:], op=ALU.mult), ["F1", "ga_rep"], ["F1"])
                bkC, brC = bank()

                def f(e, bkC=bkC):
                    ins = None
                    for h in range(4):
                        ins = e.matmul(bkC[:, h * 128:(h + 1) * 128], lhsT=ktT[:, h * 128:(h + 1) * 128], rhs=qtT[:, h * 128:(h + 1) * 128], start=True, stop=True)
                    return ins
                PE(f, ["ktT", "qtT"], [brC])
                DVE(lambda e, bkC=bkC: e.tensor_tensor(out=v3(attm[:], 128), in0=v3(bkC[:, :], 128), in1=mgla[:, None, :].broadcast_to([128, 4, 128]), op=ALU.mult),
                    [brC, "mgla"], ["attm"])

                def dstate(ke, keres):
                    bks = [bank(), bank()]

                    def f(e):
                        ins = None
                        for h in range(4):
                            ins = e.matmul(bks[h // 2][0][:, (h % 2) * 256:(h % 2) * 256 + 256], lhsT=ke[:, h * 128:(h + 1) * 128],
                                           rhs=vb[:, h * 256:(h + 1) * 256], start=True, stop=True)
                        return ins
                    PE(f, [keres, "vb"], [bks[0][1], bks[1][1]])
                    return bks
                d0 = dstate(ke0, "ke0")
                for h in range(4):
                    DVE(lambda e, h=h: e.scalar_tensor_tensor(out=S1[:, h * 256:(h + 1) * 256], in0=S[:, h * 256:(h + 1) * 256], scalar=dec[:, 2 * h:2 * h + 1],
                                                              op0=ALU.mult, in1=d0[h // 2][0][:, (h % 2) * 256:(h % 2) * 256 + 256], op1=ALU.add),
                        ["S", "dec", d0[h // 2][1]], ["S1"])
                ACT(lambda e: e.copy(out=S1b[:], in_=S1[:]), ["S1"], ["S1b"])
                yield
                ob = [bank(), bank()]

                def f(e):
                    ins = None
                    for h in range(4):
                        o_ap = ob[h // 2][0][:, (h % 2) * 256:(h % 2) * 256 + 256]
                        e.matmul(o_ap, lhsT=attm[:, h * 128:(h + 1) * 128], rhs=vb[:, h * 256:(h + 1) * 256], start=True, stop=False)
                        e.matmul(o_ap, lhsT=qi0[:, h * 128:(h + 1) * 128], rhs=Sb[:, h * 256:(h + 1) * 256], start=False, stop=False)
                        ins = e.matmul(o_ap, lhsT=qi1[:, h * 128:(h + 1) * 128], rhs=S1b[:, h * 256:(h + 1) * 256], start=False, stop=True)
                    return ins
                PE(f, ["attm", "vb", "qi0", "qi1", "Sb", "S1b"], [ob[0][1], ob[1][1]])
                for h in range(4):
                    ACT(lambda e, h=h: e.activation(out=junkA[:, 0:256], in_=ob[h // 2][0][:, (h % 2) * 256:(h % 2) * 256 + 256], func=AF.Square,
                                                    accum_out=ssqa[:, h:h + 1]), [ob[h // 2][1]], ["ssqa", "junkA"])
                rstd_op(ssqa[:], rsa[:], 256, "ssqa", "rsa")
                o_a = tmB
                for h in range(4):
                    DVE(lambda e, h=h: e.scalar_tensor_tensor(out=o_a[:, h * 256:(h + 1) * 256], in0=ob[h // 2][0][:, (h % 2) * 256:(h % 2) * 256 + 256],
                                                              scalar=rsa[:, h:h + 1], op0=ALU.mult, in1=B4[:, h * 256:(h + 1) * 256], op1=ALU.mult),
                        [ob[h // 2][1], "rsa", "F1"], ["tmB"])
                if DBG:
                    DVE(lambda e: e.tensor_copy(out=A4, in_=o_a[:]), ["tmB"], ["F0"])
                    tap("oa", A4, tok0, "F0")
                transposes(o_a, 8, "tmB", trB, "trB")
                yield
                d1 = dstate(ke1, "ke1")
                for h in range(4):
                    DVE(lambda e, h=h: e.scalar_tensor_tensor(out=S[:, h * 256:(h + 1) * 256], in0=S1[:, h * 256:(h + 1) * 256], scalar=dec[:, 2 * h + 1:2 * h + 2],
                                                              op0=ALU.mult, in1=d1[h // 2][0][:, (h % 2) * 256:(h % 2) * 256 + 256], op1=ALU.add),
                        ["S1", "dec", d1[h // 2][1]], ["S"])
                ACT(lambda e: e.copy(out=Sb[:], in_=S[:]), ["S"], ["Sb"])
                yield


                for hf in range(2):
                    bk, br = bank(); proj_tok(n0 + 13 + hf, trB, "trB", bk, br)
                    DVE(lambda e, bk=bk, hf=hf: e.tensor_tensor(out=B4[:, hf * 512:(hf + 1) * 512], in0=bk[:, :], in1=D4[:, hf * 512:(hf + 1) * 512], op=ALU.mult),
                        [br, "F3"], ["F1"])
                    yield

            def swa():
                qn = C4
                for hf in range(2):
                    bk, br = bank(); proj_tok(n0 + 6 + hf, hT, "trA", bk, br)
                    ACT(lambda e, bk=bk, hf=hf: e.activation(out=A4[:, hf * 512:(hf + 1) * 512], in_=bk[:, :], func=AF.Square), [br], ["F0"])
                    DVE(lambda e, hf=hf: e.tensor_reduce(out=ssqq[:, hf * 8:(hf + 1) * 8], in_=v3(A4[:, hf * 512:(hf + 1) * 512], 64), axis=AX.X, op=ALU.add),
                        ["F0"], ["ssqq%d" % hf])
                    rstd_op(ssqq[:, hf * 8:(hf + 1) * 8], rsq[:, hf * 8:(hf + 1) * 8], 64, "ssqq%d" % hf, "rsq%d" % hf)
                    DVE(lambda e, bk=bk, hf=hf: e.tensor_tensor(out=v3(qn[:, hf * 512:(hf + 1) * 512], 64), in0=v3(bk[:, :], 64),
                                                                in1=rsq[:, hf * 8:(hf + 1) * 8, None].broadcast_to([128, 8, 64]), op=ALU.mult),
                        [br, "rsq%d" % hf], ["F2"])
                    yield
                POOL(lambda e: e.tensor_tensor(out=qn, in0=qn, in1=gq_rep[:], op=ALU.mult), ["F2", "gq_rep"], ["F2"])
                rope(qn, "F2", 16, ti)
                ACT(lambda e: e.copy(out=tmA[:], in_=qn), ["F2"], ["tmA"])
                transposes(tmA, 8, "tmA", trC, "trC")
                yield
                bk, br = bank(); proj_tok(n0 + 8, hT, "trA", bk, br)
                ACT(lambda e, bk=bk: e.activation(out=kq[:], in_=bk[:, 0:256], func=AF.Square), [br], ["kq"])
                DVE(lambda e: e.tensor_reduce(out=ssqk[:], in_=v3(kq[:], 64), axis=AX.X, op=ALU.add), ["kq"], ["ssqk"])
                rstd_op(ssqk[:], rsk[:], 64, "ssqk", "rsk")
                DVE(lambda e, bk=bk: e.tensor_tensor(out=v3(kn[:], 64), in0=v3(bk[:, 0:256], 64), in1=rsk[:, :, None].broadcast_to([128, 4, 64]), op=ALU.mult),
                    [br, "rsk"], ["kn"])
                ACT(lambda e, bk=bk: e.copy(out=vsw[par][:], in_=bk[:, 256:512]), [br], ["vsw%d" % par])
                POOL(lambda e: e.tensor_tensor(out=kn[:], in0=kn[:], in1=gk_rep[:], op=ALU.mult), ["kn", "gk_rep"], ["kn"])
                rope(kn[:], "kn", 4, ti)
                k2v = k2[:].rearrange("p (g c d) -> p g c d", c=2, d=64)
                ACT(lambda e: e.copy(out=k2v[:, :, 0, :], in_=v3(kn[:], 64)), ["kn"], ["k2"])
                ACT(lambda e: e.copy(out=k2v[:, :, 1, :], in_=v3(kn[:], 64)), ["kn"], ["k2"])
                transposes(k2, 4, "k2", kT2[par], "kT2_%d" % par)
                yield
                has_prev = ti > 0
                ppar = 1 - par
                for g in range(4):
                    for odd in range(2):
                        bk, br = bank()
                        lo = 64 * odd

                        def f(e, bk=bk, lo=lo, g=g):
                            ins = None
                            if has_prev:
                                ins = e.matmul(bk[:, 0:256], lhsT=kT2[ppar][lo:lo + 64, g * 128:(g + 1) * 128], rhs=trC[lo:lo + 64, 2 * g * 128:(2 * g + 2) * 128],
                                               start=True, stop=True)
                            ins = e.matmul(bk[:, 256:512], lhsT=kT2[par][lo:lo + 64, g * 128:(g + 1) * 128], rhs=trC[lo:lo + 64, 2 * g * 128:(2 * g + 2) * 128],
                                           start=True, stop=True)
                            return ins
                        PE(f, ["trC", "kT2_0", "kT2_1"], [br])
                        c0 = 0 if has_prev else 256
                        pr = praw[odd]
                        dst = (pTo if odd else pTe)[g]
                        dres = ("pTo%d" if odd else "pTe%d") % g
                        ACT(lambda e, bk=bk, pr=pr, c0=c0: e.activation(out=pr[:, c0:512], in_=bk[:, c0:512], func=AF.Exp), [br], ["praw%d" % odd])
                        POOL(lambda e, pr=pr, dst=dst, c0=c0: e.tensor_tensor(out=dst[:, c0:512], in0=pr[:, c0:512], in1=mk2[:, c0:512], op=ALU.mult),
                             ["praw%d" % odd, "mk2"], [dres])
                    yield
                OB = [bank(), bank()]
                SBk, SBr = bank()

                def f(e):
                    ins = None
                    for h in range(16):
                        g, j = h // 4, h % 4
                        src = (pTo if j % 2 else pTe)[g]
                        slot = j // 2
                        o_ap = OB[h // 8][0][:, (h % 8) * 64:(h % 8) * 64 + 64]
                        if has_prev:
                            e.matmul(o_ap, lhsT=src[:, slot * 128:(slot + 1) * 128], rhs=vsw[ppar][:, g * 64:(g + 1) * 64], start=True, stop=False)
                        ins = e.matmul(o_ap, lhsT=src[:, 256 + slot * 128:256 + (slot + 1) * 128], rhs=vsw[par][:, g * 64:(g + 1) * 64], start=(not has_prev), stop=True)
                    for h in range(16):
                        g, j = h // 4, h % 4
                        src = (pTo if j % 2 else pTe)[g]
                        slot = j // 2
                        s_ap = SBk[:, h:h + 1]
                        if has_prev:
                            e.matmul(s_ap, lhsT=src[:, slot * 128:(slot + 1) * 128], rhs=ones_bf[:, 0:1], start=True, stop=False)
                        ins = e.matmul(s_ap, lhsT=src[:, 256 + slot * 128:256 + (slot + 1) * 128], rhs=ones_bf[:, 0:1], start=(not has_prev), stop=True)
                    return ins
                PE(f, ["pTe0", "pTe1", "pTe2", "pTe3", "pTo0", "pTo1", "pTo2", "pTo3", "vsw0", "vsw1", "ones_bf"], [OB[0][1], OB[1][1], SBr])
                DVE(lambda e: e.tensor_tensor(out=den[:], in0=SBk[:, 0:16], in1=esink[:], op=ALU.add), [SBr, "esink"], ["den"])
                DVE(lambda e: e.reciprocal(out=rden[:], in_=den[:]), ["den"], ["rden"])
                o_b = tmB
                for hf in range(2):
                    DVE(lambda e, hf=hf: e.tensor_tensor(out=v3(o_b[:, hf * 512:(hf + 1) * 512], 64), in0=v3(OB[hf][0][:, :], 64),
                                                         in1=rden[:, hf * 8:(hf + 1) * 8, None].broadcast_to([128, 8, 64]), op=ALU.mult),
                        [OB[hf][1], "rden"], ["tmB"])
                if DBG:
                    DVE(lambda e: e.tensor_copy(out=A4, in_=o_b[:]), ["tmB"], ["F0"])
                    tap("ob", A4, tok0, "F0")
                transposes(o_b, 8, "tmB", trD, "trD")
                yield

                for hf in range(2):
                    bk, br = bank(); proj_tok(n0 + 15 + hf, trD, "trD", bk, br)
                    DVE(lambda e, bk=bk, hf=hf: e.tensor_tensor(out=C4[:, hf * 512:(hf + 1) * 512], in0=bk[:, :], in1=E4[:, hf * 512:(hf + 1) * 512], op=ALU.mult),
                        [br, "F4"], ["F2"])
                    yield

            def gates():
                for hf in range(2):
                    bk, br = bank(); proj_tok(n0 + 9 + hf, hT, "trA", bk, br)
                    ACT(lambda e, bk=bk, hf=hf: e.activation(out=D4[:, hf * 512:(hf + 1) * 512], in_=bk[:, :], func=AF.Sigmoid), [br], ["F3"])
                    yield
                for hf in range(2):
                    bk, br = bank(); proj_tok(n0 + 11 + hf, hT, "trA", bk, br)
                    ACT(lambda e, bk=bk, hf=hf: e.activation(out=E4[:, hf * 512:(hf + 1) * 512], in_=bk[:, :], func=AF.Sigmoid), [br], ["F4"])
                    yield

            subs = [gla(), swa(), gates()]
            while subs:
                for g_ in list(subs):
                    try:
                        next(g_)
                    except StopIteration:
                        subs.remove(g_)
                yield
            POOL(lambda e: e.tensor_tensor(out=tmA[:], in0=B4, in1=C4, op=ALU.add), ["F1", "F2"], ["tmA"])
            transposes(tmA, 8, "tmA", trB, "trB")
            yield
            if STAGE <= 2.8:
                return
            for hf in range(2):
                bk, br = bank(); proj_tok(n0 + 17 + hf, trB, "trB", bk, br)
                DVE(lambda e, bk=bk, hf=hf: e.tensor_tensor(out=x2[:, hf * 512:(hf + 1) * 512], in0=bk[:, :], in1=xt[:, hf * 512:(hf + 1) * 512], op=ALU.add),
                    [br, XT], [X2])
                yield
            tap("x2", x2[:], tok0, X2)

        def peer(si, ti, nxt):
            gi = si * NT + ti
            tok0 = gi * 128
            n0 = gi * NCH
            xt = xts[gi % 2]; x2 = x2s[gi % 2]
            XT = "xt%d" % (gi % 2); X2 = "x2_%d" % (gi % 2)
            ACT(lambda e: e.activation(out=junkA[:], in_=x2[:], func=AF.Square, accum_out=ss[:, 1:2]), [X2], ["ss1", "junkA"])
            rstd_op(ss[:, 1:2], rs[:, 1:2], 1024, "ss1", "rs1")
            DVE(lambda e: e.tensor_scalar(out=tmA[:], in0=x2[:], scalar1=rs[:, 1:2], scalar2=None, op0=ALU.mult), [X2, "rs1"], ["tmA"])
            DVE(lambda e: e.scalar_tensor_tensor(out=XN[:, :], in0=x2[:], scalar=rs[:, 1:2], op0=ALU.mult, in1=gffn_rep[:], op1=ALU.mult),
                [X2, "rs1", "gffn_rep"], ["XN"])
            transposes(tmA, 8, "tmA", trA, "trA")
            for cc in range(4):
                bk, br = bank(); proj_feat(n0 + 19 + cc, trA, "trA", bk, br)
                ACT(lambda e, bk=bk, cc=cc: e.copy(out=qpT[:, cc * 512:(cc + 1) * 512], in_=bk[:, :]), [br], ["qpT"])
            sc = RG[:, 0:4096].bitcast(F32)
            SCR = ["rg0", "rg1", "rgu0", "rgu1"]
            for half in range(2):
                scb = [bank(), bank()]

                def f(e, scb=scb, half=half):
                    ins = None
                    for q in range(8):
                        hp = half * 8 + q
                        ins = e.matmul(scb[q // 4][0][:, (q % 4) * 128:(q % 4 + 1) * 128], lhsT=qpT[:, hp * 128:(hp + 1) * 128], rhs=skT[:, hp * 128:(hp + 1) * 128],
                                       start=True, stop=True)
                    return ins
                PE(f, ["qpT", "skT"], [b_[1] for b_ in scb])
                for i in range(2):
                    ACT(lambda e, i=i, scb=scb, half=half: e.copy(out=sc[:, (half * 2 + i) * 512:(half * 2 + i + 1) * 512], in_=scb[i][0][:, :]),
                        [scb[i][1]], SCR)

            def selback():
                for hg in range(4):
                    hps = [hg * 4 + q for q in range(4)]
                    sls = [sc[:, hp * 128:(hp + 1) * 128] for hp in hps]
                    for q, hp in enumerate(hps):
                        DVE(lambda e, sl=sls[q], hp=hp: e.max(out=tv[:, hp * 16:hp * 16 + 8], in_=sl), SCR, ["tv%d" % hp])
                    for q, hp in enumerate(hps):
                        DVE(lambda e, sl=sls[q], hp=hp: e.max_index(out=tiu[:, hp * 16:hp * 16 + 8], in_max=tv[:, hp * 16:hp * 16 + 8], in_values=sl), SCR + ["tv%d" % hp], ["tiu%d" % hp])
                    for q, hp in enumerate(hps):
                        DVE(lambda e, sl=sls[q], hp=hp, q=q: e.match_replace(out=sc2L[q][:], in_to_replace=tv[:, hp * 16:hp * 16 + 8], in_values=sl, imm_value=-1e30),
                            SCR + ["tv%d" % hp], ["sc2_%d" % q])
                    for q, hp in enumerate(hps):
                        DVE(lambda e, hp=hp, q=q: e.max(out=tv[:, hp * 16 + 8:hp * 16 + 16], in_=sc2L[q][:]), ["sc2_%d" % q], ["tvb%d" % hp])
                    for q, hp in enumerate(hps):
                        DVE(lambda e, hp=hp, q=q: e.max_index(out=tiu[:, hp * 16 + 8:hp * 16 + 16], in_max=tv[:, hp * 16 + 8:hp * 16 + 16], in_values=sc2L[q][:]),
                            ["sc2_%d" % q, "tvb%d" % hp], ["tiub%d" % hp])
                    yield
                TVALL = ["tv%d" % hp for hp in range(16)] + ["tvb%d" % hp for hp in range(16)]
                TIALL = ["tiu%d" % hp for hp in range(16)] + ["tiub%d" % hp for hp in range(16)]
                DVE(lambda e: e.tensor_copy(out=tif[:], in_=tiu[:]), TIALL, ["tif"])
                tfv = tif[:].rearrange("p (h c k) -> p h c k", c=2, k=16)
                tvv = tv[:].rearrange("p (h c k) -> p h c k", c=2, k=16)
                DVE(lambda e: e.tensor_scalar(out=tfv[:, :, 0, :], in0=tfv[:, :, 0, :], scalar1=128.0, scalar2=None, op0=ALU.mult), ["tif"], ["tif"])
                for hg in range(4):
                    hs_ = [hg * 2, hg * 2 + 1]
                    for q, h in enumerate(hs_):
                        for (dstL, srcv, rres, wn) in ((candL, tvv, TVALL, "cand_%d"), (ciL, tfv, ["tif"], "ci_%d")):
                            dst = dstL[q]
                            DVE(lambda e, h=h, dst=dst, srcv=srcv: e.tensor_tensor(out=v3(dst[:, 0:64], 16), in0=srcv[:, h, 0, 0:4, None].broadcast_to([128, 4, 16]),
                                                                                   in1=srcv[:, h, 1, None, :].broadcast_to([128, 4, 16]), op=ALU.add), rres, [wn % q])
                            DVE(lambda e, h=h, dst=dst, srcv=srcv: e.tensor_tensor(out=v3(dst[:, 64:112], 4), in0=srcv[:, h, 0, 4:16, None].broadcast_to([128, 12, 4]),
                                                                                   in1=srcv[:, h, 1, None, 0:4].broadcast_to([128, 12, 4]), op=ALU.add), rres, [wn % q])
                    for q, h in enumerate(hs_):
                        DVE(lambda e, h=h, q=q: e.max(out=bv[:, h * 16:h * 16 + 8], in_=candL[q][:]), ["cand_%d" % q], ["bv%d" % h])
                    for q, h in enumerate(hs_):
                        DVE(lambda e, h=h, q=q: e.match_replace(out=cand2L[q][:], in_to_replace=bv[:, h * 16:h * 16 + 8], in_values=candL[q][:], imm_value=-1e30),
                            ["cand_%d" % q, "bv%d" % h], ["cand2_%d" % q])
                    for q, h in enumerate(hs_):
                        DVE(lambda e, h=h, q=q: e.max(out=bv[:, h * 16 + 8:h * 16 + 16], in_=cand2L[q][:]), ["cand2_%d" % q], ["bvb%d" % h])
                    yield
                    for j in range(16):
                        for q, h in enumerate(hs_):
                            r = h * 16 + j
                            DVE(lambda e, r=r, q=q: e.scalar_tensor_tensor(out=junkD[:, (r % 8) * 128:(r % 8) * 128 + 112], in0=candL[q][:], scalar=bv[:, r:r + 1], op0=ALU.is_equal,
                                                                           in1=ciL[q][:], op1=ALU.mult, accum_out=idxf[:, r:r + 1]),
                                ["cand_%d" % q, "ci_%d" % q, "bv%d" % h, "bvb%d" % h], ["idxf%d" % r, "jd%d" % (r % 8)])
                        if j % 4 == 3:
                            yield
                BVALL = ["bv%d" % h for h in range(8)] + ["bvb%d" % h for h in range(8)]
                DVE(lambda e: e.tensor_scalar(out=idxf[:], in0=idxf[:], scalar1=16383.0, scalar2=0.0, op0=ALU.min, op1=ALU.max), ["idxf%d" % r for r in range(128)], ["idxf"])
                DVE(lambda e: e.tensor_copy(out=idxi[:], in_=idxf[:]), ["idxf"], ["idxi"])
                DVE(lambda e: e.tensor_scalar(out=negm[:], in0=v3(bv[:], 16)[:, :, 0], scalar1=-1.0, scalar2=None, op0=ALU.mult), BVALL, ["negm"])
                for h in range(8):
                    ACT(lambda e, h=h: e.activation(out=eg[:, h * 16:(h + 1) * 16], in_=bv[:, h * 16:(h + 1) * 16], func=AF.Exp, bias=negm[:, h:h + 1], scale=1.0,
                                                    accum_out=Z[:, h:h + 1]), BVALL + ["negm"], ["eg", "Z"])
                DVE(lambda e: e.reciprocal(out=rZ[:], in_=Z[:]), ["Z"], ["rZ"])
                DVE(lambda e: e.tensor_tensor(out=v3(gate[:], 16), in0=v3(eg[:], 16), in1=rZ[:, :, None].broadcast_to([128, 8, 16]), op=ALU.mult), ["eg", "rZ"], ["gate"])
                if DBG:
                    tap("idx", idxf[:], tok0, "idxf")

            sb_gen = selback()
            gens = [(sb_gen, 3)] + ([(nxt, 2)] if nxt is not None else [])
            while gens:
                for (g_, k_) in list(gens):
                    for _ in range(k_):
                        try:
                            next(g_)
                        except StopIteration:
                            gens.remove((g_, k_))
                            break

            PA = bank(); PB = bank()

            def gather(r):
                slot = r % NG
                p.op("pool", lambda e: e.indirect_dma_start(out=rg[slot], out_offset=None, in_=uvs,
                                                            in_offset=bass.IndirectOffsetOnAxis(ap=idxi[:, r:r + 1], axis=0)),
                     ["idxi"] + UVRES, ["rg%d" % slot, "rgu%d" % slot], dma="gr%d" % slot)

            def dot(r):
                slot = r % NG
                DVE(lambda e: e.scalar_tensor_tensor(out=rg[slot][:, 0:1024], in0=rg[slot][:, 0:1024], scalar=1.0, op0=ALU.mult, in1=XN[:, :], op1=ALU.mult,
                                                     accum_out=hd[:, r:r + 1]), ["rg%d" % slot, "XN"], ["hd%d" % r, "rgu%d" % slot])

            def combine(r):
                slot = r % NG
                ds = r % ND
                ACT(lambda e: e.activation(out=gl[:, r:r + 1], in_=hd[:, r:r + 1], func=AF.Gelu), ["hd%d" % r], ["gl%d" % r])
                DVE(lambda e: e.tensor_scalar(out=Dr[ds][:], in0=identf[:], scalar1=gl[:, r:r + 1], scalar2=gate[:, r:r + 1], op0=ALU.mult, op1=ALU.mult),
                    ["identf", "gl%d" % r, "gate"], ["Dr%d" % ds])

                def f(e):
                    e.matmul(PA[0][:, :], lhsT=Dr[ds][:], rhs=rg[slot][:, 1024:1536], start=(r == 0), stop=(r == NR - 1))
                    return e.matmul(PB[0][:, :], lhsT=Dr[ds][:], rhs=rg[slot][:, 1536:2048], start=(r == 0), stop=(r == NR - 1))
                PE(f, ["Dr%d" % ds, "rg%d" % slot], [PA[1], PB[1]])

            for r in range(min(NG, NR)):
                gather(r)
            dot(0)
            for r in range(NR):
                if r + 1 < NR:
                    dot(r + 1)
                combine(r)
                if r + NG < NR:
                    gather(r + NG)
            DVE(lambda e: e.tensor_tensor(out=xt[:, 0:512], in0=PA[0][:, :], in1=x2[:, 0:512], op=ALU.add), [PA[1], X2], [XT])
            DVE(lambda e: e.tensor_tensor(out=xt[:, 512:1024], in0=PB[0][:, :], in1=x2[:, 512:1024], op=ALU.add), [PB[1], X2], [XT])
            DMA(lambda e: e.dma_start(out=out_d[tok0:tok0 + 128, :], in_=xt[:]), [XT], (), sem="sto%d" % (gi % 2))

        tiles = [(si, ti) for si in range(NSEQ) for ti in range(NT)]
        p.dry = True
        sv = bstate["i"]
        for _ in mixer(0, 1 if NT > 1 else 0):
            pass
        bstate["i"] = sv
        wstate["order"] = list(wstate["rec"]) + [19, 20, 21, 22]
        assert sorted(wstate["order"]) == list(range(NCH)), wstate["order"]
        wstate["use"] = -1
        wstate["lastc"] = None
        p.dry = False
        if STAGE >= 1 and STAGE <= 3:
            for (si, ti) in tiles:
                for _ in mixer(si, ti):
                    pass
        elif STAGE > 3:
            for _ in mixer(*tiles[0]):
                pass
            for k, (si, ti) in enumerate(tiles):
                nxt = mixer(*tiles[k + 1]) if k + 1 < len(tiles) else None
                peer(si, ti, nxt)
        if STAGE == 0:
            DMA(lambda e: e.dma_start(out=xts[0][:, 0:128], in_=wscr[22][:, 0:256].bitcast(F32)), ["wscr22"], ["xt0"], sem="ldx0")
            DMA(lambda e: e.dma_start(out=out_d[0:128, 0:128], in_=xts[0][:, 0:128]), ["xt0"], (), sem="sto")
        evs = [(p.dsem[k], p.dcnt[k]) for k in p.dsem if k.startswith("sto") or k.startswith("dbg_")]
        p.final_wait("sp", evs)
        p.emit()
    return nc


def host_consts():
    ident = np.eye(128, dtype=np.float32)
    s = np.arange(128)[:, None]
    t = np.arange(128)[None, :]
    same = (s // 64) == (t // 64)
    tri = np.where(same & (s <= t), -1.0 / 16.0, 0.0).astype(np.float32)
    tri2 = np.where(same & (s > t), -1.0 / 16.0, 0.0).astype(np.float32)
    mgla = np.where(same & (s <= t), 1.0, 0.0).astype(np.float32)
    mown = np.where(s <= t, 1.0, 0.0).astype(np.float32)
    mprev = np.where(s > t, 1.0, 0.0).astype(np.float32)
    cmat = np.stack([ident, tri, tri2, mgla, mown]).astype(np.float32)
    cm = np.stack([(np.arange(128) < 64), (np.arange(128) >= 64)], axis=1).astype(np.float32)
    pos = np.arange(2048, dtype=np.float32)
    inv_freq = (np.float32(500000.0) ** (-np.arange(0, 16, 2, dtype=np.float32) / np.float32(16))).astype(np.float32)
    ang = (pos[:, None] * inv_freq[None, :]).astype(np.float32)
    cos = np.cos(ang).astype(np.float32).reshape(16, 128, 8).transpose(1, 0, 2).reshape(128, 128)
    sin = np.sin(ang).astype(np.float32).reshape(16, 128, 8).transpose(1, 0, 2).reshape(128, 128)
    return dict(cmat=cmat, mprev=mprev, cm=cm, rcos=np.ascontiguousarray(cos), rsin=np.ascontiguousarray(sin))


def make_in_maps(inputs, n_cores, NSEQ, NT):
    f = lambda a: np.ascontiguousarray(np.asarray(a, dtype=np.float32))
    x = f(inputs["x"])
    seq = NT * 128
    xs = x.reshape(-1, seq, 1024)
    assert xs.shape[0] == n_cores * NSEQ
    shared = dict(
        w_in=f(inputs["w_in"][0]), w_branch_a=f(inputs["w_branch_a"][0]), w_branch_b=f(inputs["w_branch_b"][0]),
        w_out=f(inputs["w_out"][0]), w_peer_q=f(inputs["w_peer_q"][0]),
        sub_keys=f(inputs["peer_sub_keys"][0]).reshape(16, 128, 128),
        peer_u=f(inputs["peer_u"][0]), peer_v=f(inputs["peer_v"][0]),
        gmix_pk=np.ascontiguousarray(f(inputs["norm_mix_g"][0]).reshape(8, 128).T),
        gffn_pk=np.ascontiguousarray(f(inputs["norm_ffn_g"][0]).reshape(8, 128).T),
        gffn_row=f(inputs["norm_ffn_g"][0]).reshape(1, 1024),
        ga_row=f(inputs["gla_norm_g"][0]).reshape(1, 256),
        gq_row=f(inputs["q_norm_g"][0]).reshape(1, 64),
        gk_row=f(inputs["k_norm_g"][0]).reshape(1, 64),
        sinks_row=f(inputs["attn_sinks"][0]).reshape(1, 16),
        w_gk2=f(inputs["w_gk2"][0]), b_gk=f(inputs["b_gk"][0]).reshape(1, 512),
    )
    shared.update(host_consts())
    maps = []
    for c in range(n_cores):
        m = dict(shared)
        m["x"] = np.ascontiguousarray(xs[c * NSEQ:(c + 1) * NSEQ].reshape(NSEQ * seq, 1024))
        maps.append(m)
    return maps


def kernel(**inputs):
    n = 8
    NSEQ, NT = 2, 16
    nc = build_nc(NSEQ, NT)
    in_maps = make_in_maps(inputs, n, NSEQ, NT)
    res = run_bass_kernel_spmd(nc, in_maps, core_ids=list(range(n)))
    out = np.concatenate([np.asarray(r["out"]).reshape(NSEQ, NT * 128, 1024) for r in res.results], axis=0)
    return out.astype(np.float32)
```

```python
import numpy as np
import concourse.bass as bass
import concourse.mybir as mybir
from concourse.bass_utils import run_bass_kernel_spmd
from contextlib import ExitStack

F32 = mybir.dt.float32
BF16 = mybir.dt.bfloat16
U32 = mybir.dt.uint32
I32 = mybir.dt.int32
AF = mybir.ActivationFunctionType
ALU = mybir.AluOpType
AX = mybir.AxisListType

EPS = 1e-6
W_IN_COLS = [0, 512, 1024, 1536, 2048, 2560, 3088, 3600, 4112, 4624, 5136, 5648, 6160]
NCH = 23
NSLOT = 3
NG = 7
ND = 4


class Prog:
    ENGS = ("pe", "dve", "act", "pool", "sp")

    def __init__(self, nc, es):
        self.nc = nc
        self.es = es
        self.q = {e: [] for e in self.ENGS}
        self.sem = {e: es.enter_context(nc.semaphore("sem_" + e)) for e in self.ENGS}
        self.semeng = {id(self.sem[e]): e for e in self.ENGS}
        self.cnt = {e: 0 for e in self.ENGS}
        self.waited = {e: {} for e in self.ENGS}
        self.lastw = {}
        self.readers = {}
        self.dsem = {}
        self.dcnt = {}

    def op(self, eng, fn, reads=(), writes=(), dma=None):
        if getattr(self, "dry", False):
            return None
        deps = []
        for r in reads:
            if r in self.lastw:
                deps.append(self.lastw[r])
        for w in writes:
            if w in self.lastw:
                deps.append(self.lastw[w])
            deps.extend(self.readers.get(w, []))
        wd = self.waited[eng]
        best = {}
        for (s, v) in deps:
            key = id(s)
            if eng == "pe" and self.semeng.get(key) == "pe":
                continue
            if wd.get(key, 0) >= v:
                continue
            if key not in best or best[key][1] < v:
                best[key] = (s, v)
        waits = []
        for key, (s, v) in best.items():
            wd[key] = v
            waits.append((s, v))
        if dma is None:
            self.cnt[eng] += 1
            ev = (self.sem[eng], self.cnt[eng])
            inc = 1
        else:
            if dma not in self.dsem:
                self.dsem[dma] = self.es.enter_context(self.nc.semaphore("d_" + dma))
                self.dcnt[dma] = 0
            self.dcnt[dma] += 16
            ev = (self.dsem[dma], self.dcnt[dma])
            inc = 16
        self.q[eng].append((waits, fn, ev[0], inc))
        for w in writes:
            self.lastw[w] = ev
            self.readers[w] = []
        for r in reads:
            if r not in writes:
                self.readers.setdefault(r, []).append(ev)
        return ev

    def final_wait(self, eng, evs):
        self.q[eng].append((list(evs), None, None, 0))

    def emit(self):
        nc = self.nc
        with nc.Block() as block:
            def mk(ename):
                def body(e):
                    for (waits, fn, s, inc) in self.q[ename]:
                        for (ws, wv) in waits:
                            e.wait_ge(ws, wv)
                        if fn is not None:
                            ins = fn(e)
                            ins.then_inc(s, inc)
                return body
            block.tensor(mk("pe"))
            block.vector(mk("dve"))
            block.scalar(mk("act"))
            block.gpsimd(mk("pool"))
            block.sync(mk("sp"))


def build_nc(NSEQ=2, NT=16, DBG=False, NR=128, STAGE=99):
    NTOK = NSEQ * NT * 128
    nc = bass.Bass("TRN2", target_bir_lowering=False)
    es = ExitStack()

    def din(name, shape, dt=F32):
        return nc.dram_tensor(name, list(shape), dt, kind="ExternalInput").ap()

    x_d = din("x", [NTOK, 1024])
    w_in_d = din("w_in", [1024, 6672])
    wa_d = din("w_branch_a", [1024, 1024])
    wb_d = din("w_branch_b", [1024, 1024])
    wo_d = din("w_out", [1024, 1024])
    wpq_d = din("w_peer_q", [1024, 2048])
    sk_d = din("sub_keys", [16, 128, 128])
    pu_d = din("peer_u", [16384, 1024])
    pv_d = din("peer_v", [16384, 1024])
    gmix_d = din("gmix_pk", [128, 8])
    gffn_d = din("gffn_pk", [128, 8])
    gffn_row_d = din("gffn_row", [1, 1024])
    ga_d = din("ga_row", [1, 256])
    gq_d = din("gq_row", [1, 64])
    gk_d = din("gk_row", [1, 64])
    sinks_d = din("sinks_row", [1, 16])
    wgk2_d = din("w_gk2", [16, 512])
    bgk_d = din("b_gk", [1, 512])
    cmat_d = din("cmat", [5, 128, 128])
    mprev_d = din("mprev", [128, 128])
    cm_d = din("cm", [128, 2])
    cos_d = din("rcos", [128, 16 * 8])
    sin_d = din("rsin", [128, 16 * 8])
    out_d = nc.dram_tensor("out", [NTOK, 1024], F32, kind="ExternalOutput").ap()
    wscr = nc.dram_tensor("wscr", [NCH, 128, 4096], BF16, kind="Internal").ap()
    uvs = nc.dram_tensor("uvscr", [16384, 2048], BF16, kind="Internal").ap()
    dbg_d = {}
    if DBG:
        for nm in ("x2", "ya", "yb", "oa", "ob"):
            dbg_d[nm] = nc.dram_tensor("dbg_" + nm, [NTOK, 1024], F32, kind="ExternalOutput").ap()
        dbg_d["idx"] = nc.dram_tensor("dbg_idx", [NTOK, 128], F32, kind="ExternalOutput").ap()
        dbg_d["wgt"] = nc.dram_tensor("dbg_wgt", [NTOK, 128], F32, kind="ExternalOutput").ap()

    with es:
        p = Prog(nc, es)
        p.dry = False

        def sb(name, shape, dt=F32):
            return es.enter_context(nc.sbuf_tensor("s_" + name, list(shape), dt))

        identf = sb("identf", [128, 128]); identb = sb("identb", [128, 128], BF16)
        tri = sb("tri", [128, 128]); tri2 = sb("tri2", [128, 128])
        mgla = sb("mgla", [128, 128]); mk2 = sb("mk2", [128, 512])
        cm = sb("cm", [128, 2])
        rcos = sb("rcos", [128, 128]); rsin = sb("rsin", [128, 128])
        gmix = sb("gmix", [128, 8]); gffn = sb("gffn", [128, 8])
        gffn_rep = sb("gffn_rep", [128, 1024])
        ga_rep = sb("ga_rep", [128, 1024]); gq_rep = sb("gq_rep", [128, 1024]); gk_rep = sb("gk_rep", [128, 256])
        g256 = sb("g256", [128, 256]); g64q = sb("g64q", [128, 64]); g64k = sb("g64k", [128, 64])
        esink = sb("esink", [128, 16])
        wgk = sb("wgk", [17, 512]); glra = sb("glra", [17, 128])
        wglr_f = sb("wglr_f", [128, 128]); wglr = sb("wglr", [128, 128], BF16)
        skT = sb("skT", [128, 2048], BF16)
        ones_bf = sb("ones_bf", [128, 2], BF16)
        wring = [sb("wring%d" % i, [128, 4096], BF16) for i in range(NSLOT)]
        FS = sb("FS", [128, 5120])
        xts = [sb("xt%d" % i, [128, 1024]) for i in range(2)]; x2s = [sb("x2_%d" % i, [128, 1024]) for i in range(2)]
        junkA = sb("junkA", [128, 1024], BF16); junkD = sb("junkD", [128, 1024], BF16)
        qTs = sb("qTs", [128, 512]); kTs = sb("kTs", [128, 512]); ks = sb("ks", [128, 512])
        Lb = sb("Lb", [128, 512]); bTs = sb("bTs", [128, 512])
        E1 = sb("E1", [128, 512], BF16); E2 = sb("E2", [128, 512], BF16); E3 = sb("E3", [128, 512], BF16); Er = sb("Er", [128, 512], BF16)
        qtT = sb("qtT", [128, 512], BF16); ktT = sb("ktT", [128, 512], BF16)
        qi0 = sb("qi0", [128, 512], BF16); qi1 = sb("qi1", [128, 512], BF16)
        ke0 = sb("ke0", [128, 512], BF16); ke1 = sb("ke1", [128, 512], BF16)
        attm = sb("attm", [128, 512], BF16)
        vb = sb("vb", [128, 1024], BF16)
        S = sb("S", [128, 1024]); S1 = sb("S1", [128, 1024])
        Sb = sb("Sb", [128, 1024], BF16); S1b = sb("S1b", [128, 1024], BF16)
        nbref = sb("nbref", [128, 8]); pbref = sb("pbref", [128, 8]); dec = sb("dec", [128, 8])
        trA = sb("trA", [128, 1024], BF16); trB = sb("trB", [128, 1024], BF16)
        trC = sb("trC", [128, 1024], BF16); trD = sb("trD", [128, 1024], BF16)
        tmA = sb("tmA", [128, 1024], BF16); tmB = sb("tmB", [128, 1024], BF16)
        kq = sb("kq", [128, 256]); kn = sb("kn", [128, 256]); k2 = sb("k2", [128, 512], BF16)
        kT2 = [sb("kT2_%d" % i, [128, 512], BF16) for i in range(2)]
        vsw = [sb("vsw_%d" % i, [128, 256], BF16) for i in range(2)]
        rt = [sb("rt%d" % i, [128, 128]) for i in range(4)]
        praw = [sb("praw%d" % i, [128, 512], BF16) for i in range(2)]
        pTe = [sb("pTe%d" % i, [128, 512], BF16) for i in range(4)]
        pTo = [sb("pTo%d" % i, [128, 512], BF16) for i in range(4)]
        ss = sb("ss", [128, 2]); rs = sb("rs", [128, 2])
        ssqa = sb("ssqa", [128, 4]); rsa = sb("rsa", [128, 4])
        ssqq = sb("ssqq", [128, 16]); rsq = sb("rsq", [128, 16])
        ssqk = sb("ssqk", [128, 4]); rsk = sb("rsk", [128, 4])
        den = sb("den", [128, 16]); rden = sb("rden", [128, 16])
        qpT = sb("qpT", [128, 2048], BF16)
        sc2L = [sb("sc2_%d" % i, [128, 128]) for i in range(4)]
        tv = sb("tv", [128, 256]); tiu = sb("tiu", [128, 256], U32); tif = sb("tif", [128, 256])
        candL = [sb("cand_%d" % i, [128, 112]) for i in range(2)]; cand2L = [sb("cand2_%d" % i, [128, 112]) for i in range(2)]
        ciL = [sb("ci_%d" % i, [128, 112]) for i in range(2)]
        bv = sb("bv", [128, 128]); idxf = sb("idxf", [128, 128]); idxi = sb("idxi", [128, 128], I32)
        negm = sb("negm", [128, 8]); Z = sb("Z", [128, 8]); rZ = sb("rZ", [128, 8])
        eg = sb("eg", [128, 128]); gate = sb("gate", [128, 128])
        hd = sb("hd", [128, 128]); gl = sb("gl", [128, 128]); wgt = sb("wgt", [128, 128])
        Dr = [sb("Dr%d" % i, [128, 128], BF16) for i in range(ND)]
        RG = sb("RG", [128, NG * 2048], BF16)
        rg = [RG[:, i * 2048:(i + 1) * 2048] for i in range(NG)]

        def F(i, n=1):
            return FS[:, i * 1024:(i + n) * 1024]

        Tps = es.enter_context(nc.psum_tensor("Tps", [128, 1024], BF16))
        XN = es.enter_context(nc.psum_tensor("XN", [128, 1024], F32))
        NBK = 5
        banks = [es.enter_context(nc.psum_tensor("bk%d" % i, [128, 512], F32)) for i in range(NBK)]
        bstate = {"i": 0}

        def bank():
            i = bstate["i"]
            bstate["i"] = (i + 1) % NBK
            return banks[i], "bk%d" % i

        def PE(fn, r=(), w=()): return p.op("pe", fn, r, w)
        def DVE(fn, r=(), w=()): return p.op("dve", fn, r, w)
        def ACT(fn, r=(), w=()): return p.op("act", fn, r, w)
        def POOL(fn, r=(), w=()): return p.op("pool", fn, r, w)
        def DMA(fn, r=(), w=(), sem=None): return p.op("sp", fn, r, w, dma=sem)

        ldc = {"n": 0, "res": []}

        def load(out_ap, in_ap, wres):
            ldc["n"] += 1
            ldc["res"].append(wres)
            return DMA(lambda e: e.dma_start(out=out_ap, in_=in_ap), (), [wres], sem="ldc")

        def load_barrier():
            ev = (p.dsem["ldc"], p.dcnt["ldc"])
            for r in ldc["res"]:
                p.lastw[r] = ev
            ldc["res"] = []

        def transposes(src, nblk, rres, dst, wres, eng="act"):
            def f(e):
                ins = None
                for kc in range(nblk):
                    ins = e.transpose(out=Tps[:, kc * 128:(kc + 1) * 128], in_=src[:, kc * 128:(kc + 1) * 128], identity=identb[:])
                return ins
            PE(f, [rres, "identb"], ["T"])
            ACT(lambda e: e.copy(out=dst[:, 0:nblk * 128], in_=Tps[:, 0:nblk * 128]), ["T"], [wres])

        load(identf[:], cmat_d[0], "identf"); load(tri[:], cmat_d[1], "tri"); load(tri2[:], cmat_d[2], "tri2")
        load(mgla[:], cmat_d[3], "mgla")
        load(mk2[:, 0:128], mprev_d, "mk2"); load(mk2[:, 128:256], mprev_d, "mk2")
        load(mk2[:, 256:384], cmat_d[4], "mk2"); load(mk2[:, 384:512], cmat_d[4], "mk2")
        load(cm[:], cm_d, "cm"); load(rcos[:], cos_d, "rcos"); load(rsin[:], sin_d, "rsin")
        load(gmix[:], gmix_d, "gmix"); load(gffn[:], gffn_d, "gffn")
        load(gffn_rep[:], gffn_row_d.broadcast_to([128, 1024]), "gffn_rep")
        load(g256[:], ga_d.broadcast_to([128, 256]), "g256")
        load(g64q[:], gq_d.broadcast_to([128, 64]), "g64q"); load(g64k[:], gk_d.broadcast_to([128, 64]), "g64k")
        load(esink[:], sinks_d.broadcast_to([128, 16]), "esink")
        load(wgk[0:16, :], wgk2_d, "wgk"); load(wgk[16:17, :], bgk_d, "wgk")
        load(wglr_f[:].rearrange("p (k n) -> p k n", n=16),
             w_in_d.rearrange("(k p) n -> p k n", p=128)[:, :, 3072:3088], "wglr_f")
        for q4 in range(4):
            load(F(0, 2)[:, q4 * 512:(q4 + 1) * 512].rearrange("p (a d) -> p a d", d=128),
                 sk_d[q4 * 4:(q4 + 1) * 4].rearrange("a k d -> k a d"), "skl%d" % q4)
        load_barrier()
        DVE(lambda e: e.tensor_copy(out=identb[:], in_=identf[:]), ["identf"], ["identb"])
        DVE(lambda e: e.tensor_copy(out=ga_rep[:].rearrange("p (h e) -> p h e", e=256),
                                    in_=g256[:, None, :].broadcast_to([128, 4, 256])), ["g256"], ["ga_rep"])
        DVE(lambda e: e.tensor_scalar(out=gq_rep[:].rearrange("p (h e) -> p h e", e=64),
                                      in0=g64q[:, None, :].broadcast_to([128, 16, 64]), scalar1=0.125, scalar2=None, op0=ALU.mult),
            ["g64q"], ["gq_rep"])
        DVE(lambda e: e.tensor_copy(out=gk_rep[:].rearrange("p (h e) -> p h e", e=64),
                                    in_=g64k[:, None, :].broadcast_to([128, 4, 64])), ["g64k"], ["gk_rep"])
        ACT(lambda e: e.activation(out=esink[:], in_=esink[:], func=AF.Exp), ["esink"], ["esink"])
        POOL(lambda e: e.memset(glra[:], 1.0), (), ["glra"])
        POOL(lambda e: e.memset(idxi[:], 0), (), ["idxi", "idxi_e"])
        POOL(lambda e: e.memset(qi0[:], 0.0), (), ["qi0"])
        POOL(lambda e: e.memset(qi1[:], 0.0), (), ["qi1"])
        POOL(lambda e: e.memset(ones_bf[:], 1.0), (), ["ones_bf"])
        DVE(lambda e: e.tensor_tensor(out=wglr[:].rearrange("p (k n) -> p k n", n=16),
                                      in0=wglr_f[:].rearrange("p (k n) -> p k n", n=16),
                                      in1=gmix[:, :, None].broadcast_to([128, 8, 16]), op=ALU.mult),
            ["wglr_f", "gmix"], ["wglr"])
        for q4 in range(4):
            bk, bres = bank()

            def f(e, q4=q4, bk=bk):
                ins = None
                for a in range(4):
                    hp = q4 * 4 + a
                    ins = e.transpose(out=bk[:, a * 128:(a + 1) * 128], in_=F(0, 2)[:, hp * 128:(hp + 1) * 128], identity=identf[:])
                return ins
            PE(f, ["skl%d" % q4, "F0", "F1", "identf"], [bres])
            ACT(lambda e, q4=q4, bk=bk: e.copy(out=skT[:, q4 * 512:(q4 + 1) * 512], in_=bk[:, :]), [bres], ["skT"])

        def wsrc(c):
            if c < 13:
                return w_in_d, W_IN_COLS[c], gmix
            if c < 15:
                return wa_d, (c - 13) * 512, None
            if c < 17:
                return wb_d, (c - 15) * 512, None
            if c < 19:
                return wo_d, (c - 17) * 512, None
            return wpq_d, (c - 19) * 512, gffn

        hcount = 0
        for c in range(NCH):
            src, col, g = wsrc(c)
            slot = c % NSLOT
            wr = wring[slot]
            for half in range(2):
                st_i = hcount % 2
                hcount += 1
                stg = F(st_i * 2, 2)
                sres = ["F%d" % (st_i * 2), "F%d" % (st_i * 2 + 1)]
                srcap = src.rearrange("(k p) n -> p k n", p=128)[:, half * 4:(half + 1) * 4, col:col + 512]
                DMA(lambda e, stg=stg, srcap=srcap: e.dma_start(out=stg.rearrange("p (k n) -> p k n", n=512), in_=srcap),
                    (), sres, sem="stg%d" % st_i)
                for k4 in range(4):
                    kc = half * 4 + k4
                    o_ap = wr[:, kc * 512:(kc + 1) * 512]
                    i_ap = stg[:, k4 * 512:(k4 + 1) * 512]
                    if g is None:
                        if k4 % 2 == 0:
                            DVE(lambda e, o_ap=o_ap, i_ap=i_ap: e.tensor_copy(out=o_ap, in_=i_ap), sres, ["W%dk%d" % (slot, kc)])
                        else:
                            ACT(lambda e, o_ap=o_ap, i_ap=i_ap: e.copy(out=o_ap, in_=i_ap), sres, ["W%dk%d" % (slot, kc)])
                    else:
                        gs = g[:, kc:kc + 1]
                        if k4 % 2 == 0:
                            DVE(lambda e, o_ap=o_ap, i_ap=i_ap, gs=gs: e.tensor_scalar(out=o_ap, in0=i_ap, scalar1=gs, scalar2=None, op0=ALU.mult),
                                sres + ["gmix", "gffn"], ["W%dk%d" % (slot, kc)])
                        else:
                            ACT(lambda e, o_ap=o_ap, i_ap=i_ap, gs=gs: e.activation(out=o_ap, in_=i_ap, func=AF.Copy, scale=gs),
                                sres + ["gmix", "gffn"], ["W%dk%d" % (slot, kc)])
            DMA(lambda e, c=c, wr=wr: e.dma_start(out=wscr[c], in_=wr[:]), ["W%d" % slot] + ["W%dk%d" % (slot, k) for k in range(8)], ["wscr%d" % c], sem="wst%d" % slot)

        UVRES = ["ccs%d" % i for i in range(4)]
        ci_ = 0
        for q8 in range(16):
            r0, r1 = q8 * 1024, (q8 + 1) * 1024
            for (src_d, c0) in ((pu_d, 0), (pv_d, 1024)):
                p.op("pool", lambda e, r0=r0, r1=r1, src_d=src_d, c0=c0: e.dma_start(out=uvs[r0:r1, c0:c0 + 1024], in_=src_d[r0:r1, :]),
                     ["wscr%d" % c_ for c_ in range(NCH)], ["ccs%d" % (ci_ % 4)], dma="cc%d" % (ci_ % 4))
                ci_ += 1
        total_uses = NSEQ * NT * NCH
        wstate = {"issued": 0, "use": -1, "lastc": None, "rec": [], "order": None}

        def wget(n):
            c = n % NCH
            if wstate["lastc"] != c:
                wstate["use"] += 1
                wstate["lastc"] = c
                if p.dry:
                    wstate["rec"].append(c)
                else:
                    assert wstate["order"][wstate["use"] % NCH] == c, (wstate["use"], c)
            if p.dry:
                return wring[0], "W0"
            u = wstate["use"]
            while wstate["issued"] < min(total_uses, u + NSLOT):
                m = wstate["issued"]
                cid = wstate["order"][m % NCH]
                slot = m % NSLOT
                DMA(lambda e, cid=cid, slot=slot: e.dma_start(out=wring[slot][:], in_=wscr[cid]), ["wscr%d" % cid], ["W%d" % slot], sem="wld%d" % slot)
                wstate["issued"] += 1
            slot = u % NSLOT
            return wring[slot], "W%d" % slot

        def proj_tok(n, lhs, lres, bk, bres):
            W, wres = wget(n)

            def f(e):
                ins = None
                for kc in range(8):
                    ins = e.matmul(bk[:, :], lhsT=lhs[:, kc * 128:(kc + 1) * 128], rhs=W[:, kc * 512:(kc + 1) * 512], start=(kc == 0), stop=(kc == 7))
                return ins
            PE(f, [lres, wres], [bres])

        def proj_feat(n, rhs, rres, bk, bres):
            W, wres = wget(n)

            def f(e):
                ins = None
                for j in range(4):
                    for kc in range(8):
                        ins = e.matmul(bk[:, j * 128:(j + 1) * 128], lhsT=W[:, kc * 512 + j * 128: kc * 512 + (j + 1) * 128],
                                       rhs=rhs[:, kc * 128:(kc + 1) * 128], start=(kc == 0), stop=(kc == 7))
                return ins
            PE(f, [rres, wres], [bres])

        def tap(name, src, tok0, rres, ncols=1024):
            if DBG:
                DMA(lambda e: e.dma_start(out=dbg_d[name][tok0:tok0 + 128, :], in_=src), [rres], (), sem="dbg_" + name)

        def v3(ap, inner):
            return ap.rearrange("p (a b) -> p a b", b=inner)

        def rope(buf, bres, nh, ti):
            bvw = v3(buf, 64)
            x1 = bvw[:, :, 0:8]
            x2_ = bvw[:, :, 8:16]
            cs = rcos[:, None, ti * 8:(ti + 1) * 8].broadcast_to([128, nh, 8])
            sn = rsin[:, None, ti * 8:(ti + 1) * 8].broadcast_to([128, nh, 8])
            t = [v3(rt[i][:, 0:nh * 8], 8) for i in range(4)]
            DVE(lambda e: e.tensor_tensor(out=t[0], in0=x1, in1=cs, op=ALU.mult), [bres, "rcos"], ["rt0"])
            DVE(lambda e: e.tensor_tensor(out=t[1], in0=x2_, in1=sn, op=ALU.mult), [bres, "rsin"], ["rt1"])
            DVE(lambda e: e.tensor_tensor(out=t[2], in0=x2_, in1=cs, op=ALU.mult), [bres, "rcos"], ["rt2"])
            DVE(lambda e: e.tensor_tensor(out=t[3], in0=x1, in1=sn, op=ALU.mult), [bres, "rsin"], ["rt3"])
            DVE(lambda e: e.tensor_tensor(out=x1, in0=t[0], in1=t[1], op=ALU.subtract), ["rt0", "rt1", bres], [bres])
            DVE(lambda e: e.tensor_tensor(out=x2_, in0=t[2], in1=t[3], op=ALU.add), ["rt2", "rt3", bres], [bres])

        def rstd_op(ssq_ap, out_ap, n, rres, wres):
            ACT(lambda e: e.activation(out=out_ap, in_=ssq_ap, func=AF.Sqrt, bias=EPS, scale=1.0 / n), [rres], [wres])
            DVE(lambda e: e.reciprocal(out=out_ap, in_=out_ap), [wres], [wres])

        SCALE = 128.0 ** -0.5

        def mixer(si, ti):
            gi = si * NT + ti
            tok0 = gi * 128
            n0 = gi * NCH
            par = ti % 2
            xt = xts[gi % 2]; x2 = x2s[gi % 2]
            XT = "xt%d" % (gi % 2); X2 = "x2_%d" % (gi % 2)
            A4, B4, C4, D4, E4 = F(0), F(1), F(2), F(3), F(4)
            PL = DVE if gi == 0 else POOL

            if ti == 0:
                PL(lambda e: e.memset(S[:], 0.0), (), ["S"])
                PL(lambda e: e.memset(Sb[:], 0.0), (), ["Sb"])
            DMA(lambda e: e.dma_start(out=xt[:], in_=x_d[tok0:tok0 + 128, :]), (), [XT], sem="ldx%d" % (gi % 2))
            ACT(lambda e: e.activation(out=junkA[:], in_=xt[:], func=AF.Square, accum_out=ss[:, 0:1]), [XT], ["ss0", "junkA"])
            rstd_op(ss[:, 0:1], rs[:, 0:1], 1024, "ss0", "rs0")
            DVE(lambda e: e.tensor_scalar(out=tmA[:], in0=xt[:], scalar1=rs[:, 0:1], scalar2=None, op0=ALU.mult), [XT, "rs0"], ["tmA"])
            transposes(tmA, 8, "tmA", trA, "trA")
            hT = trA
            yield

            def gla():
                bk, br = bank(); proj_feat(n0 + 0, hT, "trA", bk, br)
                ACT(lambda e, bk=bk: e.copy(out=qTs[:], in_=bk[:, :]), [br], ["qTs"])
                yield
                bk, br = bank(); proj_feat(n0 + 1, hT, "trA", bk, br)
                ACT(lambda e, bk=bk: e.copy(out=kTs[:], in_=bk[:, :]), [br], ["kTs"])
                bk, br = bank(); proj_tok(n0 + 1, hT, "trA", bk, br)
                ACT(lambda e, bk=bk: e.copy(out=ks[:], in_=bk[:, :]), [br], ["ks"])
                yield
                bk, br = bank()

                def f(e, bk=bk):
                    ins = None
                    for kc in range(8):
                        ins = e.matmul(bk[0:16, 0:128], lhsT=wglr[:, kc * 16:(kc + 1) * 16], rhs=hT[:, kc * 128:(kc + 1) * 128], start=(kc == 0), stop=(kc == 7))
                    return ins
                PE(f, ["trA", "wglr"], [br])
                ACT(lambda e, bk=bk: e.copy(out=glra[0:16, :], in_=bk[0:16, 0:128]), [br], ["glra"])
                bk, br = bank()
                PE(lambda e, bk=bk: e.matmul(bk[:, :], lhsT=glra[0:17, :], rhs=wgk[0:17, :], start=True, stop=True), ["glra", "wgk"], [br])
                ACT(lambda e, bk=bk: e.activation(out=Lb[:], in_=bk[:, :], func=AF.Exp, scale=-1.0), [br], ["Lb"])
                ACT(lambda e: e.activation(out=Lb[:], in_=Lb[:], func=AF.Ln, bias=1.0), ["Lb"], ["Lb"])
                bkA, brA = bank()

                def f(e, bkA=bkA):
                    ins = None
                    for h in range(4):
                        ins = e.matmul(bkA[:, h * 128:(h + 1) * 128], lhsT=Lb[:, h * 128:(h + 1) * 128], rhs=tri[:], start=True, stop=True)
                    return ins
                PE(f, ["Lb", "tri"], [brA])
                bkB, brB = bank()
                PE(lambda e, bkB=bkB: e.matmul(bkB[:, :], lhsT=tri2[:], rhs=Lb[:], start=True, stop=True), ["Lb", "tri2"], [brB])
                ACT(lambda e, bkA=bkA: e.copy(out=bTs[:], in_=bkA[:, :]), [brA], ["bTs"])
                ACT(lambda e, bkB=bkB: e.activation(out=Er[:], in_=bkB[:, :], func=AF.Exp), [brB], ["Er"])
                yield
                bTv = v3(bTs[:], 64)
                DVE(lambda e: e.tensor_scalar(out=nbref[:], in0=bTv[:, :, 32], scalar1=-1.0, scalar2=None, op0=ALU.mult), ["bTs"], ["nbref"])
                DVE(lambda e: e.tensor_copy(out=pbref[:], in_=bTv[:, :, 32]), ["bTs"], ["pbref"])
                ACT(lambda e: e.activation(out=dec[:], in_=bTv[:, :, 63], func=AF.Exp), ["bTs"], ["dec"])
                for g in range(8):
                    ACT(lambda e, g=g: e.activation(out=E1[:, g * 64:(g + 1) * 64], in_=bTs[:, g * 64:(g + 1) * 64], func=AF.Exp,
                                                    bias=nbref[:, g:g + 1], scale=1.0), ["bTs", "nbref"], ["E1"])
                for g in range(8):
                    ACT(lambda e, g=g: e.activation(out=E2[:, g * 64:(g + 1) * 64], in_=bTs[:, g * 64:(g + 1) * 64], func=AF.Exp,
                                                    bias=pbref[:, g:g + 1], scale=-1.0), ["bTs", "pbref"], ["E2"])
                ACT(lambda e: e.activation(out=E3[:], in_=bTs[:], func=AF.Exp), ["bTs"], ["E3"])
                yield
                DVE(lambda e: e.scalar_tensor_tensor(out=qtT[:], in0=qTs[:], scalar=SCALE, op0=ALU.mult, in1=E1[:], op1=ALU.mult), ["qTs", "E1"], ["qtT"])
                DVE(lambda e: e.tensor_tensor(out=ktT[:], in0=kTs[:], in1=E2[:], op=ALU.mult), ["kTs", "E2"], ["ktT"])
                DVE(lambda e: e.scalar_tensor_tensor(out=v3(qi0[:], 128)[:, :, 0:64], in0=v3(qTs[:], 128)[:, :, 0:64], scalar=SCALE, op0=ALU.mult,
                                                     in1=v3(E3[:], 128)[:, :, 0:64], op1=ALU.mult), ["qTs", "E3"], ["qi0"])
                DVE(lambda e: e.scalar_tensor_tensor(out=v3(qi1[:], 128)[:, :, 64:128], in0=v3(qTs[:], 128)[:, :, 64:128], scalar=SCALE, op0=ALU.mult,
                                                     in1=v3(E3[:], 128)[:, :, 64:128], op1=ALU.mult), ["qTs", "E3"], ["qi1"])
                DVE(lambda e: e.scalar_tensor_tensor(out=ke0[:], in0=ks[:], scalar=cm[:, 0:1], op0=ALU.mult, in1=Er[:], op1=ALU.mult), ["ks", "Er", "cm"], ["ke0"])
                DVE(lambda e: e.scalar_tensor_tensor(out=ke1[:], in0=ks[:], scalar=cm[:, 1:2], op0=ALU.mult, in1=Er[:], op1=ALU.mult), ["ks", "Er", "cm"], ["ke1"])
                yield
                for hf in range(2):
                    bk, br = bank(); proj_tok(n0 + 2 + hf, hT, "trA", bk, br)
                    ACT(lambda e, bk=bk, hf=hf: e.copy(out=vb[:, hf * 512:(hf + 1) * 512], in_=bk[:, :]), [br], ["vb"])
                    yield
                for hf in range(2):
                    bk, br = bank(); proj_tok(n0 + 4 + hf, hT, "trA", bk, br)
                    ACT(lambda e, bk=bk, hf=hf: e.activation(out=A4[:, hf * 512:(hf + 1) * 512], in_=bk[:, :], func=AF.Sigmoid), [br], ["F0"])
                    DVE(lambda e, bk=bk, hf=hf: e.tensor_tensor(out=B4[:, hf * 512:(hf + 1) * 512], in0=bk[:, :], in1=A4[:, hf * 512:(hf + 1) * 512], op=ALU.mult),
                        [br, "F0"], ["F1"])
                    yield
                PL(lambda e: e.tensor_tensor(out=B4, in0=B4, in1=ga_rep[:], op=ALU.mult), ["F1", "ga_rep"], ["F1"])
                bkC, brC = bank()

                def f(e, bkC=bkC):
                    ins = None
                    for h in range(4):
                        ins = e.matmul(bkC[:, h * 128:(h + 1) * 128], lhsT=ktT[:, h * 128:(h + 1) * 128], rhs=qtT[:, h * 128:(h + 1) * 128], start=True, stop=True)
                    return ins
                PE(f, ["ktT", "qtT"], [brC])
                DVE(lambda e, bkC=bkC: e.tensor_tensor(out=v3(attm[:], 128), in0=v3(bkC[:, :], 128), in1=mgla[:, None, :].broadcast_to([128, 4, 128]), op=ALU.mult),
                    [brC, "mgla"], ["attm"])

                def dstate(ke, keres):
                    bks = [bank(), bank()]

                    def f(e):
                        ins = None
                        for h in range(4):
                            ins = e.matmul(bks[h // 2][0][:, (h % 2) * 256:(h % 2) * 256 + 256], lhsT=ke[:, h * 128:(h + 1) * 128],
                                           rhs=vb[:, h * 256:(h + 1) * 256], start=True, stop=True)
                        return ins
                    PE(f, [keres, "vb"], [bks[0][1], bks[1][1]])
                    return bks
                d0 = dstate(ke0, "ke0")
                for h in range(4):
                    DVE(lambda e, h=h: e.scalar_tensor_tensor(out=S1[:, h * 256:(h + 1) * 256], in0=S[:, h * 256:(h + 1) * 256], scalar=dec[:, 2 * h:2 * h + 1],
                                                              op0=ALU.mult, in1=d0[h // 2][0][:, (h % 2) * 256:(h % 2) * 256 + 256], op1=ALU.add),
                        ["S", "dec", d0[h // 2][1]], ["S1"])
                ACT(lambda e: e.copy(out=S1b[:], in_=S1[:]), ["S1"], ["S1b"])
                yield
                ob = [bank(), bank()]

                def f(e):
                    ins = None
                    for h in range(4):
                        o_ap = ob[h // 2][0][:, (h % 2) * 256:(h % 2) * 256 + 256]
                        e.matmul(o_ap, lhsT=attm[:, h * 128:(h + 1) * 128], rhs=vb[:, h * 256:(h + 1) * 256], start=True, stop=False)
                        e.matmul(o_ap, lhsT=qi0[:, h * 128:(h + 1) * 128], rhs=Sb[:, h * 256:(h + 1) * 256], start=False, stop=False)
                        ins = e.matmul(o_ap, lhsT=qi1[:, h * 128:(h + 1) * 128], rhs=S1b[:, h * 256:(h + 1) * 256], start=False, stop=True)
                    return ins
                PE(f, ["attm", "vb", "qi0", "qi1", "Sb", "S1b"], [ob[0][1], ob[1][1]])
                for h in range(4):
                    ACT(lambda e, h=h: e.activation(out=junkA[:, 0:256], in_=ob[h // 2][0][:, (h % 2) * 256:(h % 2) * 256 + 256], func=AF.Square,
                                                    accum_out=ssqa[:, h:h + 1]), [ob[h // 2][1]], ["ssqa", "junkA"])
                rstd_op(ssqa[:], rsa[:], 256, "ssqa", "rsa")
                o_a = tmB
                for h in range(4):
                    DVE(lambda e, h=h: e.scalar_tensor_tensor(out=o_a[:, h * 256:(h + 1) * 256], in0=ob[h // 2][0][:, (h % 2) * 256:(h % 2) * 256 + 256],
                                                              scalar=rsa[:, h:h + 1], op0=ALU.mult, in1=B4[:, h * 256:(h + 1) * 256], op1=ALU.mult),
                        [ob[h // 2][1], "rsa", "F1"], ["tmB"])
                if DBG:
                    DVE(lambda e: e.tensor_copy(out=A4, in_=o_a[:]), ["tmB"], ["F0"])
                    tap("oa", A4, tok0, "F0")
                transposes(o_a, 8, "tmB", trB, "trB")
                yield
                d1 = dstate(ke1, "ke1")
                for h in range(4):
                    DVE(lambda e, h=h: e.scalar_tensor_tensor(out=S[:, h * 256:(h + 1) * 256], in0=S1[:, h * 256:(h + 1) * 256], scalar=dec[:, 2 * h + 1:2 * h + 2],
                                                              op0=ALU.mult, in1=d1[h // 2][0][:, (h % 2) * 256:(h % 2) * 256 + 256], op1=ALU.add),
                        ["S1", "dec", d1[h // 2][1]], ["S"])
                ACT(lambda e: e.copy(out=Sb[:], in_=S[:]), ["S"], ["Sb"])
                yield


                for hf in range(2):
                    bk, br = bank(); proj_tok(n0 + 13 + hf, trB, "trB", bk, br)
                    DVE(lambda e, bk=bk, hf=hf: e.tensor_tensor(out=B4[:, hf * 512:(hf + 1) * 512], in0=bk[:, :], in1=D4[:, hf * 512:(hf + 1) * 512], op=ALU.mult),
                        [br, "F3"], ["F1"])
                    yield

            def swa():
                qn = C4
                for hf in range(2):
                    bk, br = bank(); proj_tok(n0 + 6 + hf, hT, "trA", bk, br)
                    ACT(lambda e, bk=bk, hf=hf: e.activation(out=A4[:, hf * 512:(hf + 1) * 512], in_=bk[:, :], func=AF.Square), [br], ["F0"])
                    DVE(lambda e, hf=hf: e.tensor_reduce(out=ssqq[:, hf * 8:(hf + 1) * 8], in_=v3(A4[:, hf * 512:(hf + 1) * 512], 64), axis=AX.X, op=ALU.add),
                        ["F0"], ["ssqq%d" % hf])
                    rstd_op(ssqq[:, hf * 8:(hf + 1) * 8], rsq[:, hf * 8:(hf + 1) * 8], 64, "ssqq%d" % hf, "rsq%d" % hf)
                    DVE(lambda e, bk=bk, hf=hf: e.tensor_tensor(out=v3(qn[:, hf * 512:(hf + 1) * 512], 64), in0=v3(bk[:, :], 64),
                                                                in1=rsq[:, hf * 8:(hf + 1) * 8, None].broadcast_to([128, 8, 64]), op=ALU.mult),
                        [br, "rsq%d" % hf], ["F2"])
                    yield
                PL(lambda e: e.tensor_tensor(out=qn, in0=qn, in1=gq_rep[:], op=ALU.mult), ["F2", "gq_rep"], ["F2"])
                rope(qn, "F2", 16, ti)
                ACT(lambda e: e.copy(out=tmA[:], in_=qn), ["F2"], ["tmA"])
                transposes(tmA, 8, "tmA", trC, "trC")
                yield
                bk, br = bank(); proj_tok(n0 + 8, hT, "trA", bk, br)
                ACT(lambda e, bk=bk: e.activation(out=kq[:], in_=bk[:, 0:256], func=AF.Square), [br], ["kq"])
                DVE(lambda e: e.tensor_reduce(out=ssqk[:], in_=v3(kq[:], 64), axis=AX.X, op=ALU.add), ["kq"], ["ssqk"])
                rstd_op(ssqk[:], rsk[:], 64, "ssqk", "rsk")
                DVE(lambda e, bk=bk: e.tensor_tensor(out=v3(kn[:], 64), in0=v3(bk[:, 0:256], 64), in1=rsk[:, :, None].broadcast_to([128, 4, 64]), op=ALU.mult),
                    [br, "rsk"], ["kn"])
                ACT(lambda e, bk=bk: e.copy(out=vsw[par][:], in_=bk[:, 256:512]), [br], ["vsw%d" % par])
                PL(lambda e: e.tensor_tensor(out=kn[:], in0=kn[:], in1=gk_rep[:], op=ALU.mult), ["kn", "gk_rep"], ["kn"])
                rope(kn[:], "kn", 4, ti)
                k2v = k2[:].rearrange("p (g c d) -> p g c d", c=2, d=64)
                ACT(lambda e: e.copy(out=k2v[:, :, 0, :], in_=v3(kn[:], 64)), ["kn"], ["k2"])
                ACT(lambda e: e.copy(out=k2v[:, :, 1, :], in_=v3(kn[:], 64)), ["kn"], ["k2"])
                transposes(k2, 4, "k2", kT2[par], "kT2_%d" % par)
                yield
                has_prev = ti > 0
                ppar = 1 - par
                for g in range(4):
                    for odd in range(2):
                        bk, br = bank()
                        lo = 64 * odd

                        def f(e, bk=bk, lo=lo, g=g):
                            ins = None
                            if has_prev:
                                ins = e.matmul(bk[:, 0:256], lhsT=kT2[ppar][lo:lo + 64, g * 128:(g + 1) * 128], rhs=trC[lo:lo + 64, 2 * g * 128:(2 * g + 2) * 128],
                                               start=True, stop=True)
                            ins = e.matmul(bk[:, 256:512], lhsT=kT2[par][lo:lo + 64, g * 128:(g + 1) * 128], rhs=trC[lo:lo + 64, 2 * g * 128:(2 * g + 2) * 128],
                                           start=True, stop=True)
                            return ins
                        PE(f, ["trC", "kT2_0", "kT2_1"], [br])
                        c0 = 0 if has_prev else 256
                        pr = praw[odd]
                        dst = (pTo if odd else pTe)[g]
                        dres = ("pTo%d" if odd else "pTe%d") % g
                        ACT(lambda e, bk=bk, pr=pr, c0=c0: e.activation(out=pr[:, c0:512], in_=bk[:, c0:512], func=AF.Exp), [br], ["praw%d" % odd])
                        PL(lambda e, pr=pr, dst=dst, c0=c0: e.tensor_tensor(out=dst[:, c0:512], in0=pr[:, c0:512], in1=mk2[:, c0:512], op=ALU.mult),
                             ["praw%d" % odd, "mk2"], [dres])
                    yield
                OB = [bank(), bank()]
                SBk, SBr = bank()

                def f(e):
                    ins = None
                    for h in range(16):
                        g, j = h // 4, h % 4
                        src = (pTo if j % 2 else pTe)[g]
                        slot = j // 2
                        o_ap = OB[h // 8][0][:, (h % 8) * 64:(h % 8) * 64 + 64]
                        if has_prev:
                            e.matmul(o_ap, lhsT=src[:, slot * 128:(slot + 1) * 128], rhs=vsw[ppar][:, g * 64:(g + 1) * 64], start=True, stop=False)
                        ins = e.matmul(o_ap, lhsT=src[:, 256 + slot * 128:256 + (slot + 1) * 128], rhs=vsw[par][:, g * 64:(g + 1) * 64], start=(not has_prev), stop=True)
                    for h in range(16):
                        g, j = h // 4, h % 4
                        src = (pTo if j % 2 else pTe)[g]
                        slot = j // 2
                        s_ap = SBk[:, h:h + 1]
                        if has_prev:
                            e.matmul(s_ap, lhsT=src[:, slot * 128:(slot + 1) * 128], rhs=ones_bf[:, 0:1], start=True, stop=False)
                        ins = e.matmul(s_ap, lhsT=src[:, 256 + slot * 128:256 + (slot + 1) * 128], rhs=ones_bf[:, 0:1], start=(not has_prev), stop=True)
                    return ins
                PE(f, ["pTe0", "pTe1", "pTe2", "pTe3", "pTo0", "pTo1", "pTo2", "pTo3", "vsw0", "vsw1", "ones_bf"], [OB[0][1], OB[1][1], SBr])
                DVE(lambda e: e.tensor_tensor(out=den[:], in0=SBk[:, 0:16], in1=esink[:], op=ALU.add), [SBr, "esink"], ["den"])
                DVE(lambda e: e.reciprocal(out=rden[:], in_=den[:]), ["den"], ["rden"])
                o_b = tmB
                for hf in range(2):
                    DVE(lambda e, hf=hf: e.tensor_tensor(out=v3(o_b[:, hf * 512:(hf + 1) * 512], 64), in0=v3(OB[hf][0][:, :], 64),
                                                         in1=rden[:, hf * 8:(hf + 1) * 8, None].broadcast_to([128, 8, 64]), op=ALU.mult),
                        [OB[hf][1], "rden"], ["tmB"])
                if DBG:
                    DVE(lambda e: e.tensor_copy(out=A4, in_=o_b[:]), ["tmB"], ["F0"])
                    tap("ob", A4, tok0, "F0")
                transposes(o_b, 8, "tmB", trD, "trD")
                yield

                for hf in range(2):
                    bk, br = bank(); proj_tok(n0 + 15 + hf, trD, "trD", bk, br)
                    DVE(lambda e, bk=bk, hf=hf: e.tensor_tensor(out=C4[:, hf * 512:(hf + 1) * 512], in0=bk[:, :], in1=E4[:, hf * 512:(hf + 1) * 512], op=ALU.mult),
                        [br, "F4"], ["F2"])
                    yield

            def gates():
                for hf in range(2):
                    bk, br = bank(); proj_tok(n0 + 9 + hf, hT, "trA", bk, br)
                    ACT(lambda e, bk=bk, hf=hf: e.activation(out=D4[:, hf * 512:(hf + 1) * 512], in_=bk[:, :], func=AF.Sigmoid), [br], ["F3"])
                    yield
                for hf in range(2):
                    bk, br = bank(); proj_tok(n0 + 11 + hf, hT, "trA", bk, br)
                    ACT(lambda e, bk=bk, hf=hf: e.activation(out=E4[:, hf * 512:(hf + 1) * 512], in_=bk[:, :], func=AF.Sigmoid), [br], ["F4"])
                    yield

            subs = [gla(), swa(), gates()]
            while subs:
                for g_ in list(subs):
                    try:
                        next(g_)
                    except StopIteration:
                        subs.remove(g_)
                yield
            PL(lambda e: e.tensor_tensor(out=tmA[:], in0=B4, in1=C4, op=ALU.add), ["F1", "F2"], ["tmA"])
            transposes(tmA, 8, "tmA", trB, "trB")
            yield
            if STAGE <= 2.8:
                return
            for hf in range(2):
                bk, br = bank(); proj_tok(n0 + 17 + hf, trB, "trB", bk, br)
                DVE(lambda e, bk=bk, hf=hf: e.tensor_tensor(out=x2[:, hf * 512:(hf + 1) * 512], in0=bk[:, :], in1=xt[:, hf * 512:(hf + 1) * 512], op=ALU.add),
                    [br, XT], [X2])
                yield
            tap("x2", x2[:], tok0, X2)

        def peer(si, ti, nxt):
            gi = si * NT + ti
            tok0 = gi * 128
            n0 = gi * NCH
            xt = xts[gi % 2]; x2 = x2s[gi % 2]
            XT = "xt%d" % (gi % 2); X2 = "x2_%d" % (gi % 2)
            ACT(lambda e: e.activation(out=junkA[:], in_=x2[:], func=AF.Square, accum_out=ss[:, 1:2]), [X2], ["ss1", "junkA"])
            rstd_op(ss[:, 1:2], rs[:, 1:2], 1024, "ss1", "rs1")
            DVE(lambda e: e.tensor_scalar(out=tmA[:], in0=x2[:], scalar1=rs[:, 1:2], scalar2=None, op0=ALU.mult), [X2, "rs1"], ["tmA"])
            DVE(lambda e: e.scalar_tensor_tensor(out=XN[:, :], in0=x2[:], scalar=rs[:, 1:2], op0=ALU.mult, in1=gffn_rep[:], op1=ALU.mult),
                [X2, "rs1", "gffn_rep"], ["XN"])
            transposes(tmA, 8, "tmA", trA, "trA")
            for cc in range(4):
                bk, br = bank(); proj_feat(n0 + 19 + cc, trA, "trA", bk, br)
                ACT(lambda e, bk=bk, cc=cc: e.copy(out=qpT[:, cc * 512:(cc + 1) * 512], in_=bk[:, :]), [br], ["qpT"])
            sc = RG[:, 0:4096].bitcast(F32)
            SCR = ["rg0", "rg1", "rgu0", "rgu1"]
            for half in range(2):
                scb = [bank(), bank()]

                def f(e, scb=scb, half=half):
                    ins = None
                    for q in range(8):
                        hp = half * 8 + q
                        ins = e.matmul(scb[q // 4][0][:, (q % 4) * 128:(q % 4 + 1) * 128], lhsT=qpT[:, hp * 128:(hp + 1) * 128], rhs=skT[:, hp * 128:(hp + 1) * 128],
                                       start=True, stop=True)
                    return ins
                PE(f, ["qpT", "skT"], [b_[1] for b_ in scb])
                for i in range(2):
                    ACT(lambda e, i=i, scb=scb, half=half: e.copy(out=sc[:, (half * 2 + i) * 512:(half * 2 + i + 1) * 512], in_=scb[i][0][:, :]),
                        [scb[i][1]], SCR)

            def selback():
                for hg in range(4):
                    hps = [hg * 4 + q for q in range(4)]
                    sls = [sc[:, hp * 128:(hp + 1) * 128] for hp in hps]
                    for q, hp in enumerate(hps):
                        DVE(lambda e, sl=sls[q], hp=hp: e.max(out=tv[:, hp * 16:hp * 16 + 8], in_=sl), SCR, ["tv%d" % hp])
                    for q, hp in enumerate(hps):
                        DVE(lambda e, sl=sls[q], hp=hp: e.max_index(out=tiu[:, hp * 16:hp * 16 + 8], in_max=tv[:, hp * 16:hp * 16 + 8], in_values=sl), SCR + ["tv%d" % hp], ["tiu%d" % hp])
                    for q, hp in enumerate(hps):
                        DVE(lambda e, sl=sls[q], hp=hp, q=q: e.match_replace(out=sc2L[q][:], in_to_replace=tv[:, hp * 16:hp * 16 + 8], in_values=sl, imm_value=-1e30),
                            SCR + ["tv%d" % hp], ["sc2_%d" % q])
                    for q, hp in enumerate(hps):
                        DVE(lambda e, hp=hp, q=q: e.max(out=tv[:, hp * 16 + 8:hp * 16 + 16], in_=sc2L[q][:]), ["sc2_%d" % q], ["tvb%d" % hp])
                    for q, hp in enumerate(hps):
                        DVE(lambda e, hp=hp, q=q: e.max_index(out=tiu[:, hp * 16 + 8:hp * 16 + 16], in_max=tv[:, hp * 16 + 8:hp * 16 + 16], in_values=sc2L[q][:]),
                            ["sc2_%d" % q, "tvb%d" % hp], ["tiub%d" % hp])
                    yield
                TVALL = ["tv%d" % hp for hp in range(16)] + ["tvb%d" % hp for hp in range(16)]
                TIALL = ["tiu%d" % hp for hp in range(16)] + ["tiub%d" % hp for hp in range(16)]
                DVE(lambda e: e.tensor_copy(out=tif[:], in_=tiu[:]), TIALL, ["tif"])
                tfv = tif[:].rearrange("p (h c k) -> p h c k", c=2, k=16)
                tvv = tv[:].rearrange("p (h c k) -> p h c k", c=2, k=16)
                DVE(lambda e: e.tensor_scalar(out=tfv[:, :, 0, :], in0=tfv[:, :, 0, :], scalar1=128.0, scalar2=None, op0=ALU.mult), ["tif"], ["tif"])
                for hg in range(4):
                    hs_ = [hg * 2, hg * 2 + 1]
                    for q, h in enumerate(hs_):
                        for (dstL, srcv, rres, wn) in ((candL, tvv, TVALL, "cand_%d"), (ciL, tfv, ["tif"], "ci_%d")):
                            dst = dstL[q]
                            DVE(lambda e, h=h, dst=dst, srcv=srcv: e.tensor_tensor(out=v3(dst[:, 0:64], 16), in0=srcv[:, h, 0, 0:4, None].broadcast_to([128, 4, 16]),
                                                                                   in1=srcv[:, h, 1, None, :].broadcast_to([128, 4, 16]), op=ALU.add), rres, [wn % q])
                            DVE(lambda e, h=h, dst=dst, srcv=srcv: e.tensor_tensor(out=v3(dst[:, 64:112], 4), in0=srcv[:, h, 0, 4:16, None].broadcast_to([128, 12, 4]),
                                                                                   in1=srcv[:, h, 1, None, 0:4].broadcast_to([128, 12, 4]), op=ALU.add), rres, [wn % q])
                    for q, h in enumerate(hs_):
                        DVE(lambda e, h=h, q=q: e.max(out=bv[:, h * 16:h * 16 + 8], in_=candL[q][:]), ["cand_%d" % q], ["bv%d" % h])
                    for q, h in enumerate(hs_):
                        DVE(lambda e, h=h, q=q: e.match_replace(out=cand2L[q][:], in_to_replace=bv[:, h * 16:h * 16 + 8], in_values=candL[q][:], imm_value=-1e30),
                            ["cand_%d" % q, "bv%d" % h], ["cand2_%d" % q])
                    for q, h in enumerate(hs_):
                        DVE(lambda e, h=h, q=q: e.max(out=bv[:, h * 16 + 8:h * 16 + 16], in_=cand2L[q][:]), ["cand2_%d" % q], ["bvb%d" % h])
                    yield
                    for j in range(16):
                        for q, h in enumerate(hs_):
                            r = h * 16 + j
                            DVE(lambda e, r=r, q=q: e.scalar_tensor_tensor(out=junkD[:, (r % 8) * 128:(r % 8) * 128 + 112], in0=candL[q][:], scalar=bv[:, r:r + 1], op0=ALU.is_equal,
                                                                           in1=ciL[q][:], op1=ALU.mult, accum_out=idxf[:, r:r + 1]),
                                ["cand_%d" % q, "ci_%d" % q, "bv%d" % h, "bvb%d" % h], ["idxf%d" % r, "jd%d" % (r % 8)])
                        if j % 4 == 3:
                            yield
                BVALL = ["bv%d" % h for h in range(8)] + ["bvb%d" % h for h in range(8)]
                DVE(lambda e: e.tensor_scalar(out=idxf[:], in0=idxf[:], scalar1=16383.0, scalar2=0.0, op0=ALU.min, op1=ALU.max), ["idxf%d" % r for r in range(128)], ["idxf"])
                DVE(lambda e: e.tensor_copy(out=idxi[:], in_=idxf[:]), ["idxf"], ["idxi"])
                DVE(lambda e: e.tensor_scalar(out=negm[:], in0=v3(bv[:], 16)[:, :, 0], scalar1=-1.0, scalar2=None, op0=ALU.mult), BVALL, ["negm"])
                for h in range(8):
                    ACT(lambda e, h=h: e.activation(out=eg[:, h * 16:(h + 1) * 16], in_=bv[:, h * 16:(h + 1) * 16], func=AF.Exp, bias=negm[:, h:h + 1], scale=1.0,
                                                    accum_out=Z[:, h:h + 1]), BVALL + ["negm"], ["eg", "Z"])
                DVE(lambda e: e.reciprocal(out=rZ[:], in_=Z[:]), ["Z"], ["rZ"])
                DVE(lambda e: e.tensor_tensor(out=v3(gate[:], 16), in0=v3(eg[:], 16), in1=rZ[:, :, None].broadcast_to([128, 8, 16]), op=ALU.mult), ["eg", "rZ"], ["gate"])
                if DBG:
                    tap("idx", idxf[:], tok0, "idxf")

            sb_gen = selback()
            gens = [(sb_gen, 3)] + ([(nxt, 2)] if nxt is not None else [])
            while gens:
                for (g_, k_) in list(gens):
                    for _ in range(k_):
                        try:
                            next(g_)
                        except StopIteration:
                            gens.remove((g_, k_))
                            break

            PA = bank(); PB = bank()

            def gather(r):
                slot = r % NG
                p.op("pool", lambda e: e.indirect_dma_start(out=rg[slot], out_offset=None, in_=uvs,
                                                            in_offset=bass.IndirectOffsetOnAxis(ap=idxi[:, r:r + 1], axis=0)),
                     ["idxi"] + UVRES, ["rg%d" % slot, "rgu%d" % slot], dma="gr%d" % slot)

            def dot(r):
                slot = r % NG
                DVE(lambda e: e.scalar_tensor_tensor(out=rg[slot][:, 0:1024], in0=rg[slot][:, 0:1024], scalar=1.0, op0=ALU.mult, in1=XN[:, :], op1=ALU.mult,
                                                     accum_out=hd[:, r:r + 1]), ["rg%d" % slot, "XN"], ["hd%d" % r, "rgu%d" % slot])

            def combine(r):
                slot = r % NG
                ds = r % ND
                ACT(lambda e: e.activation(out=gl[:, r:r + 1], in_=hd[:, r:r + 1], func=AF.Gelu), ["hd%d" % r], ["gl%d" % r])
                DVE(lambda e: e.tensor_scalar(out=Dr[ds][:], in0=identf[:], scalar1=gl[:, r:r + 1], scalar2=gate[:, r:r + 1], op0=ALU.mult, op1=ALU.mult),
                    ["identf", "gl%d" % r, "gate"], ["Dr%d" % ds])

                def f(e):
                    e.matmul(PA[0][:, :], lhsT=Dr[ds][:], rhs=rg[slot][:, 1024:1536], start=(r == 0), stop=(r == NR - 1))
                    return e.matmul(PB[0][:, :], lhsT=Dr[ds][:], rhs=rg[slot][:, 1536:2048], start=(r == 0), stop=(r == NR - 1))
                PE(f, ["Dr%d" % ds, "rg%d" % slot], [PA[1], PB[1]])

            for r in range(min(NG, NR)):
                gather(r)
            dot(0)
            for r in range(NR):
                if r + 1 < NR:
                    dot(r + 1)
                combine(r)
                if r + NG < NR:
                    gather(r + NG)
            DVE(lambda e: e.tensor_tensor(out=xt[:, 0:512], in0=PA[0][:, :], in1=x2[:, 0:512], op=ALU.add), [PA[1], X2], [XT])
            DVE(lambda e: e.tensor_tensor(out=xt[:, 512:1024], in0=PB[0][:, :], in1=x2[:, 512:1024], op=ALU.add), [PB[1], X2], [XT])
            DMA(lambda e: e.dma_start(out=out_d[tok0:tok0 + 128, :], in_=xt[:]), [XT], (), sem="sto%d" % (gi % 2))

        tiles = [(si, ti) for si in range(NSEQ) for ti in range(NT)]
        p.dry = True
        sv = bstate["i"]
        for _ in mixer(0, 1 if NT > 1 else 0):
            pass
        bstate["i"] = sv
        wstate["order"] = list(wstate["rec"]) + [19, 20, 21, 22]
        assert sorted(wstate["order"]) == list(range(NCH)), wstate["order"]
        wstate["use"] = -1
        wstate["lastc"] = None
        p.dry = False
        if STAGE >= 1 and STAGE <= 3:
            for (si, ti) in tiles:
                for _ in mixer(si, ti):
                    pass
        elif STAGE > 3:
            for _ in mixer(*tiles[0]):
                pass
            for k, (si, ti) in enumerate(tiles):
                nxt = mixer(*tiles[k + 1]) if k + 1 < len(tiles) else None
                peer(si, ti, nxt)
        if STAGE == 0:
            DMA(lambda e: e.dma_start(out=xts[0][:, 0:128], in_=wscr[22][:, 0:256].bitcast(F32)), ["wscr22"], ["xt0"], sem="ldx0")
            DMA(lambda e: e.dma_start(out=out_d[0:128, 0:128], in_=xts[0][:, 0:128]), ["xt0"], (), sem="sto")
        evs = [(p.dsem[k], p.dcnt[k]) for k in p.dsem if k.startswith("sto") or k.startswith("dbg_")]
        p.final_wait("sp", evs)
        p.emit()
    return nc


def host_consts():
    ident = np.eye(128, dtype=np.float32)
    s = np.arange(128)[:, None]
    t = np.arange(128)[None, :]
    same = (s // 64) == (t // 64)
    tri = np.where(same & (s <= t), -1.0 / 16.0, 0.0).astype(np.float32)
    tri2 = np.where(same & (s > t), -1.0 / 16.0, 0.0).astype(np.float32)
    mgla = np.where(same & (s <= t), 1.0, 0.0).astype(np.float32)
    mown = np.where(s <= t, 1.0, 0.0).astype(np.float32)
    mprev = np.where(s > t, 1.0, 0.0).astype(np.float32)
    cmat = np.stack([ident, tri, tri2, mgla, mown]).astype(np.float32)
    cm = np.stack([(np.arange(128) < 64), (np.arange(128) >= 64)], axis=1).astype(np.float32)
    pos = np.arange(2048, dtype=np.float32)
    inv_freq = (np.float32(500000.0) ** (-np.arange(0, 16, 2, dtype=np.float32) / np.float32(16))).astype(np.float32)
    ang = (pos[:, None] * inv_freq[None, :]).astype(np.float32)
    cos = np.cos(ang).astype(np.float32).reshape(16, 128, 8).transpose(1, 0, 2).reshape(128, 128)
    sin = np.sin(ang).astype(np.float32).reshape(16, 128, 8).transpose(1, 0, 2).reshape(128, 128)
    return dict(cmat=cmat, mprev=mprev, cm=cm, rcos=np.ascontiguousarray(cos), rsin=np.ascontiguousarray(sin))


def make_in_maps(inputs, n_cores, NSEQ, NT):
    f = lambda a: np.ascontiguousarray(np.asarray(a, dtype=np.float32))
    x = f(inputs["x"])
    seq = NT * 128
    xs = x.reshape(-1, seq, 1024)
    assert xs.shape[0] == n_cores * NSEQ
    shared = dict(
        w_in=f(inputs["w_in"][0]), w_branch_a=f(inputs["w_branch_a"][0]), w_branch_b=f(inputs["w_branch_b"][0]),
        w_out=f(inputs["w_out"][0]), w_peer_q=f(inputs["w_peer_q"][0]),
        sub_keys=f(inputs["peer_sub_keys"][0]).reshape(16, 128, 128),
        peer_u=f(inputs["peer_u"][0]), peer_v=f(inputs["peer_v"][0]),
        gmix_pk=np.ascontiguousarray(f(inputs["norm_mix_g"][0]).reshape(8, 128).T),
        gffn_pk=np.ascontiguousarray(f(inputs["norm_ffn_g"][0]).reshape(8, 128).T),
        gffn_row=f(inputs["norm_ffn_g"][0]).reshape(1, 1024),
        ga_row=f(inputs["gla_norm_g"][0]).reshape(1, 256),
        gq_row=f(inputs["q_norm_g"][0]).reshape(1, 64),
        gk_row=f(inputs["k_norm_g"][0]).reshape(1, 64),
        sinks_row=f(inputs["attn_sinks"][0]).reshape(1, 16),
        w_gk2=f(inputs["w_gk2"][0]), b_gk=f(inputs["b_gk"][0]).reshape(1, 512),
    )
    shared.update(host_consts())
    maps = []
    for c in range(n_cores):
        m = dict(shared)
        m["x"] = np.ascontiguousarray(xs[c * NSEQ:(c + 1) * NSEQ].reshape(NSEQ * seq, 1024))
        maps.append(m)
    return maps


def kernel(**inputs):
    n = 8
    NSEQ, NT = 2, 16
    nc = build_nc(NSEQ, NT)
    in_maps = make_in_maps(inputs, n, NSEQ, NT)
    res = run_bass_kernel_spmd(nc, in_maps, core_ids=list(range(n)))
    out = np.concatenate([np.asarray(r["out"]).reshape(NSEQ, NT * 128, 1024) for r in res.results], axis=0)
    return out.astype(np.float32)
```

```python
import numpy as np
import concourse.bass as bass
import concourse.mybir as mybir
from concourse.bass_utils import run_bass_kernel_spmd
from contextlib import ExitStack

F32 = mybir.dt.float32
BF16 = mybir.dt.bfloat16
U32 = mybir.dt.uint32
I32 = mybir.dt.int32
AF = mybir.ActivationFunctionType
ALU = mybir.AluOpType
AX = mybir.AxisListType

EPS = 1e-6
W_IN_COLS = [0, 512, 1024, 1536, 2048, 2560, 3088, 3600, 4112, 4624, 5136, 5648, 6160]
NCH = 23
NSLOT = 3
NG = 7
ND = 4


class Prog:
    ENGS = ("pe", "dve", "act", "pool", "sp")

    def __init__(self, nc, es):
        self.nc = nc
        self.es = es
        self.q = {e: [] for e in self.ENGS}
        self.sem = {e: es.enter_context(nc.semaphore("sem_" + e)) for e in self.ENGS}
        self.semeng = {id(self.sem[e]): e for e in self.ENGS}
        self.cnt = {e: 0 for e in self.ENGS}
        self.waited = {e: {} for e in self.ENGS}
        self.lastw = {}
        self.readers = {}
        self.dsem = {}
        self.dcnt = {}

    def op(self, eng, fn, reads=(), writes=(), dma=None):
        if getattr(self, "dry", False):
            return None
        deps = []
        for r in reads:
            if r in self.lastw:
                deps.append(self.lastw[r])
        for w in writes:
            if w in self.lastw:
                deps.append(self.lastw[w])
            deps.extend(self.readers.get(w, []))
        wd = self.waited[eng]
        best = {}
        for (s, v) in deps:
            key = id(s)
            if eng == "pe" and self.semeng.get(key) == "pe":
                continue
            if wd.get(key, 0) >= v:
                continue
            if key not in best or best[key][1] < v:
                best[key] = (s, v)
        waits = []
        for key, (s, v) in best.items():
            wd[key] = v
            waits.append((s, v))
        if dma is None:
            self.cnt[eng] += 1
            ev = (self.sem[eng], self.cnt[eng])
            inc = 1
        else:
            if dma not in self.dsem:
                self.dsem[dma] = self.es.enter_context(self.nc.semaphore("d_" + dma))
                self.dcnt[dma] = 0
            self.dcnt[dma] += 16
            ev = (self.dsem[dma], self.dcnt[dma])
            inc = 16
        self.q[eng].append((waits, fn, ev[0], inc))
        for w in writes:
            self.lastw[w] = ev
            self.readers[w] = []
        for r in reads:
            if r not in writes:
                self.readers.setdefault(r, []).append(ev)
        return ev

    def final_wait(self, eng, evs):
        self.q[eng].append((list(evs), None, None, 0))

    def emit(self):
        nc = self.nc
        with nc.Block() as block:
            def mk(ename):
                def body(e):
                    for (waits, fn, s, inc) in self.q[ename]:
                        for (ws, wv) in waits:
                            e.wait_ge(ws, wv)
                        if fn is not None:
                            ins = fn(e)
                            ins.then_inc(s, inc)
                return body
            block.tensor(mk("pe"))
            block.vector(mk("dve"))
            block.scalar(mk("act"))
            block.gpsimd(mk("pool"))
            block.sync(mk("sp"))


def build_nc(NSEQ=2, NT=16, DBG=False, NR=128, STAGE=99):
    NTOK = NSEQ * NT * 128
    nc = bass.Bass("TRN2", target_bir_lowering=False)
    es = ExitStack()

    def din(name, shape, dt=F32):
        return nc.dram_tensor(name, list(shape), dt, kind="ExternalInput").ap()

    x_d = din("x", [NTOK, 1024])
    w_in_d = din("w_in", [1024, 6672])
    wa_d = din("w_branch_a", [1024, 1024])
    wb_d = din("w_branch_b", [1024, 1024])
    wo_d = din("w_out", [1024, 1024])
    wpq_d = din("w_peer_q", [1024, 2048])
    sk_d = din("sub_keys", [16, 128, 128])
    pu_d = din("peer_u", [16384, 1024])
    pv_d = din("peer_v", [16384, 1024])
    gmix_d = din("gmix_pk", [128, 8])
    gffn_d = din("gffn_pk", [128, 8])
    gffn_row_d = din("gffn_row", [1, 1024])
    ga_d = din("ga_row", [1, 256])
    gq_d = din("gq_row", [1, 64])
    gk_d = din("gk_row", [1, 64])
    sinks_d = din("sinks_row", [1, 16])
    wgk2_d = din("w_gk2", [16, 512])
    bgk_d = din("b_gk", [1, 512])
    cmat_d = din("cmat", [5, 128, 128])
    mprev_d = din("mprev", [128, 128])
    cm_d = din("cm", [128, 2])
    cos_d = din("rcos", [128, 16 * 8])
    sin_d = din("rsin", [128, 16 * 8])
    out_d = nc.dram_tensor("out", [NTOK, 1024], F32, kind="ExternalOutput").ap()
    wscr = nc.dram_tensor("wscr", [NCH, 128, 4096], BF16, kind="Internal").ap()
    uvs = nc.dram_tensor("uvscr", [16384, 2048], BF16, kind="Internal").ap()
    dbg_d = {}
    if DBG:
        for nm in ("x2", "ya", "yb", "oa", "ob"):
            dbg_d[nm] = nc.dram_tensor("dbg_" + nm, [NTOK, 1024], F32, kind="ExternalOutput").ap()
        dbg_d["idx"] = nc.dram_tensor("dbg_idx", [NTOK, 128], F32, kind="ExternalOutput").ap()
        dbg_d["wgt"] = nc.dram_tensor("dbg_wgt", [NTOK, 128], F32, kind="ExternalOutput").ap()

    with es:
        p = Prog(nc, es)
        p.dry = False

        def sb(name, shape, dt=F32):
            return es.enter_context(nc.sbuf_tensor("s_" + name, list(shape), dt))

        identf = sb("identf", [128, 128]); identb = sb("identb", [128, 128], BF16)
        tri = sb("tri", [128, 128]); tri2 = sb("tri2", [128, 128])
        mgla = sb("mgla", [128, 128]); mk2 = sb("mk2", [128, 512])
        cm = sb("cm", [128, 2])
        rcos = sb("rcos", [128, 128]); rsin = sb("rsin", [128, 128])
        gmix = sb("gmix", [128, 8]); gffn = sb("gffn", [128, 8])
        gffn_rep = sb("gffn_rep", [128, 1024])
        ga_rep = sb("ga_rep", [128, 1024]); gq_rep = sb("gq_rep", [128, 1024]); gk_rep = sb("gk_rep", [128, 256])
        g256 = sb("g256", [128, 256]); g64q = sb("g64q", [128, 64]); g64k = sb("g64k", [128, 64])
        esink = sb("esink", [128, 16])
        wgk = sb("wgk", [17, 512]); glra = sb("glra", [17, 128])
        wglr_f = sb("wglr_f", [128, 128]); wglr = sb("wglr", [128, 128], BF16)
        skT = sb("skT", [128, 2048], BF16)
        ones_bf = sb("ones_bf", [128, 2], BF16)
        wring = [sb("wring%d" % i, [128, 4096], BF16) for i in range(NSLOT)]
        FS = sb("FS", [128, 5120])
        xts = [sb("xt%d" % i, [128, 1024]) for i in range(2)]; x2s = [sb("x2_%d" % i, [128, 1024]) for i in range(2)]
        junkA = sb("junkA", [128, 1024], BF16); junkD = sb("junkD", [128, 1024], BF16)
        qTs = sb("qTs", [128, 512]); kTs = sb("kTs", [128, 512]); ks = sb("ks", [128, 512])
        Lb = sb("Lb", [128, 512]); bTs = sb("bTs", [128, 512])
        E1 = sb("E1", [128, 512], BF16); E2 = sb("E2", [128, 512], BF16); E3 = sb("E3", [128, 512], BF16); Er = sb("Er", [128, 512], BF16)
        qtT = sb("qtT", [128, 512], BF16); ktT = sb("ktT", [128, 512], BF16)
        qi0 = sb("qi0", [128, 512], BF16); qi1 = sb("qi1", [128, 512], BF16)
        ke0 = sb("ke0", [128, 512], BF16); ke1 = sb("ke1", [128, 512], BF16)
        attm = sb("attm", [128, 512], BF16)
        vb = sb("vb", [128, 1024], BF16)
        S = sb("S", [128, 1024]); S1 = sb("S1", [128, 1024])
        Sb = sb("Sb", [128, 1024], BF16); S1b = sb("S1b", [128, 1024], BF16)
        nbref = sb("nbref", [128, 8]); pbref = sb("pbref", [128, 8]); dec = sb("dec", [128, 8])
        trA = sb("trA", [128, 1024], BF16); trB = sb("trB", [128, 1024], BF16)
        trC = sb("trC", [128, 1024], BF16); trD = sb("trD", [128, 1024], BF16)
        tmA = sb("tmA", [128, 1024], BF16); tmB = sb("tmB", [128, 1024], BF16)
        kq = sb("kq", [128, 256]); kn = sb("kn", [128, 256]); k2 = sb("k2", [128, 512], BF16)
        kT2 = [sb("kT2_%d" % i, [128, 512], BF16) for i in range(2)]
        vsw = [sb("vsw_%d" % i, [128, 256], BF16) for i in range(2)]
        rt = [sb("rt%d" % i, [128, 128]) for i in range(4)]
        praw = [sb("praw%d" % i, [128, 512], BF16) for i in range(2)]
        pTe = [sb("pTe%d" % i, [128, 512], BF16) for i in range(4)]
        pTo = [sb("pTo%d" % i, [128, 512], BF16) for i in range(4)]
        ss = sb("ss", [128, 2]); rs = sb("rs", [128, 2])
        ssqa = sb("ssqa", [128, 4]); rsa = sb("rsa", [128, 4])
        ssqq = sb("ssqq", [128, 16]); rsq = sb("rsq", [128, 16])
        ssqk = sb("ssqk", [128, 4]); rsk = sb("rsk", [128, 4])
        den = sb("den", [128, 16]); rden = sb("rden", [128, 16])
        qpT = sb("qpT", [128, 2048], BF16)
        sc2L = [sb("sc2_%d" % i, [128, 128]) for i in range(4)]
        tv = sb("tv", [128, 256]); tiu = sb("tiu", [128, 256], U32); tif = sb("tif", [128, 256])
        candL = [sb("cand_%d" % i, [128, 112]) for i in range(2)]; cand2L = [sb("cand2_%d" % i, [128, 112]) for i in range(2)]
        ciL = [sb("ci_%d" % i, [128, 112]) for i in range(2)]
        bv = sb("bv", [128, 128]); idxf = sb("idxf", [128, 128]); idxi = sb("idxi", [128, 128], I32)
        negm = sb("negm", [128, 8]); Z = sb("Z", [128, 8]); rZ = sb("rZ", [128, 8])
        eg = sb("eg", [128, 128]); gate = sb("gate", [128, 128])
        hd = sb("hd", [128, 128]); gl = sb("gl", [128, 128]); wgt = sb("wgt", [128, 128])
        Dr = [sb("Dr%d" % i, [128, 128], BF16) for i in range(ND)]
        RG = sb("RG", [128, NG * 2048], BF16)
        rg = [RG[:, i * 2048:(i + 1) * 2048] for i in range(NG)]

        def F(i, n=1):
            return FS[:, i * 1024:(i + n) * 1024]

        Tps = es.enter_context(nc.psum_tensor("Tps", [128, 1024], BF16))
        XN = es.enter_context(nc.psum_tensor("XN", [128, 1024], F32))
        NBK = 5
        banks = [es.enter_context(nc.psum_tensor("bk%d" % i, [128, 512], F32)) for i in range(NBK)]
        bstate = {"i": 0}

        def bank():
            i = bstate["i"]
            bstate["i"] = (i + 1) % NBK
            return banks[i], "bk%d" % i

        def PE(fn, r=(), w=()): return p.op("pe", fn, r, w)
        def DVE(fn, r=(), w=()): return p.op("dve", fn, r, w)
        def ACT(fn, r=(), w=()): return p.op("act", fn, r, w)
        def POOL(fn, r=(), w=()): return p.op("pool", fn, r, w)
        def DMA(fn, r=(), w=(), sem=None): return p.op("sp", fn, r, w, dma=sem)

        ldc = {"n": 0, "res": []}

        def load(out_ap, in_ap, wres):
            ldc["n"] += 1
            ldc["res"].append(wres)
            return DMA(lambda e: e.dma_start(out=out_ap, in_=in_ap), (), [wres], sem="ldc")

        def load_barrier():
            ev = (p.dsem["ldc"], p.dcnt["ldc"])
            for r in ldc["res"]:
                p.lastw[r] = ev
            ldc["res"] = []

        def transposes(src, nblk, rres, dst, wres, eng="act"):
            def f(e):
                ins = None
                for kc in range(nblk):
                    ins = e.transpose(out=Tps[:, kc * 128:(kc + 1) * 128], in_=src[:, kc * 128:(kc + 1) * 128], identity=identb[:])
                return ins
            PE(f, [rres, "identb"], ["T"])
            ACT(lambda e: e.copy(out=dst[:, 0:nblk * 128], in_=Tps[:, 0:nblk * 128]), ["T"], [wres])

        load(identf[:], cmat_d[0], "identf"); load(tri[:], cmat_d[1], "tri"); load(tri2[:], cmat_d[2], "tri2")
        load(mgla[:], cmat_d[3], "mgla")
        load(mk2[:, 0:128], mprev_d, "mk2"); load(mk2[:, 128:256], mprev_d, "mk2")
        load(mk2[:, 256:384], cmat_d[4], "mk2"); load(mk2[:, 384:512], cmat_d[4], "mk2")
        load(cm[:], cm_d, "cm"); load(rcos[:], cos_d, "rcos"); load(rsin[:], sin_d, "rsin")
        load(gmix[:], gmix_d, "gmix"); load(gffn[:], gffn_d, "gffn")
        load(gffn_rep[:], gffn_row_d.broadcast_to([128, 1024]), "gffn_rep")
        load(g256[:], ga_d.broadcast_to([128, 256]), "g256")
        load(g64q[:], gq_d.broadcast_to([128, 64]), "g64q"); load(g64k[:], gk_d.broadcast_to([128, 64]), "g64k")
        load(esink[:], sinks_d.broadcast_to([128, 16]), "esink")
        load(wgk[0:16, :], wgk2_d, "wgk"); load(wgk[16:17, :], bgk_d, "wgk")
        load(wglr_f[:].rearrange("p (k n) -> p k n", n=16),
             w_in_d.rearrange("(k p) n -> p k n", p=128)[:, :, 3072:3088], "wglr_f")
        for q4 in range(4):
            load(F(0, 2)[:, q4 * 512:(q4 + 1) * 512].rearrange("p (a d) -> p a d", d=128),
                 sk_d[q4 * 4:(q4 + 1) * 4].rearrange("a k d -> k a d"), "skl%d" % q4)
        load_barrier()
        DVE(lambda e: e.tensor_copy(out=identb[:], in_=identf[:]), ["identf"], ["identb"])
        DVE(lambda e: e.tensor_copy(out=ga_rep[:].rearrange("p (h e) -> p h e", e=256),
                                    in_=g256[:, None, :].broadcast_to([128, 4, 256])), ["g256"], ["ga_rep"])
        DVE(lambda e: e.tensor_scalar(out=gq_rep[:].rearrange("p (h e) -> p h e", e=64),
                                      in0=g64q[:, None, :].broadcast_to([128, 16, 64]), scalar1=0.125, scalar2=None, op0=ALU.mult),
            ["g64q"], ["gq_rep"])
        DVE(lambda e: e.tensor_copy(out=gk_rep[:].rearrange("p (h e) -> p h e", e=64),
                                    in_=g64k[:, None, :].broadcast_to([128, 4, 64])), ["g64k"], ["gk_rep"])
        ACT(lambda e: e.activation(out=esink[:], in_=esink[:], func=AF.Exp), ["esink"], ["esink"])
        POOL(lambda e: e.memset(glra[:], 1.0), (), ["glra"])
        POOL(lambda e: e.memset(idxi[:], 0), (), ["idxi", "idxi_e"])
        POOL(lambda e: e.memset(qi0[:], 0.0), (), ["qi0"])
        POOL(lambda e: e.memset(qi1[:], 0.0), (), ["qi1"])
        POOL(lambda e: e.memset(ones_bf[:], 1.0), (), ["ones_bf"])
        DVE(lambda e: e.tensor_tensor(out=wglr[:].rearrange("p (k n) -> p k n", n=16),
                                      in0=wglr_f[:].rearrange("p (k n) -> p k n", n=16),
                                      in1=gmix[:, :, None].broadcast_to([128, 8, 16]), op=ALU.mult),
            ["wglr_f", "gmix"], ["wglr"])
        for q4 in range(4):
            bk, bres = bank()

            def f(e, q4=q4, bk=bk):
                ins = None
                for a in range(4):
                    hp = q4 * 4 + a
                    ins = e.transpose(out=bk[:, a * 128:(a + 1) * 128], in_=F(0, 2)[:, hp * 128:(hp + 1) * 128], identity=identf[:])
                return ins
            PE(f, ["skl%d" % q4, "F0", "F1", "identf"], [bres])
            ACT(lambda e, q4=q4, bk=bk: e.copy(out=skT[:, q4 * 512:(q4 + 1) * 512], in_=bk[:, :]), [bres], ["skT"])

        def wsrc(c):
            if c < 13:
                return w_in_d, W_IN_COLS[c], gmix
            if c < 15:
                return wa_d, (c - 13) * 512, None
            if c < 17:
                return wb_d, (c - 15) * 512, None
            if c < 19:
                return wo_d, (c - 17) * 512, None
            return wpq_d, (c - 19) * 512, gffn

        hcount = 0
        for c in range(NCH):
            src, col, g = wsrc(c)
            slot = c % NSLOT
            wr = wring[slot]
            for half in range(2):
                st_i = hcount % 2
                hcount += 1
                stg = F(st_i * 2, 2)
                sres = ["F%d" % (st_i * 2), "F%d" % (st_i * 2 + 1)]
                srcap = src.rearrange("(k p) n -> p k n", p=128)[:, half * 4:(half + 1) * 4, col:col + 512]
                DMA(lambda e, stg=stg, srcap=srcap: e.dma_start(out=stg.rearrange("p (k n) -> p k n", n=512), in_=srcap),
                    (), sres, sem="stg%d" % st_i)
                for k4 in range(4):
                    kc = half * 4 + k4
                    o_ap = wr[:, kc * 512:(kc + 1) * 512]
                    i_ap = stg[:, k4 * 512:(k4 + 1) * 512]
                    if g is None:
                        if k4 % 2 == 0:
                            DVE(lambda e, o_ap=o_ap, i_ap=i_ap: e.tensor_copy(out=o_ap, in_=i_ap), sres, ["W%dk%d" % (slot, kc)])
                        else:
                            ACT(lambda e, o_ap=o_ap, i_ap=i_ap: e.copy(out=o_ap, in_=i_ap), sres, ["W%dk%d" % (slot, kc)])
                    else:
                        gs = g[:, kc:kc + 1]
                        if k4 % 2 == 0:
                            DVE(lambda e, o_ap=o_ap, i_ap=i_ap, gs=gs: e.tensor_scalar(out=o_ap, in0=i_ap, scalar1=gs, scalar2=None, op0=ALU.mult),
                                sres + ["gmix", "gffn"], ["W%dk%d" % (slot, kc)])
                        else:
                            ACT(lambda e, o_ap=o_ap, i_ap=i_ap, gs=gs: e.activation(out=o_ap, in_=i_ap, func=AF.Copy, scale=gs),
                                sres + ["gmix", "gffn"], ["W%dk%d" % (slot, kc)])
            DMA(lambda e, c=c, wr=wr: e.dma_start(out=wscr[c], in_=wr[:]), ["W%d" % slot] + ["W%dk%d" % (slot, k) for k in range(8)], ["wscr%d" % c], sem="wst%d" % slot)

        UVRES = ["ccs%d" % i for i in range(4)]
        ci_ = 0
        for q8 in range(16):
            r0, r1 = q8 * 1024, (q8 + 1) * 1024
            for (src_d, c0) in ((pu_d, 0), (pv_d, 1024)):
                p.op("pool", lambda e, r0=r0, r1=r1, src_d=src_d, c0=c0: e.dma_start(out=uvs[r0:r1, c0:c0 + 1024], in_=src_d[r0:r1, :]),
                     ["wscr%d" % c_ for c_ in range(8)], ["ccs%d" % (ci_ % 4)], dma="cc%d" % (ci_ % 4))
                ci_ += 1
        total_uses = NSEQ * NT * NCH
        wstate = {"issued": 0, "use": -1, "lastc": None, "rec": [], "order": None}

        def wget(n):
            c = n % NCH
            if wstate["lastc"] != c:
                wstate["use"] += 1
                wstate["lastc"] = c
                if p.dry:
                    wstate["rec"].append(c)
                else:
                    assert wstate["order"][wstate["use"] % NCH] == c, (wstate["use"], c)
            if p.dry:
                return wring[0], "W0"
            u = wstate["use"]
            while wstate["issued"] < min(total_uses, u + NSLOT):
                m = wstate["issued"]
                cid = wstate["order"][m % NCH]
                slot = m % NSLOT
                DMA(lambda e, cid=cid, slot=slot: e.dma_start(out=wring[slot][:], in_=wscr[cid]), ["wscr%d" % cid], ["W%d" % slot], sem="wld%d" % slot)
                wstate["issued"] += 1
            slot = u % NSLOT
            return wring[slot], "W%d" % slot

        def proj_tok(n, lhs, lres, bk, bres):
            W, wres = wget(n)

            def f(e):
                ins = None
                for kc in range(8):
                    ins = e.matmul(bk[:, :], lhsT=lhs[:, kc * 128:(kc + 1) * 128], rhs=W[:, kc * 512:(kc + 1) * 512], start=(kc == 0), stop=(kc == 7))
                return ins
            PE(f, [lres, wres], [bres])

        def proj_feat(n, rhs, rres, bk, bres):
            W, wres = wget(n)

            def f(e):
                ins = None
                for j in range(4):
                    for kc in range(8):
                        ins = e.matmul(bk[:, j * 128:(j + 1) * 128], lhsT=W[:, kc * 512 + j * 128: kc * 512 + (j + 1) * 128],
                                       rhs=rhs[:, kc * 128:(kc + 1) * 128], start=(kc == 0), stop=(kc == 7))
                return ins
            PE(f, [rres, wres], [bres])

        def tap(name, src, tok0, rres, ncols=1024):
            if DBG:
                DMA(lambda e: e.dma_start(out=dbg_d[name][tok0:tok0 + 128, :], in_=src), [rres], (), sem="dbg_" + name)

        def v3(ap, inner):
            return ap.rearrange("p (a b) -> p a b", b=inner)

        def rope(buf, bres, nh, ti):
            bvw = v3(buf, 64)
            x1 = bvw[:, :, 0:8]
            x2_ = bvw[:, :, 8:16]
            cs = rcos[:, None, ti * 8:(ti + 1) * 8].broadcast_to([128, nh, 8])
            sn = rsin[:, None, ti * 8:(ti + 1) * 8].broadcast_to([128, nh, 8])
            t = [v3(rt[i][:, 0:nh * 8], 8) for i in range(4)]
            DVE(lambda e: e.tensor_tensor(out=t[0], in0=x1, in1=cs, op=ALU.mult), [bres, "rcos"], ["rt0"])
            DVE(lambda e: e.tensor_tensor(out=t[1], in0=x2_, in1=sn, op=ALU.mult), [bres, "rsin"], ["rt1"])
            DVE(lambda e: e.tensor_tensor(out=t[2], in0=x2_, in1=cs, op=ALU.mult), [bres, "rcos"], ["rt2"])
            DVE(lambda e: e.tensor_tensor(out=t[3], in0=x1, in1=sn, op=ALU.mult), [bres, "rsin"], ["rt3"])
            DVE(lambda e: e.tensor_tensor(out=x1, in0=t[0], in1=t[1], op=ALU.subtract), ["rt0", "rt1", bres], [bres])
            DVE(lambda e: e.tensor_tensor(out=x2_, in0=t[2], in1=t[3], op=ALU.add), ["rt2", "rt3", bres], [bres])

        def rstd_op(ssq_ap, out_ap, n, rres, wres):
            ACT(lambda e: e.activation(out=out_ap, in_=ssq_ap, func=AF.Sqrt, bias=EPS, scale=1.0 / n), [rres], [wres])
            DVE(lambda e: e.reciprocal(out=out_ap, in_=out_ap), [wres], [wres])

        SCALE = 128.0 ** -0.5

        def mixer(si, ti):
            gi = si * NT + ti
            tok0 = gi * 128
            n0 = gi * NCH
            par = ti % 2
            xt = xts[gi % 2]; x2 = x2s[gi % 2]
            XT = "xt%d" % (gi % 2); X2 = "x2_%d" % (gi % 2)
            A4, B4, C4, D4, E4 = F(0), F(1), F(2), F(3), F(4)
            PL = DVE if gi == 0 else POOL

            if ti == 0:
                PL(lambda e: e.memset(S[:], 0.0), (), ["S"])
                PL(lambda e: e.memset(Sb[:], 0.0), (), ["Sb"])
            DMA(lambda e: e.dma_start(out=xt[:], in_=x_d[tok0:tok0 + 128, :]), (), [XT], sem="ldx%d" % (gi % 2))
            ACT(lambda e: e.activation(out=junkA[:], in_=xt[:], func=AF.Square, accum_out=ss[:, 0:1]), [XT], ["ss0", "junkA"])
            rstd_op(ss[:, 0:1], rs[:, 0:1], 1024, "ss0", "rs0")
            DVE(lambda e: e.tensor_scalar(out=tmA[:], in0=xt[:], scalar1=rs[:, 0:1], scalar2=None, op0=ALU.mult), [XT, "rs0"], ["tmA"])
            transposes(tmA, 8, "tmA", trA, "trA")
            hT = trA
            yield

            def gla():
                bk, br = bank(); proj_feat(n0 + 0, hT, "trA", bk, br)
                ACT(lambda e, bk=bk: e.copy(out=qTs[:], in_=bk[:, :]), [br], ["qTs"])
                yield
                bk, br = bank(); proj_feat(n0 + 1, hT, "trA", bk, br)
                ACT(lambda e, bk=bk: e.copy(out=kTs[:], in_=bk[:, :]), [br], ["kTs"])
                bk, br = bank(); proj_tok(n0 + 1, hT, "trA", bk, br)
                ACT(lambda e, bk=bk: e.copy(out=ks[:], in_=bk[:, :]), [br], ["ks"])
                yield
                bk, br = bank()

                def f(e, bk=bk):
                    ins = None
                    for kc in range(8):
                        ins = e.matmul(bk[0:16, 0:128], lhsT=wglr[:, kc * 16:(kc + 1) * 16], rhs=hT[:, kc * 128:(kc + 1) * 128], start=(kc == 0), stop=(kc == 7))
                    return ins
                PE(f, ["trA", "wglr"], [br])
                ACT(lambda e, bk=bk: e.copy(out=glra[0:16, :], in_=bk[0:16, 0:128]), [br], ["glra"])
                bk, br = bank()
                PE(lambda e, bk=bk: e.matmul(bk[:, :], lhsT=glra[0:17, :], rhs=wgk[0:17, :], start=True, stop=True), ["glra", "wgk"], [br])
                ACT(lambda e, bk=bk: e.activation(out=Lb[:], in_=bk[:, :], func=AF.Exp, scale=-1.0), [br], ["Lb"])
                ACT(lambda e: e.activation(out=Lb[:], in_=Lb[:], func=AF.Ln, bias=1.0), ["Lb"], ["Lb"])
                bkA, brA = bank()

                def f(e, bkA=bkA):
                    ins = None
                    for h in range(4):
                        ins = e.matmul(bkA[:, h * 128:(h + 1) * 128], lhsT=Lb[:, h * 128:(h + 1) * 128], rhs=tri[:], start=True, stop=True)
                    return ins
                PE(f, ["Lb", "tri"], [brA])
                bkB, brB = bank()
                PE(lambda e, bkB=bkB: e.matmul(bkB[:, :], lhsT=tri2[:], rhs=Lb[:], start=True, stop=True), ["Lb", "tri2"], [brB])
                ACT(lambda e, bkA=bkA: e.copy(out=bTs[:], in_=bkA[:, :]), [brA], ["bTs"])
                ACT(lambda e, bkB=bkB: e.activation(out=Er[:], in_=bkB[:, :], func=AF.Exp), [brB], ["Er"])
                yield
                bTv = v3(bTs[:], 64)
                DVE(lambda e: e.tensor_scalar(out=nbref[:], in0=bTv[:, :, 32], scalar1=-1.0, scalar2=None, op0=ALU.mult), ["bTs"], ["nbref"])
                DVE(lambda e: e.tensor_copy(out=pbref[:], in_=bTv[:, :, 32]), ["bTs"], ["pbref"])
                ACT(lambda e: e.activation(out=dec[:], in_=bTv[:, :, 63], func=AF.Exp), ["bTs"], ["dec"])
                for g in range(8):
                    ACT(lambda e, g=g: e.activation(out=E1[:, g * 64:(g + 1) * 64], in_=bTs[:, g * 64:(g + 1) * 64], func=AF.Exp,
                                                    bias=nbref[:, g:g + 1], scale=1.0), ["bTs", "nbref"], ["E1"])
                for g in range(8):
                    ACT(lambda e, g=g: e.activation(out=E2[:, g * 64:(g + 1) * 64], in_=bTs[:, g * 64:(g + 1) * 64], func=AF.Exp,
                                                    bias=pbref[:, g:g + 1], scale=-1.0), ["bTs", "pbref"], ["E2"])
                ACT(lambda e: e.activation(out=E3[:], in_=bTs[:], func=AF.Exp), ["bTs"], ["E3"])
                yield
                DVE(lambda e: e.scalar_tensor_tensor(out=qtT[:], in0=qTs[:], scalar=SCALE, op0=ALU.mult, in1=E1[:], op1=ALU.mult), ["qTs", "E1"], ["qtT"])
                DVE(lambda e: e.tensor_tensor(out=ktT[:], in0=kTs[:], in1=E2[:], op=ALU.mult), ["kTs", "E2"], ["ktT"])
                DVE(lambda e: e.scalar_tensor_tensor(out=v3(qi0[:], 128)[:, :, 0:64], in0=v3(qTs[:], 128)[:, :, 0:64], scalar=SCALE, op0=ALU.mult,
                                                     in1=v3(E3[:], 128)[:, :, 0:64], op1=ALU.mult), ["qTs", "E3"], ["qi0"])
                DVE(lambda e: e.scalar_tensor_tensor(out=v3(qi1[:], 128)[:, :, 64:128], in0=v3(qTs[:], 128)[:, :, 64:128], scalar=SCALE, op0=ALU.mult,
                                                     in1=v3(E3[:], 128)[:, :, 64:128], op1=ALU.mult), ["qTs", "E3"], ["qi1"])
                DVE(lambda e: e.scalar_tensor_tensor(out=ke0[:], in0=ks[:], scalar=cm[:, 0:1], op0=ALU.mult, in1=Er[:], op1=ALU.mult), ["ks", "Er", "cm"], ["ke0"])
                DVE(lambda e: e.scalar_tensor_tensor(out=ke1[:], in0=ks[:], scalar=cm[:, 1:2], op0=ALU.mult, in1=Er[:], op1=ALU.mult), ["ks", "Er", "cm"], ["ke1"])
                yield
                for hf in range(2):
                    bk, br = bank(); proj_tok(n0 + 2 + hf, hT, "trA", bk, br)
                    ACT(lambda e, bk=bk, hf=hf: e.copy(out=vb[:, hf * 512:(hf + 1) * 512], in_=bk[:, :]), [br], ["vb"])
                    yield
                for hf in range(2):
                    bk, br = bank(); proj_tok(n0 + 4 + hf, hT, "trA", bk, br)
                    ACT(lambda e, bk=bk, hf=hf: e.activation(out=A4[:, hf * 512:(hf + 1) * 512], in_=bk[:, :], func=AF.Sigmoid), [br], ["F0"])
                    DVE(lambda e, bk=bk, hf=hf: e.tensor_tensor(out=B4[:, hf * 512:(hf + 1) * 512], in0=bk[:, :], in1=A4[:, hf * 512:(hf + 1) * 512], op=ALU.mult),
                        [br, "F0"], ["F1"])
                    yield
                PL(lambda e: e.tensor_tensor(out=B4, in0=B4, in1=ga_rep[:], op=ALU.mult), ["F1", "ga_rep"], ["F1"])
                bkC, brC = bank()

                def f(e, bkC=bkC):
                    ins = None
                    for h in range(4):
                        ins = e.matmul(bkC[:, h * 128:(h + 1) * 128], lhsT=ktT[:, h * 128:(h + 1) * 128], rhs=qtT[:, h * 128:(h + 1) * 128], start=True, stop=True)
                    return ins
                PE(f, ["ktT", "qtT"], [brC])
                DVE(lambda e, bkC=bkC: e.tensor_tensor(out=v3(attm[:], 128), in0=v3(bkC[:, :], 128), in1=mgla[:, None, :].broadcast_to([128, 4, 128]), op=ALU.mult),
                    [brC, "mgla"], ["attm"])

                def dstate(ke, keres):
                    bks = [bank(), bank()]

                    def f(e):
                        ins = None
                        for h in range(4):
                            ins = e.matmul(bks[h // 2][0][:, (h % 2) * 256:(h % 2) * 256 + 256], lhsT=ke[:, h * 128:(h + 1) * 128],
                                           rhs=vb[:, h * 256:(h + 1) * 256], start=True, stop=True)
                        return ins
                    PE(f, [keres, "vb"], [bks[0][1], bks[1][1]])
                    return bks
                d0 = dstate(ke0, "ke0")
                for h in range(4):
                    DVE(lambda e, h=h: e.scalar_tensor_tensor(out=S1[:, h * 256:(h + 1) * 256], in0=S[:, h * 256:(h + 1) * 256], scalar=dec[:, 2 * h:2 * h + 1],
                                                              op0=ALU.mult, in1=d0[h // 2][0][:, (h % 2) * 256:(h % 2) * 256 + 256], op1=ALU.add),
                        ["S", "dec", d0[h // 2][1]], ["S1"])
                ACT(lambda e: e.copy(out=S1b[:], in_=S1[:]), ["S1"], ["S1b"])
                yield
                ob = [bank(), bank()]

                def f(e):
                    ins = None
                    for h in range(4):
                        o_ap = ob[h // 2][0][:, (h % 2) * 256:(h % 2) * 256 + 256]
                        e.matmul(o_ap, lhsT=attm[:, h * 128:(h + 1) * 128], rhs=vb[:, h * 256:(h + 1) * 256], start=True, stop=False)
                        e.matmul(o_ap, lhsT=qi0[:, h * 128:(h + 1) * 128], rhs=Sb[:, h * 256:(h + 1) * 256], start=False, stop=False)
                        ins = e.matmul(o_ap, lhsT=qi1[:, h * 128:(h + 1) * 128], rhs=S1b[:, h * 256:(h + 1) * 256], start=False, stop=True)
                    return ins
                PE(f, ["attm", "vb", "qi0", "qi1", "Sb", "S1b"], [ob[0][1], ob[1][1]])
                for h in range(4):
                    ACT(lambda e, h=h: e.activation(out=junkA[:, 0:256], in_=ob[h // 2][0][:, (h % 2) * 256:(h % 2) * 256 + 256], func=AF.Square,
                                                    accum_out=ssqa[:, h:h + 1]), [ob[h // 2][1]], ["ssqa", "junkA"])
                rstd_op(ssqa[:], rsa[:], 256, "ssqa", "rsa")
                o_a = tmB
                for h in range(4):
                    DVE(lambda e, h=h: e.scalar_tensor_tensor(out=o_a[:, h * 256:(h + 1) * 256], in0=ob[h // 2][0][:, (h % 2) * 256:(h % 2) * 256 + 256],
                                                              scalar=rsa[:, h:h + 1], op0=ALU.mult, in1=B4[:, h * 256:(h + 1) * 256], op1=ALU.mult),
                        [ob[h // 2][1], "rsa", "F1"], ["tmB"])
                if DBG:
                    DVE(lambda e: e.tensor_copy(out=A4, in_=o_a[:]), ["tmB"], ["F0"])
                    tap("oa", A4, tok0, "F0")
                transposes(o_a, 8, "tmB", trB, "trB")
                yield
                d1 = dstate(ke1, "ke1")
                for h in range(4):
                    DVE(lambda e, h=h: e.scalar_tensor_tensor(out=S[:, h * 256:(h + 1) * 256], in0=S1[:, h * 256:(h + 1) * 256], scalar=dec[:, 2 * h + 1:2 * h + 2],
                                                              op0=ALU.mult, in1=d1[h // 2][0][:, (h % 2) * 256:(h % 2) * 256 + 256], op1=ALU.add),
                        ["S1", "dec", d1[h // 2][1]], ["S"])
                ACT(lambda e: e.copy(out=Sb[:], in_=S[:]), ["S"], ["Sb"])
                yield


                for hf in range(2):
                    bk, br = bank(); proj_tok(n0 + 13 + hf, trB, "trB", bk, br)
                    DVE(lambda e, bk=bk, hf=hf: e.tensor_tensor(out=B4[:, hf * 512:(hf + 1) * 512], in0=bk[:, :], in1=D4[:, hf * 512:(hf + 1) * 512], op=ALU.mult),
                        [br, "F3"], ["F1"])
                    yield

            def swa():
                qn = C4
                for hf in range(2):
                    bk, br = bank(); proj_tok(n0 + 6 + hf, hT, "trA", bk, br)
                    ACT(lambda e, bk=bk, hf=hf: e.activation(out=A4[:, hf * 512:(hf + 1) * 512], in_=bk[:, :], func=AF.Square), [br], ["F0"])
                    DVE(lambda e, hf=hf: e.tensor_reduce(out=ssqq[:, hf * 8:(hf + 1) * 8], in_=v3(A4[:, hf * 512:(hf + 1) * 512], 64), axis=AX.X, op=ALU.add),
                        ["F0"], ["ssqq%d" % hf])
                    rstd_op(ssqq[:, hf * 8:(hf + 1) * 8], rsq[:, hf * 8:(hf + 1) * 8], 64, "ssqq%d" % hf, "rsq%d" % hf)
                    DVE(lambda e, bk=bk, hf=hf: e.tensor_tensor(out=v3(qn[:, hf * 512:(hf + 1) * 512], 64), in0=v3(bk[:, :], 64),
                                                                in1=rsq[:, hf * 8:(hf + 1) * 8, None].broadcast_to([128, 8, 64]), op=ALU.mult),
                        [br, "rsq%d" % hf], ["F2"])
                    yield
                PL(lambda e: e.tensor_tensor(out=qn, in0=qn, in1=gq_rep[:], op=ALU.mult), ["F2", "gq_rep"], ["F2"])
                rope(qn, "F2", 16, ti)
                ACT(lambda e: e.copy(out=tmA[:], in_=qn), ["F2"], ["tmA"])
                transposes(tmA, 8, "tmA", trC, "trC")
                yield
                bk, br = bank(); proj_tok(n0 + 8, hT, "trA", bk, br)
                ACT(lambda e, bk=bk: e.activation(out=kq[:], in_=bk[:, 0:256], func=AF.Square), [br], ["kq"])
                DVE(lambda e: e.tensor_reduce(out=ssqk[:], in_=v3(kq[:], 64), axis=AX.X, op=ALU.add), ["kq"], ["ssqk"])
                rstd_op(ssqk[:], rsk[:], 64, "ssqk", "rsk")
                DVE(lambda e, bk=bk: e.tensor_tensor(out=v3(kn[:], 64), in0=v3(bk[:, 0:256], 64), in1=rsk[:, :, None].broadcast_to([128, 4, 64]), op=ALU.mult),
                    [br, "rsk"], ["kn"])
                ACT(lambda e, bk=bk: e.copy(out=vsw[par][:], in_=bk[:, 256:512]), [br], ["vsw%d" % par])
                PL(lambda e: e.tensor_tensor(out=kn[:], in0=kn[:], in1=gk_rep[:], op=ALU.mult), ["kn", "gk_rep"], ["kn"])
                rope(kn[:], "kn", 4, ti)
                k2v = k2[:].rearrange("p (g c d) -> p g c d", c=2, d=64)
                ACT(lambda e: e.copy(out=k2v[:, :, 0, :], in_=v3(kn[:], 64)), ["kn"], ["k2"])
                ACT(lambda e: e.copy(out=k2v[:, :, 1, :], in_=v3(kn[:], 64)), ["kn"], ["k2"])
                transposes(k2, 4, "k2", kT2[par], "kT2_%d" % par)
                yield
                has_prev = ti > 0
                ppar = 1 - par
                for g in range(4):
                    for odd in range(2):
                        bk, br = bank()
                        lo = 64 * odd

                        def f(e, bk=bk, lo=lo, g=g):
                            ins = None
                            if has_prev:
                                ins = e.matmul(bk[:, 0:256], lhsT=kT2[ppar][lo:lo + 64, g * 128:(g + 1) * 128], rhs=trC[lo:lo + 64, 2 * g * 128:(2 * g + 2) * 128],
                                               start=True, stop=True)
                            ins = e.matmul(bk[:, 256:512], lhsT=kT2[par][lo:lo + 64, g * 128:(g + 1) * 128], rhs=trC[lo:lo + 64, 2 * g * 128:(2 * g + 2) * 128],
                                           start=True, stop=True)
                            return ins
                        PE(f, ["trC", "kT2_0", "kT2_1"], [br])
                        c0 = 0 if has_prev else 256
                        pr = praw[odd]
                        dst = (pTo if odd else pTe)[g]
                        dres = ("pTo%d" if odd else "pTe%d") % g
                        ACT(lambda e, bk=bk, pr=pr, c0=c0: e.activation(out=pr[:, c0:512], in_=bk[:, c0:512], func=AF.Exp), [br], ["praw%d" % odd])
                        PL(lambda e, pr=pr, dst=dst, c0=c0: e.tensor_tensor(out=dst[:, c0:512], in0=pr[:, c0:512], in1=mk2[:, c0:512], op=ALU.mult),
                             ["praw%d" % odd, "mk2"], [dres])
                    yield
                OB = [bank(), bank()]
                SBk, SBr = bank()

                def f(e):
                    ins = None
                    for h in range(16):
                        g, j = h // 4, h % 4
                        src = (pTo if j % 2 else pTe)[g]
                        slot = j // 2
                        o_ap = OB[h // 8][0][:, (h % 8) * 64:(h % 8) * 64 + 64]
                        if has_prev:
                            e.matmul(o_ap, lhsT=src[:, slot * 128:(slot + 1) * 128], rhs=vsw[ppar][:, g * 64:(g + 1) * 64], start=True, stop=False)
                        ins = e.matmul(o_ap, lhsT=src[:, 256 + slot * 128:256 + (slot + 1) * 128], rhs=vsw[par][:, g * 64:(g + 1) * 64], start=(not has_prev), stop=True)
                    for h in range(16):
                        g, j = h // 4, h % 4
                        src = (pTo if j % 2 else pTe)[g]
                        slot = j // 2
                        s_ap = SBk[:, h:h + 1]
                        if has_prev:
                            e.matmul(s_ap, lhsT=src[:, slot * 128:(slot + 1) * 128], rhs=ones_bf[:, 0:1], start=True, stop=False)
                        ins = e.matmul(s_ap, lhsT=src[:, 256 + slot * 128:256 + (slot + 1) * 128], rhs=ones_bf[:, 0:1], start=(not has_prev), stop=True)
                    return ins
                PE(f, ["pTe0", "pTe1", "pTe2", "pTe3", "pTo0", "pTo1", "pTo2", "pTo3", "vsw0", "vsw1", "ones_bf"], [OB[0][1], OB[1][1], SBr])
                DVE(lambda e: e.tensor_tensor(out=den[:], in0=SBk[:, 0:16], in1=esink[:], op=ALU.add), [SBr, "esink"], ["den"])
                DVE(lambda e: e.reciprocal(out=rden[:], in_=den[:]), ["den"], ["rden"])
                o_b = tmB
                for hf in range(2):
                    DVE(lambda e, hf=hf: e.tensor_tensor(out=v3(o_b[:, hf * 512:(hf + 1) * 512], 64), in0=v3(OB[hf][0][:, :], 64),
                                                         in1=rden[:, hf * 8:(hf + 1) * 8, None].broadcast_to([128, 8, 64]), op=ALU.mult),
                        [OB[hf][1], "rden"], ["tmB"])
                if DBG:
                    DVE(lambda e: e.tensor_copy(out=A4, in_=o_b[:]), ["tmB"], ["F0"])
                    tap("ob", A4, tok0, "F0")
                transposes(o_b, 8, "tmB", trD, "trD")
                yield

                for hf in range(2):
                    bk, br = bank(); proj_tok(n0 + 15 + hf, trD, "trD", bk, br)
                    DVE(lambda e, bk=bk, hf=hf: e.tensor_tensor(out=C4[:, hf * 512:(hf + 1) * 512], in0=bk[:, :], in1=E4[:, hf * 512:(hf + 1) * 512], op=ALU.mult),
                        [br, "F4"], ["F2"])
                    yield

            def gates():
                for hf in range(2):
                    bk, br = bank(); proj_tok(n0 + 9 + hf, hT, "trA", bk, br)
                    ACT(lambda e, bk=bk, hf=hf: e.activation(out=D4[:, hf * 512:(hf + 1) * 512], in_=bk[:, :], func=AF.Sigmoid), [br], ["F3"])
                    yield
                for hf in range(2):
                    bk, br = bank(); proj_tok(n0 + 11 + hf, hT, "trA", bk, br)
                    ACT(lambda e, bk=bk, hf=hf: e.activation(out=E4[:, hf * 512:(hf + 1) * 512], in_=bk[:, :], func=AF.Sigmoid), [br], ["F4"])
                    yield

            subs = [gla(), swa(), gates()]
            while subs:
                for g_ in list(subs):
                    try:
                        next(g_)
                    except StopIteration:
                        subs.remove(g_)
                yield
            PL(lambda e: e.tensor_tensor(out=tmA[:], in0=B4, in1=C4, op=ALU.add), ["F1", "F2"], ["tmA"])
            transposes(tmA, 8, "tmA", trB, "trB")
            yield
            if STAGE <= 2.8:
                return
            for hf in range(2):
                bk, br = bank(); proj_tok(n0 + 17 + hf, trB, "trB", bk, br)
                DVE(lambda e, bk=bk, hf=hf: e.tensor_tensor(out=x2[:, hf * 512:(hf + 1) * 512], in0=bk[:, :], in1=xt[:, hf * 512:(hf + 1) * 512], op=ALU.add),
                    [br, XT], [X2])
                yield
            tap("x2", x2[:], tok0, X2)

        def peer(si, ti, nxt):
            gi = si * NT + ti
            tok0 = gi * 128
            n0 = gi * NCH
            xt = xts[gi % 2]; x2 = x2s[gi % 2]
            XT = "xt%d" % (gi % 2); X2 = "x2_%d" % (gi % 2)
            ACT(lambda e: e.activation(out=junkA[:], in_=x2[:], func=AF.Square, accum_out=ss[:, 1:2]), [X2], ["ss1", "junkA"])
            rstd_op(ss[:, 1:2], rs[:, 1:2], 1024, "ss1", "rs1")
            DVE(lambda e: e.tensor_scalar(out=tmA[:], in0=x2[:], scalar1=rs[:, 1:2], scalar2=None, op0=ALU.mult), [X2, "rs1"], ["tmA"])
            DVE(lambda e: e.scalar_tensor_tensor(out=XN[:, :], in0=x2[:], scalar=rs[:, 1:2], op0=ALU.mult, in1=gffn_rep[:], op1=ALU.mult),
                [X2, "rs1", "gffn_rep"], ["XN"])
            transposes(tmA, 8, "tmA", trA, "trA")
            for cc in range(4):
                bk, br = bank(); proj_feat(n0 + 19 + cc, trA, "trA", bk, br)
                ACT(lambda e, bk=bk, cc=cc: e.copy(out=qpT[:, cc * 512:(cc + 1) * 512], in_=bk[:, :]), [br], ["qpT"])
            sc = RG[:, 0:4096].bitcast(F32)
            SCR = ["rg0", "rg1", "rgu0", "rgu1"]
            for half in range(2):
                scb = [bank(), bank()]

                def f(e, scb=scb, half=half):
                    ins = None
                    for q in range(8):
                        hp = half * 8 + q
                        ins = e.matmul(scb[q // 4][0][:, (q % 4) * 128:(q % 4 + 1) * 128], lhsT=qpT[:, hp * 128:(hp + 1) * 128], rhs=skT[:, hp * 128:(hp + 1) * 128],
                                       start=True, stop=True)
                    return ins
                PE(f, ["qpT", "skT"], [b_[1] for b_ in scb])
                for i in range(2):
                    ACT(lambda e, i=i, scb=scb, half=half: e.copy(out=sc[:, (half * 2 + i) * 512:(half * 2 + i + 1) * 512], in_=scb[i][0][:, :]),
                        [scb[i][1]], SCR)

            def selback():
                for hg in range(4):
                    hps = [hg * 4 + q for q in range(4)]
                    sls = [sc[:, hp * 128:(hp + 1) * 128] for hp in hps]
                    for q, hp in enumerate(hps):
                        DVE(lambda e, sl=sls[q], hp=hp: e.max(out=tv[:, hp * 16:hp * 16 + 8], in_=sl), SCR, ["tv%d" % hp])
                    for q, hp in enumerate(hps):
                        DVE(lambda e, sl=sls[q], hp=hp: e.max_index(out=tiu[:, hp * 16:hp * 16 + 8], in_max=tv[:, hp * 16:hp * 16 + 8], in_values=sl), SCR + ["tv%d" % hp], ["tiu%d" % hp])
                    for q, hp in enumerate(hps):
                        DVE(lambda e, sl=sls[q], hp=hp, q=q: e.match_replace(out=sc2L[q][:], in_to_replace=tv[:, hp * 16:hp * 16 + 8], in_values=sl, imm_value=-1e30),
                            SCR + ["tv%d" % hp], ["sc2_%d" % q])
                    for q, hp in enumerate(hps):
                        DVE(lambda e, hp=hp, q=q: e.max(out=tv[:, hp * 16 + 8:hp * 16 + 16], in_=sc2L[q][:]), ["sc2_%d" % q], ["tvb%d" % hp])
                    for q, hp in enumerate(hps):
                        DVE(lambda e, hp=hp, q=q: e.max_index(out=tiu[:, hp * 16 + 8:hp * 16 + 16], in_max=tv[:, hp * 16 + 8:hp * 16 + 16], in_values=sc2L[q][:]),
                            ["sc2_%d" % q, "tvb%d" % hp], ["tiub%d" % hp])
                    yield
                TVALL = ["tv%d" % hp for hp in range(16)] + ["tvb%d" % hp for hp in range(16)]
                TIALL = ["tiu%d" % hp for hp in range(16)] + ["tiub%d" % hp for hp in range(16)]
                DVE(lambda e: e.tensor_copy(out=tif[:], in_=tiu[:]), TIALL, ["tif"])
                tfv = tif[:].rearrange("p (h c k) -> p h c k", c=2, k=16)
                tvv = tv[:].rearrange("p (h c k) -> p h c k", c=2, k=16)
                DVE(lambda e: e.tensor_scalar(out=tfv[:, :, 0, :], in0=tfv[:, :, 0, :], scalar1=128.0, scalar2=None, op0=ALU.mult), ["tif"], ["tif"])
                for hg in range(4):
                    hs_ = [hg * 2, hg * 2 + 1]
                    for q, h in enumerate(hs_):
                        for (dstL, srcv, rres, wn) in ((candL, tvv, TVALL, "cand_%d"), (ciL, tfv, ["tif"], "ci_%d")):
                            dst = dstL[q]
                            DVE(lambda e, h=h, dst=dst, srcv=srcv: e.tensor_tensor(out=v3(dst[:, 0:64], 16), in0=srcv[:, h, 0, 0:4, None].broadcast_to([128, 4, 16]),
                                                                                   in1=srcv[:, h, 1, None, :].broadcast_to([128, 4, 16]), op=ALU.add), rres, [wn % q])
                            DVE(lambda e, h=h, dst=dst, srcv=srcv: e.tensor_tensor(out=v3(dst[:, 64:112], 4), in0=srcv[:, h, 0, 4:16, None].broadcast_to([128, 12, 4]),
                                                                                   in1=srcv[:, h, 1, None, 0:4].broadcast_to([128, 12, 4]), op=ALU.add), rres, [wn % q])
                    for q, h in enumerate(hs_):
                        DVE(lambda e, h=h, q=q: e.max(out=bv[:, h * 16:h * 16 + 8], in_=candL[q][:]), ["cand_%d" % q], ["bv%d" % h])
                    for q, h in enumerate(hs_):
                        DVE(lambda e, h=h, q=q: e.match_replace(out=cand2L[q][:], in_to_replace=bv[:, h * 16:h * 16 + 8], in_values=candL[q][:], imm_value=-1e30),
                            ["cand_%d" % q, "bv%d" % h], ["cand2_%d" % q])
                    for q, h in enumerate(hs_):
                        DVE(lambda e, h=h, q=q: e.max(out=bv[:, h * 16 + 8:h * 16 + 16], in_=cand2L[q][:]), ["cand2_%d" % q], ["bvb%d" % h])
                    yield
                    for j in range(16):
                        for q, h in enumerate(hs_):
                            r = h * 16 + j
                            DVE(lambda e, r=r, q=q: e.scalar_tensor_tensor(out=junkD[:, (r % 8) * 128:(r % 8) * 128 + 112], in0=candL[q][:], scalar=bv[:, r:r + 1], op0=ALU.is_equal,
                                                                           in1=ciL[q][:], op1=ALU.mult, accum_out=idxf[:, r:r + 1]),
                                ["cand_%d" % q, "ci_%d" % q, "bv%d" % h, "bvb%d" % h], ["idxf%d" % r, "jd%d" % (r % 8)])
                        if j % 4 == 3:
                            yield
                BVALL = ["bv%d" % h for h in range(8)] + ["bvb%d" % h for h in range(8)]
                DVE(lambda e: e.tensor_scalar(out=idxf[:], in0=idxf[:], scalar1=16383.0, scalar2=0.0, op0=ALU.min, op1=ALU.max), ["idxf%d" % r for r in range(128)], ["idxf"])
                DVE(lambda e: e.tensor_copy(out=idxi[:], in_=idxf[:]), ["idxf"], ["idxi"])
                DVE(lambda e: e.tensor_scalar(out=negm[:], in0=v3(bv[:], 16)[:, :, 0], scalar1=-1.0, scalar2=None, op0=ALU.mult), BVALL, ["negm"])
                for h in range(8):
                    ACT(lambda e, h=h: e.activation(out=eg[:, h * 16:(h + 1) * 16], in_=bv[:, h * 16:(h + 1) * 16], func=AF.Exp, bias=negm[:, h:h + 1], scale=1.0,
                                                    accum_out=Z[:, h:h + 1]), BVALL + ["negm"], ["eg", "Z"])
                DVE(lambda e: e.reciprocal(out=rZ[:], in_=Z[:]), ["Z"], ["rZ"])
                DVE(lambda e: e.tensor_tensor(out=v3(gate[:], 16), in0=v3(eg[:], 16), in1=rZ[:, :, None].broadcast_to([128, 8, 16]), op=ALU.mult), ["eg", "rZ"], ["gate"])
                if DBG:
                    tap("idx", idxf[:], tok0, "idxf")

            sb_gen = selback()
            gens = [(sb_gen, 3)] + ([(nxt, 2)] if nxt is not None else [])
            while gens:
                for (g_, k_) in list(gens):
                    for _ in range(k_):
                        try:
                            next(g_)
                        except StopIteration:
                            gens.remove((g_, k_))
                            break

            PA = bank(); PB = bank()

            def gather(r):
                slot = r % NG
                p.op("pool", lambda e: e.indirect_dma_start(out=rg[slot], out_offset=None, in_=uvs,
                                                            in_offset=bass.IndirectOffsetOnAxis(ap=idxi[:, r:r + 1], axis=0)),
                     ["idxi"] + UVRES, ["rg%d" % slot, "rgu%d" % slot], dma="gr%d" % slot)

            def dot(r):
                slot = r % NG
                DVE(lambda e: e.scalar_tensor_tensor(out=rg[slot][:, 0:1024], in0=rg[slot][:, 0:1024], scalar=1.0, op0=ALU.mult, in1=XN[:, :], op1=ALU.mult,
                                                     accum_out=hd[:, r:r + 1]), ["rg%d" % slot, "XN"], ["hd%d" % r, "rgu%d" % slot])

            def combine(r):
                slot = r % NG
                ds = r % ND
                ACT(lambda e: e.activation(out=gl[:, r:r + 1], in_=hd[:, r:r + 1], func=AF.Gelu), ["hd%d" % r], ["gl%d" % r])
                DVE(lambda e: e.tensor_scalar(out=Dr[ds][:], in0=identf[:], scalar1=gl[:, r:r + 1], scalar2=gate[:, r:r + 1], op0=ALU.mult, op1=ALU.mult),
                    ["identf", "gl%d" % r, "gate"], ["Dr%d" % ds])

                def f(e):
                    e.matmul(PA[0][:, :], lhsT=Dr[ds][:], rhs=rg[slot][:, 1024:1536], start=(r == 0), stop=(r == NR - 1))
                    return e.matmul(PB[0][:, :], lhsT=Dr[ds][:], rhs=rg[slot][:, 1536:2048], start=(r == 0), stop=(r == NR - 1))
                PE(f, ["Dr%d" % ds, "rg%d" % slot], [PA[1], PB[1]])

            for r in range(min(NG, NR)):
                gather(r)
            dot(0)
            for r in range(NR):
                if r + 1 < NR:
                    dot(r + 1)
                combine(r)
                if r + NG < NR:
                    gather(r + NG)
            DVE(lambda e: e.tensor_tensor(out=xt[:, 0:512], in0=PA[0][:, :], in1=x2[:, 0:512], op=ALU.add), [PA[1], X2], [XT])
            DVE(lambda e: e.tensor_tensor(out=xt[:, 512:1024], in0=PB[0][:, :], in1=x2[:, 512:1024], op=ALU.add), [PB[1], X2], [XT])
            DMA(lambda e: e.dma_start(out=out_d[tok0:tok0 + 128, :], in_=xt[:]), [XT], (), sem="sto%d" % (gi % 2))

        tiles = [(si, ti) for si in range(NSEQ) for ti in range(NT)]
        p.dry = True
        sv = bstate["i"]
        for _ in mixer(0, 1 if NT > 1 else 0):
            pass
        bstate["i"] = sv
        wstate["order"] = list(wstate["rec"]) + [19, 20, 21, 22]
        assert sorted(wstate["order"]) == list(range(NCH)), wstate["order"]
        wstate["use"] = -1
        wstate["lastc"] = None
        p.dry = False
        if STAGE >= 1 and STAGE <= 3:
            for (si, ti) in tiles:
                for _ in mixer(si, ti):
                    pass
        elif STAGE > 3:
            for _ in mixer(*tiles[0]):
                pass
            for k, (si, ti) in enumerate(tiles):
                nxt = mixer(*tiles[k + 1]) if k + 1 < len(tiles) else None
                peer(si, ti, nxt)
        if STAGE == 0:
            DMA(lambda e: e.dma_start(out=xts[0][:, 0:128], in_=wscr[22][:, 0:256].bitcast(F32)), ["wscr22"], ["xt0"], sem="ldx0")
            DMA(lambda e: e.dma_start(out=out_d[0:128, 0:128], in_=xts[0][:, 0:128]), ["xt0"], (), sem="sto")
        evs = [(p.dsem[k], p.dcnt[k]) for k in p.dsem if k.startswith("sto") or k.startswith("dbg_")]
        p.final_wait("sp", evs)
        p.emit()
    return nc


def host_consts():
    ident = np.eye(128, dtype=np.float32)
    s = np.arange(128)[:, None]
    t = np.arange(128)[None, :]
    same = (s // 64) == (t // 64)
    tri = np.where(same & (s <= t), -1.0 / 16.0, 0.0).astype(np.float32)
    tri2 = np.where(same & (s > t), -1.0 / 16.0, 0.0).astype(np.float32)
    mgla = np.where(same & (s <= t), 1.0, 0.0).astype(np.float32)
    mown = np.where(s <= t, 1.0, 0.0).astype(np.float32)
    mprev = np.where(s > t, 1.0, 0.0).astype(np.float32)
    cmat = np.stack([ident, tri, tri2, mgla, mown]).astype(np.float32)
    cm = np.stack([(np.arange(128) < 64), (np.arange(128) >= 64)], axis=1).astype(np.float32)
    pos = np.arange(2048, dtype=np.float32)
    inv_freq = (np.float32(500000.0) ** (-np.arange(0, 16, 2, dtype=np.float32) / np.float32(16))).astype(np.float32)
    ang = (pos[:, None] * inv_freq[None, :]).astype(np.float32)
    cos = np.cos(ang).astype(np.float32).reshape(16, 128, 8).transpose(1, 0, 2).reshape(128, 128)
    sin = np.sin(ang).astype(np.float32).reshape(16, 128, 8).transpose(1, 0, 2).reshape(128, 128)
    return dict(cmat=cmat, mprev=mprev, cm=cm, rcos=np.ascontiguousarray(cos), rsin=np.ascontiguousarray(sin))


def make_in_maps(inputs, n_cores, NSEQ, NT):
    f = lambda a: np.ascontiguousarray(np.asarray(a, dtype=np.float32))
    x = f(inputs["x"])
    seq = NT * 128
    xs = x.reshape(-1, seq, 1024)
    assert xs.shape[0] == n_cores * NSEQ
    shared = dict(
        w_in=f(inputs["w_in"][0]), w_branch_a=f(inputs["w_branch_a"][0]), w_branch_b=f(inputs["w_branch_b"][0]),
        w_out=f(inputs["w_out"][0]), w_peer_q=f(inputs["w_peer_q"][0]),
        sub_keys=f(inputs["peer_sub_keys"][0]).reshape(16, 128, 128),
        peer_u=f(inputs["peer_u"][0]), peer_v=f(inputs["peer_v"][0]),
        gmix_pk=np.ascontiguousarray(f(inputs["norm_mix_g"][0]).reshape(8, 128).T),
        gffn_pk=np.ascontiguousarray(f(inputs["norm_ffn_g"][0]).reshape(8, 128).T),
        gffn_row=f(inputs["norm_ffn_g"][0]).reshape(1, 1024),
        ga_row=f(inputs["gla_norm_g"][0]).reshape(1, 256),
        gq_row=f(inputs["q_norm_g"][0]).reshape(1, 64),
        gk_row=f(inputs["k_norm_g"][0]).reshape(1, 64),
        sinks_row=f(inputs["attn_sinks"][0]).reshape(1, 16),
        w_gk2=f(inputs["w_gk2"][0]), b_gk=f(inputs["b_gk"][0]).reshape(1, 512),
    )
    shared.update(host_consts())
    maps = []
    for c in range(n_cores):
        m = dict(shared)
        m["x"] = np.ascontiguousarray(xs[c * NSEQ:(c + 1) * NSEQ].reshape(NSEQ * seq, 1024))
        maps.append(m)
    return maps


def kernel(**inputs):
    n = 8
    NSEQ, NT = 2, 16
    nc = build_nc(NSEQ, NT)
    in_maps = make_in_maps(inputs, n, NSEQ, NT)
    res = run_bass_kernel_spmd(nc, in_maps, core_ids=list(range(n)))
    out = np.concatenate([np.asarray(r["out"]).reshape(NSEQ, NT * 128, 1024) for r in res.results], axis=0)
    return out.astype(np.float32)
```
